# Optimizing a Trainium2 kernel written in Bass

```python
import jax, jax.numpy as jnp
from jax import lax
import numpy as np

D_MODEL = 1024
BATCH = 8
SEQ = 4096
DEPTH = 4

GRID_W = 64
CTX_LEN = 256
N_MIXERS = 3
D_FF = 2816
N_MOD = 9
ALPHA = (2 * DEPTH) ** 0.25
BETA = (8 * DEPTH) ** -0.25
LN_EPS = 1e-5

ML_INNER = 2 * D_MODEL
ML_HEADS = 4
ML_HEAD_DIM = ML_INNER // ML_HEADS
ML_QKV_BLOCK = 4
ML_CHUNK = 64
ML_CONV = 3

AT_HEADS = 16
AT_KV_HEADS = 4
AT_HEAD_DIM = 64
AT_WINDOW = 128
AT_BLOCK = 128
AT_SIDE_BLOCKS = -(-AT_WINDOW // AT_BLOCK)
ROPE_BASE = 10000.0

SC_WIDTH = 3

N_A = (DEPTH + 2) // N_MIXERS
N_B = (DEPTH + 1) // N_MIXERS
N_C = DEPTH // N_MIXERS

kernel_name = 'hybrid_mlstm_swa_shortconv_flow_backbone'


def layer_norm(x, g, b):
    xf = x.astype(jnp.float32)
    mu = xf.mean(-1, keepdims=True)
    var = jnp.mean(jnp.square(xf - mu), -1, keepdims=True)
    return ((xf - mu) * lax.rsqrt(var + LN_EPS) * g.astype(jnp.float32) + b.astype(jnp.float32)).astype(x.dtype)


def modulate(s, m, slot):
    return s * (1 + m[..., 3 * slot + 1, :]) + m[..., 3 * slot, :]


def gate_of(m, slot):
    return m[..., 3 * slot + 2, :]


def residual_post_norm(s, y, gate, weight, g, b):
    return layer_norm(ALPHA * s + weight * gate * y, g, b)


def swiglu(u, w_in, w_out):
    g, v = jnp.split(u @ w_in, 2, axis=-1)
    return (jax.nn.silu(g) * v) @ w_out


def dwconv_centred(u, w):
    K, C = w.shape
    return lax.conv_general_dilated(u, w[:, None, :], window_strides=(1,), padding=[(K // 2, K // 2)],
                                    dimension_numbers=('NWC', 'WIO', 'NWC'), feature_group_count=C)


def blockdiag(u, w):
    nblk, bs, _ = w.shape
    return jnp.einsum('blnc,ncd->blnd', u.reshape(u.shape[:2] + (nblk, bs)), w).reshape(u.shape)


def axial_rope(n_tokens, head_dim):
    rows_n = n_tokens // GRID_W
    rows = jnp.repeat(jnp.arange(rows_n), GRID_W).astype(jnp.float32)
    cols = jnp.tile(jnp.arange(GRID_W), rows_n).astype(jnp.float32)
    axis_dim = head_dim // 2
    freqs = ROPE_BASE ** (-jnp.arange(0, axis_dim, 2, dtype=jnp.float32) / axis_dim)
    ang = jnp.concatenate([rows[:, None] * freqs, cols[:, None] * freqs], axis=-1)
    return jnp.cos(ang), jnp.sin(ang)


def apply_rope(x, cos, sin):
    xf = x.astype(jnp.float32).reshape(x.shape[:-1] + (-1, 2))
    x1, x2 = xf[..., 0], xf[..., 1]
    return jnp.stack([x1 * cos - x2 * sin, x1 * sin + x2 * cos], axis=-1).reshape(x.shape).astype(x.dtype)


def sink_attend(q, k, v, mask, sink):
    G, R = q.shape[2], q.shape[3]
    s = jnp.einsum('bqgrd,bkgd->bgrqk', q, k).astype(jnp.float32)
    if mask is not None:
        s = jnp.where(mask, s, -jnp.inf)
    sk = sink.astype(jnp.float32).reshape(1, G, R, 1, 1)
    mx = jnp.maximum(s.max(-1, keepdims=True), sk)
    p = jnp.exp(s - mx)
    den = p.sum(-1, keepdims=True) + jnp.exp(sk - mx)
    return jnp.einsum('bgrqk,bkgd->bqgrd', (p / den).astype(v.dtype), v)


def window_attention(ul, uc, w_qkv, sink, w_o, ctx_out):
    B, S, _ = ul.shape
    Lc = uc.shape[1]
    H, G, d = AT_HEADS, AT_KV_HEADS, AT_HEAD_DIM
    R = H // G
    P = AT_SIDE_BLOCKS * AT_BLOCK
    nb = S // AT_BLOCK
    KW = (2 * AT_SIDE_BLOCKS + 1) * AT_BLOCK

    def proj(u):
        L = u.shape[1]
        q, k, v = jnp.split(u @ w_qkv, [H * d, (H + G) * d], axis=-1)
        return q.reshape(B, L, G, R, d) * d ** -0.5, k.reshape(B, L, G, d), v.reshape(B, L, G, d)

    ql, kl, vl = proj(ul)
    qc, kc, vc = proj(uc)
    cos, sin = axial_rope(S, d)
    ql = apply_rope(ql, cos[:, None, None], sin[:, None, None])
    kl = apply_rope(kl, cos[:, None], sin[:, None])

    def band(a):
        ab = jnp.pad(a, ((0, 0), (P, P), (0, 0), (0, 0))).reshape(B, nb + 2 * AT_SIDE_BLOCKS, AT_BLOCK, G, d)
        win = jnp.concatenate([ab[:, o:o + nb] for o in range(2 * AT_SIDE_BLOCKS + 1)], axis=2)
        return jnp.moveaxis(win, 1, 0)

    qb = jnp.moveaxis(ql.reshape(B, nb, AT_BLOCK, G, R, d), 1, 0)
    q_pos = jnp.arange(nb)[:, None] * AT_BLOCK + jnp.arange(AT_BLOCK)
    k_pos = jnp.arange(nb)[:, None] * AT_BLOCK - P + jnp.arange(KW)
    valid = ((jnp.abs(q_pos[:, :, None] - k_pos[:, None, :]) <= AT_WINDOW)
             & (k_pos >= 0)[:, None, :] & (k_pos < S)[:, None, :])
    mask = jnp.concatenate([valid, jnp.ones((nb, AT_BLOCK, Lc), dtype=bool)], axis=-1)

    def attend_block(args):
        q, kw, vw, m = args
        return sink_attend(q, jnp.concatenate([kw, kc], axis=1), jnp.concatenate([vw, vc], axis=1), m, sink)

    ol = lax.map(attend_block, (qb, band(kl), band(vl), mask))
    yl = jnp.moveaxis(ol, 0, 1).reshape(B, S, H * d) @ w_o
    yc = sink_attend(qc, kc, vc, None, sink).reshape(B, Lc, H * d) @ w_o if ctx_out else None
    return yl, yc


def mlstm_chunked(q, k, v, ig, fg, state):
    B, H, L, dh = q.shape
    T = ML_CHUNK
    nc = L // T
    k = k * dh ** -0.5
    chunk = lambda a: jnp.moveaxis(a.reshape((B, H, nc, T) + a.shape[3:]), 2, 0)
    logf = jax.nn.log_sigmoid(fg)
    causal = jnp.tril(jnp.ones((T, T), dtype=bool))

    def step(carry, inp):
        C, n, m = carry
        qc, kc, vc, ic, lf = inp
        b = jnp.cumsum(lf, axis=-1)
        g = b[..., -1]
        d_intra = jnp.where(causal, b[..., :, None] - b[..., None, :] + ic[..., None, :], -jnp.inf)
        d_inter = b + m[..., None]
        m_t = jnp.maximum(d_inter, d_intra.max(-1))
        w_intra = jnp.einsum('bhtd,bhsd->bhts', qc, kc) * jnp.exp(d_intra - m_t[..., None])
        w_inter = jnp.exp(d_inter - m_t)
        num = jnp.einsum('bhts,bhse->bhte', w_intra, vc) + w_inter[..., None] * jnp.einsum('bhtd,bhde->bhte', qc, C)
        den = w_intra.sum(-1) + w_inter * jnp.einsum('bhtd,bhd->bht', qc, n)
        h = num / jnp.maximum(jnp.abs(den), jnp.exp(-m_t))[..., None]
        a = g[..., None] - b + ic
        m_new = jnp.maximum(g + m, a.max(-1))
        w_s = jnp.exp(a - m_new[..., None])
        decay = jnp.exp(g + m - m_new)
        C = decay[..., None, None] * C + jnp.einsum('bhsd,bhse->bhde', kc * w_s[..., None], vc)
        n = decay[..., None] * n + jnp.einsum('bhs,bhsd->bhd', w_s, kc)
        return (C, n, m_new), h

    state, hs = lax.scan(step, state, (chunk(q), chunk(k), chunk(v), chunk(ig), chunk(logf)))
    return state, jnp.moveaxis(hs, 0, 2).reshape(B, H, L, dh)


def mlstm_mixer(ul, uc, w_up, conv_w, conv_b, w_qkv, w_if, b_if, skip, norm_g, w_down, ctx_out):
    E, H, dh = ML_INNER, ML_HEADS, ML_HEAD_DIM
    B = ul.shape[0]

    def prep(u):
        L = u.shape[1]
        xm, z = jnp.split(u @ w_up, 2, axis=-1)
        xc = jax.nn.silu(dwconv_centred(xm, conv_w) + conv_b)
        q, k, v = blockdiag(xc, w_qkv[0]), blockdiag(xc, w_qkv[1]), blockdiag(xm, w_qkv[2])
        gates = (jnp.einsum('ble,xeg->xbgl', q, w_if[:, :E]) + jnp.einsum('ble,xeg->xbgl', k, w_if[:, E:2 * E])
                 + jnp.einsum('ble,xeg->xbgl', v, w_if[:, 2 * E:])).astype(jnp.float32) \
            + b_if.astype(jnp.float32)[:, None, :, None]
        heads = lambda a: a.reshape(B, L, H, dh).transpose(0, 2, 1, 3).astype(jnp.float32)
        return xc, z, heads(q), heads(k), heads(v), gates

    p_c, p_l = prep(uc), prep(ul)
    zero = (jnp.zeros((B, H, dh, dh), jnp.float32), jnp.zeros((B, H, dh), jnp.float32), jnp.zeros((B, H), jnp.float32))

    def scan_dir(p, direction, state):
        q, k, v, gates = p[2:]
        ig, fg = gates[direction, :, :H], gates[direction, :, H:]
        if direction == 1:
            q, k, v, ig, fg = [jnp.flip(a, axis=2) for a in (q, k, v, ig, fg)]
        state, h = mlstm_chunked(q, k, v, ig, fg, state)
        return state, (jnp.flip(h, axis=2) if direction == 1 else h)

    st_f, hc_f = scan_dir(p_c, 0, zero)
    st_b, hc_b = scan_dir(p_c, 1, zero)
    _, hl_f = scan_dir(p_l, 0, st_f)
    _, hl_b = scan_dir(p_l, 1, st_b)

    def finish(p, h):
        xc, z = p[0], p[1]
        L = xc.shape[1]
        h = jax.nn.sigmoid(z.astype(jnp.float32)).reshape(B, L, H, dh) * h.transpose(0, 2, 1, 3)
        mu = h.mean(-1, keepdims=True)
        var = jnp.mean(jnp.square(h - mu), -1, keepdims=True)
        h = ((h - mu) * lax.rsqrt(var + LN_EPS) * norm_g.astype(jnp.float32).reshape(H, dh)).reshape(B, L, E)
        return (h.astype(xc.dtype) + skip * xc) @ w_down

    return finish(p_l, hl_f + hl_b), (finish(p_c, hc_f + hc_b) if ctx_out else None)


def short_conv_mixer(u, w_in, conv_w, w_out):
    bg, cg, xt = jnp.split(u @ w_in, 3, axis=-1)
    return (bg * dwconv_centred(cg * xt, conv_w)) @ w_out


def setup_inputs(seed: int = 0) -> dict:
    key = jax.random.key(seed)
    ks = jax.random.split(key, 26)
    f32 = jnp.float32
    nrm = lambda k, shape, s: jax.random.normal(k, shape, f32) * s
    D, E, H, F = D_MODEL, ML_INNER, ML_HEADS, D_FF
    at_cols = (AT_HEADS + 2 * AT_KV_HEADS) * AT_HEAD_DIM
    ml_b_if = jnp.concatenate([nrm(ks[15], (N_A, 2, H), 0.1),
                               jnp.linspace(3.0, 6.0, H, dtype=f32) + nrm(ks[16], (N_A, 2, H), 0.1)], axis=-1)
    return {
        'x': nrm(ks[0], (BATCH, SEQ, D), 1.0),
        'c': nrm(ks[1], (BATCH, D), 1.0),
        'ctx': nrm(ks[2], (BATCH, CTX_LEN, D), 1.0),
        'c_ctx': nrm(ks[3], (D,), 1.0),
        'mod_w': nrm(ks[4], (DEPTH, D, N_MOD * D), 0.5 * D ** -0.5),
        'mod_b': nrm(ks[5], (DEPTH, N_MOD * D), 0.02),
        'ln_g': 1.0 + nrm(ks[6], (DEPTH, 3, D), 0.02),
        'ln_b': nrm(ks[7], (DEPTH, 3, D), 0.02),
        'ffn_w_in': nrm(ks[8], (DEPTH, 2, D, 2 * F), D ** -0.5),
        'ffn_w_out': nrm(ks[9], (DEPTH, 2, F, D), BETA * F ** -0.5),
        'ml_w_up': nrm(ks[10], (N_A, D, 2 * E), D ** -0.5),
        'ml_conv_w': nrm(ks[11], (N_A, ML_CONV, E), ML_CONV ** -0.5),
        'ml_conv_b': nrm(ks[12], (N_A, E), 0.02),
        'ml_w_qkv': nrm(ks[13], (N_A, 3, E // ML_QKV_BLOCK, ML_QKV_BLOCK, ML_QKV_BLOCK), ML_QKV_BLOCK ** -0.5),
        'ml_w_if': nrm(ks[14], (N_A, 2, 3 * E, 2 * H), 0.1 * (3 * E) ** -0.5),
        'ml_b_if': ml_b_if,
        'ml_skip': 1.0 + nrm(ks[17], (N_A, E), 0.02),
        'ml_norm_g': 1.0 + nrm(ks[18], (N_A, E), 0.02),
        'ml_w_down': nrm(ks[19], (N_A, E, D), BETA * E ** -0.5),
        'at_w_qkv': nrm(ks[20], (N_B, D, at_cols), D ** -0.5),
        'at_sink': nrm(ks[21], (N_B, AT_HEADS), 0.5),
        'at_w_o': nrm(ks[22], (N_B, AT_HEADS * AT_HEAD_DIM, D), BETA * (AT_HEADS * AT_HEAD_DIM) ** -0.5),
        'sc_w_in': nrm(ks[23], (N_C, D, 3 * D), D ** -0.5),
        'sc_conv_w': nrm(ks[24], (N_C, SC_WIDTH, D), SC_WIDTH ** -0.5),
        'sc_w_out': nrm(ks[25], (N_C, D, D), BETA * D ** -0.5),
    }


def reference(x, c, ctx, c_ctx, mod_w, mod_b, ln_g, ln_b, ffn_w_in, ffn_w_out,
              ml_w_up, ml_conv_w, ml_conv_b, ml_w_qkv, ml_w_if, ml_b_if, ml_skip, ml_norm_g, ml_w_down,
              at_w_qkv, at_sink, at_w_o, sc_w_in, sc_conv_w, sc_w_out):
    B = x.shape[0]
    lat_cond = jax.nn.silu(c)
    ctx_cond = jax.nn.silu(c_ctx)
    h, hc = x, ctx
    for i in range(DEPTH):
        kind, j, last = i % N_MIXERS, i // N_MIXERS, i == DEPTH - 1
        m_l = (lat_cond @ mod_w[i] + mod_b[i]).reshape(B, 1, N_MOD, D_MODEL)
        m_c = (ctx_cond @ mod_w[i] + mod_b[i]).reshape(N_MOD, D_MODEL)
        h = residual_post_norm(h, swiglu(modulate(h, m_l, 0), ffn_w_in[i, 0], ffn_w_out[i, 0]),
                               gate_of(m_l, 0), 0.5, ln_g[i, 0], ln_b[i, 0])
        hc = residual_post_norm(hc, swiglu(modulate(hc, m_c, 0), ffn_w_in[i, 0], ffn_w_out[i, 0]),
                                gate_of(m_c, 0), 0.5, ln_g[i, 0], ln_b[i, 0])
        ul, uc = modulate(h, m_l, 1), modulate(hc, m_c, 1)
        if kind == 0:
            yl, yc = mlstm_mixer(ul, uc, ml_w_up[j], ml_conv_w[j], ml_conv_b[j], ml_w_qkv[j], ml_w_if[j],
                                 ml_b_if[j], ml_skip[j], ml_norm_g[j], ml_w_down[j], not last)
        elif kind == 1:
            yl, yc = window_attention(ul, uc, at_w_qkv[j], at_sink[j], at_w_o[j], not last)
        else:
            yl = short_conv_mixer(ul, sc_w_in[j], sc_conv_w[j], sc_w_out[j])
            yc = None if last else short_conv_mixer(uc, sc_w_in[j], sc_conv_w[j], sc_w_out[j])
        h = residual_post_norm(h, yl, gate_of(m_l, 1), 1.0, ln_g[i, 1], ln_b[i, 1])
        h = residual_post_norm(h, swiglu(modulate(h, m_l, 2), ffn_w_in[i, 1], ffn_w_out[i, 1]),
                               gate_of(m_l, 2), 0.5, ln_g[i, 2], ln_b[i, 2])
        if not last:
            hc = residual_post_norm(hc, yc, gate_of(m_c, 1), 1.0, ln_g[i, 1], ln_b[i, 1])
            hc = residual_post_norm(hc, swiglu(modulate(hc, m_c, 2), ffn_w_in[i, 1], ffn_w_out[i, 1]),
                                    gate_of(m_c, 2), 0.5, ln_g[i, 2], ln_b[i, 2])
    return h
```

```python
import contextlib
import numpy as np
import concourse.bass as bass
import concourse.mybir as mybir
from concourse.bass_utils import run_bass_kernel_spmd

F32 = mybir.dt.float32
BF16 = mybir.dt.bfloat16
I32 = mybir.dt.int32
AF = mybir.ActivationFunctionType
ALU = mybir.AluOpType
AX = mybir.AxisListType


ATTACH_WAITS = True


class Op:
    __slots__ = ("idx", "eng", "fn", "dma", "deps", "signal", "sem", "val", "prewait")

    def __init__(self, idx, eng, fn, dma):
        self.idx = idx
        self.eng = eng
        self.fn = fn
        self.dma = dma
        self.deps = set()
        self.signal = False
        self.sem = None
        self.val = 0
        self.prewait = None


class Prog:
    ENGS = ("pe", "act", "dve", "pool", "sp")

    def __init__(self, nc, n_dma_sems=6):
        self.nc = nc
        self.ops = []
        self.lastw = {}
        self.readers = {}
        self.n_dma_sems = n_dma_sems

    def add(self, eng, fn, reads=(), writes=(), dma=False):
        op = Op(len(self.ops), eng, fn, dma)
        raw = set()
        for r in reads:
            w = self.lastw.get(r)
            if w is not None:
                raw.add(w)
        other = set()
        for r in writes:
            w = self.lastw.get(r)
            if w is not None:
                other.add(w)
            for q in self.readers.get(r, ()):
                other.add(q)
        op.deps = set(raw)
        for d in other:
            dop = self.ops[d]
            if (not dma) and (not dop.dma) and dop.eng == eng and eng != "pool":
                continue
            op.deps.add(d)
        for r in reads:
            self.readers.setdefault(r, []).append(op.idx)
        for r in writes:
            self.lastw[r] = op.idx
            self.readers[r] = []
        op.deps.discard(op.idx)
        self.ops.append(op)
        return op

    def pe(self, fn, reads=(), writes=()):
        return self.add("pe", fn, reads, writes)

    def act(self, fn, reads=(), writes=()):
        return self.add("act", fn, reads, writes)

    def dve(self, fn, reads=(), writes=()):
        return self.add("dve", fn, reads, writes)

    def pool(self, fn, reads=(), writes=()):
        return self.add("pool", fn, reads, writes)

    def dma(self, out, in_, reads=(), writes=(), q="sp"):
        return self.add(q, lambda e: e.dma_start(out=out, in_=in_), reads, writes, dma=True)

    def barrier(self):
        last = {}
        dmas = {e: [] for e in self.ENGS}
        for op in self.ops:
            if op.fn is None:
                continue
            if op.dma:
                dmas[op.eng].append(op.idx)
            else:
                last[op.eng] = op.idx
        deps = set(last.values())
        for e in self.ENGS:
            deps.update(dmas[e][-self.n_dma_sems:])
        for e in self.ENGS:
            op = Op(len(self.ops), e, None, False)
            op.deps = set(deps)
            self.ops.append(op)
        self.lastw = {}
        self.readers = {}

    def emit(self, final_waits=()):
        nc = self.nc
        ops = self.ops
        for op in ops:
            if op.eng == "pe" and not op.dma:
                op.deps = {d for d in op.deps if not (ops[d].eng == "pe" and not ops[d].dma)}
            op.deps = {d for d in op.deps if ops[d].fn is not None}
            for d in op.deps:
                ops[d].signal = True
        for idx in final_waits:
            ops[idx].signal = True
        esem = {e: nc.alloc_semaphore("s_" + e) for e in self.ENGS}
        dsem = {e: [nc.alloc_semaphore("d_%s%d" % (e, i)) for i in range(self.n_dma_sems)]
                for e in ("sp", "act", "pool")}
        ecount = {e: 0 for e in self.ENGS}
        dcount = {e: 0 for e in self.ENGS}
        P = self.n_dma_sems
        for op in ops:
            if op.dma:
                i = dcount[op.eng]
                dcount[op.eng] += 1
                op.sem = dsem[op.eng][i % P]
                op.val = 16 * (i // P + 1)
                op.signal = True
                if i >= P:
                    op.prewait = (op.sem, 16 * (i // P))
            elif op.signal:
                ecount[op.eng] += 1
                op.sem = esem[op.eng]
                op.val = ecount[op.eng]
        per_eng = {e: [] for e in self.ENGS}
        for op in ops:
            per_eng[op.eng].append(op)
        self.stats = {e: len(v) for e, v in per_eng.items()}
        nwaits = [0]

        def run(e, eng_handle, extra_final):
            waited = {}

            def wait(sem, val):
                k = id(sem)
                if waited.get(k, 0) >= val:
                    return
                waited[k] = val
                eng_handle.wait_ge(sem, val)
                nwaits[0] += 1

            for op in per_eng[e]:
                if op.prewait is not None:
                    wait(*op.prewait)
                need = {}
                for d in op.deps:
                    dop = ops[d]
                    k = id(dop.sem)
                    if k not in need or need[k][1] < dop.val:
                        need[k] = (dop.sem, dop.val)
                pend = [(sem, val) for sem, val in need.values() if waited.get(id(sem), 0) < val]
                if op.fn is None:
                    for sem, val in pend:
                        wait(sem, val)
                    continue
                attach = None
                if pend and ATTACH_WAITS:
                    attach = pend.pop()
                for sem, val in pend:
                    wait(sem, val)
                ins = op.fn(eng_handle)
                if attach is not None:
                    ins._wait_ge(attach[0], attach[1])
                    waited[id(attach[0])] = attach[1]
                if op.signal:
                    ins.then_inc(op.sem, 16 if op.dma else 1)
            if extra_final:
                for idx in final_waits:
                    wait(ops[idx].sem, ops[idx].val)
                for q in ("sp", "act", "pool"):
                    n = dcount[q]
                    for s in range(min(P, n)):
                        last_i = ((n - 1 - s) // P) * P + s
                        wait(dsem[q][s], 16 * (last_i // P + 1))

        with nc.Block() as block:
            @block.tensor
            def _(eng):
                run("pe", eng, False)

            @block.scalar
            def _(eng):
                run("act", eng, False)

            @block.vector
            def _(eng):
                run("dve", eng, False)

            @block.gpsimd
            def _(eng):
                run("pool", eng, False)

            @block.sync
            def _(eng):
                run("sp", eng, True)
        self.stats["waits"] = nwaits[0]
        return self.stats

import numpy as np

D = 1024
NCTX = 256
SEQ = 4096
LT = NCTX + SEQ
DEPTH = 4
DFF = 2816
NJ = DFF // 128
ALPHA = (2 * DEPTH) ** 0.25
LN_EPS = 1e-5
TILES = [(0, 256, 1)] + [(256 + 512 * i, 512, 0) for i in range(8)]

VEC_SPEC = [("cond", 16), ("mod_b", 4 * 72), ("ln_g", 4 * 3 * 8), ("ln_b", 4 * 3 * 8),
            ("ml_conv_w", 2 * 3 * 16), ("ml_conv_b", 2 * 16), ("ml_skip", 2 * 16),
            ("sc_conv_w", 3 * 8), ("at_sink", 16), ("ml_b_if", 2 * 16)]
VOFF = {}
_o = 0
for _n, _c in VEC_SPEC:
    VOFF[_n] = _o
    _o += _c
NV = _o


_UNC = [0]


def UN(name):
    _UNC[0] += 1
    return "%s_%d" % (name, _UNC[0])


def fm(v):
    v = np.asarray(v, np.float32)
    F = v.shape[-1]
    lead = v.shape[:-1]
    a = v.reshape(lead + (F // 128, 128))
    a = np.moveaxis(a, -1, 0)
    return a.reshape(128, -1)


def pack_vecs(inp, b):
    out = np.zeros((128, NV), np.float32)

    def put(name, arr):
        arr = np.asarray(arr, np.float32)
        out[:, VOFF[name]:VOFF[name] + arr.shape[1]] = arr

    cond = np.stack([inp["c"][b], inp["c_ctx"]], axis=-1)
    put("cond", cond.reshape(8, 128, 2).transpose(1, 0, 2).reshape(128, 16))
    put("mod_b", fm(inp["mod_b"]))
    put("ln_g", fm(inp["ln_g"]))
    put("ln_b", fm(inp["ln_b"]))
    put("ml_conv_w", fm(inp["ml_conv_w"]))
    put("ml_conv_b", fm(inp["ml_conv_b"]))
    put("ml_skip", fm(inp["ml_skip"]))
    put("sc_conv_w", fm(inp["sc_conv_w"]))
    put("at_sink", np.broadcast_to(inp["at_sink"].reshape(1, 16), (128, 16)))
    put("ml_b_if", np.broadcast_to(inp["ml_b_if"].reshape(1, 32), (128, 32)))
    return out


class KB:
    def __init__(self, nc):
        self.nc = nc
        self.P = Prog(nc)
        self.uid = 0
        nc_ = nc
        self.pb = [nc_.alloc_psum_tensor("pb%d" % i, [128, 512], F32) for i in range(8)]
        self.vecs = nc_.alloc_sbuf_tensor("sb_vecs", [128, NV], F32)
        self.mod = nc_.alloc_sbuf_tensor("sb_mod", [128, DEPTH, 72, 2], F32)
        self.ident = nc_.alloc_sbuf_tensor("sb_ident", [128, 128], F32)
        self.identb = nc_.alloc_sbuf_tensor("identb", [128, 128], BF16)
        self.onesm = nc_.alloc_sbuf_tensor("onesm", [128, 128], F32)
        self.epsc = nc_.alloc_sbuf_tensor("epsc", [128, 4], F32)
        self.dram = {}

    def din(self, name, shape, dt=F32):
        t = self.nc.dram_tensor(name, list(shape), dt, kind="ExternalInput")
        self.dram[name] = t
        return t.ap()

    def dout(self, name, shape, dt=F32):
        t = self.nc.dram_tensor(name, list(shape), dt, kind="ExternalOutput")
        self.dram[name] = t
        return t.ap()

    def dscr(self, name, shape, dt=F32):
        t = self.nc.dram_tensor(name, list(shape), dt)
        self.dram[name] = t
        return t.ap()

    def v(self, name, idx=0, n=1):
        o = VOFF[name] + idx
        return self.vecs[:, o:o + n]

    def mv(self, i, slot, c, col):
        return self.mod[:, i, slot * 8 + c, col:col + 1]


def setup_consts(kb, vecs_d, ident_d):
    P = kb.P
    P.dma(kb.vecs[:], vecs_d, writes=["vecs"])
    P.dma(kb.ident[:], ident_d, writes=["ident"])
    P.dve(lambda e: e.tensor_copy(out=kb.identb[:], in_=kb.ident[:]), reads=["ident"], writes=["identb"])
    P.pool(lambda e: e.memset(kb.onesm[:], 1.0 / 1024.0), writes=["onesm"])
    P.pool(lambda e: e.memset(kb.epsc[:, 0:1], LN_EPS / (ALPHA * ALPHA)), writes=["epsc"])
    P.pool(lambda e: e.memset(kb.epsc[:, 1:2], LN_EPS), writes=["epsc"])
    P.pool(lambda e: e.memset(kb.epsc[:, 2:3], -0.5 * float(np.log(512.0))), writes=["epsc"])
    P.pool(lambda e: e.memset(kb.epsc[:, 3:4], 1.0), writes=["epsc"])


def compute_mod(kb, modw_d, layers, dmap=None):
    nc, P = kb.nc, kb.P
    condT = kb.v("cond", 0, 16)
    P.act(lambda e: e.activation(out=condT, in_=condT, func=AF.Silu), reads=["vecs"], writes=["vecs"])
    with nc.sbuf_tensor(UN("mw_stage"), [128, 2, 8, 512], F32) as stage, \
            nc.sbuf_tensor(UN("mrow"), [2, 9216], F32) as mrow:
        cnt = 0
        for i in layers:
            for nb in range(18):
                s = cnt % 2
                cnt += 1
                src = modw_d[i if dmap is None else dmap[i]].rearrange("(kc p) n -> p kc n", p=128)[:, :, nb * 512:(nb + 1) * 512]
                P.dma(stage[:, s], src, writes=[("mws", s)])
                ps = kb.pb[s]
                for kc in range(8):
                    P.pe(lambda e, kc=kc, s=s, ps=ps: e.matmul(
                        ps[0:2, :], lhsT=kb.vecs[:, VOFF["cond"] + 2 * kc:VOFF["cond"] + 2 * kc + 2],
                        rhs=stage[:, s, kc, :], start=(kc == 0), stop=(kc == 7)),
                        reads=["vecs", ("mws", s)], writes=[("pb", s)])
                P.act(lambda e, nb=nb, ps=ps: e.activation(out=mrow[0:2, nb * 512:(nb + 1) * 512], in_=ps[0:2, :],
                                                            func=AF.Identity),
                      reads=[("pb", s)], writes=["mrow"])
            pst = kb.pb[2]
            for k in range(72):
                P.pe(lambda e, k=k: e.transpose(out=pst[:, 2 * k:2 * k + 2], in_=mrow[0:2, k * 128:(k + 1) * 128],
                                                identity=kb.ident[0:2, 0:2]),
                     reads=["mrow", "ident"], writes=[("pb", 2)])
            for j in range(2):
                P.dve(lambda e, i=i, j=j: e.tensor_tensor(
                    out=kb.mod[:, i, :, j], in0=pst[:, j:144:2], in1=kb.v("mod_b", i * 72, 72), op=ALU.add),
                    reads=[("pb", 2), "vecs"], writes=[("mod", i)])
            for s3 in range(3):
                w = 1.0 if s3 == 1 else 0.5
                sl = kb.mod[:, i, (3 * s3 + 1) * 8:(3 * s3 + 2) * 8, :]
                P.dve(lambda e, sl=sl: e.tensor_scalar(out=sl, in0=sl, scalar1=1.0, scalar2=None, op0=ALU.add),
                      reads=[("mod", i)], writes=[("mod", i)])
                gl = kb.mod[:, i, (3 * s3 + 2) * 8:(3 * s3 + 3) * 8, :]
                P.dve(lambda e, gl=gl, w=w: e.tensor_scalar(out=gl, in0=gl, scalar1=w / ALPHA, scalar2=None,
                                                            op0=ALU.mult),
                      reads=[("mod", i)], writes=[("mod", i)])
    P.barrier()


class LNBufs:
    def __init__(self, nc, es, n, tag):
        self.n = n
        self.z = es.enter_context(nc.sbuf_tensor(UN("lnz" + tag), [128, 8, n], F32))
        self.zq = es.enter_context(nc.sbuf_tensor(UN("lnzq" + tag), [128, 8, n], F32))
        self.sm = es.enter_context(nc.sbuf_tensor(UN("lnsm" + tag), [128, 3, n], F32))
        self.tag = tag


def ln_accum(kb, lb, o, y_ps, y_res, hsrc, hres, gs_ap, n, extra_reads=()):
    P = kb.P
    t = lb.tag
    P.dve(lambda e: e.scalar_tensor_tensor(out=lb.z[:, o, :n], in0=y_ps, scalar=gs_ap, in1=hsrc,
                                           op0=ALU.mult, op1=ALU.add),
          reads=[y_res, hres] + list(extra_reads), writes=[("lnz" + t, o)])
    P.act(lambda e: e.activation(out=lb.zq[:, o, :n], in_=lb.z[:, o, :n], func=AF.Square),
          reads=[("lnz" + t, o)], writes=[("lnzq" + t, o)])
    P.pe(lambda e: e.matmul(kb.pb[6][:, :n], lhsT=kb.onesm[:], rhs=lb.z[:, o, :n], start=(o == 0), stop=(o == 7)),
         reads=[("lnz" + t, o), "onesm"], writes=[("pb", 6)])
    P.pe(lambda e: e.matmul(kb.pb[7][:, :n], lhsT=kb.onesm[:], rhs=lb.zq[:, o, :n], start=(o == 0), stop=(o == 7)),
         reads=[("lnzq" + t, o), "onesm"], writes=[("pb", 7)])


def ln_finish(kb, lb, n, li, k, dst_ap_fn, dst_res):
    P = kb.P
    t = lb.tag
    sm = lb.sm
    mean, msq, std = (sm[:, i, :n] for i in range(3))
    var, rstd, nmr = msq, std, mean
    R = ("lnsm" + t)
    P.act(lambda e: e.activation(out=mean, in_=kb.pb[6][:, :n], func=AF.Identity), reads=[("pb", 6)], writes=[(R, 0)])
    P.pool(lambda e: e.tensor_tensor(out=msq, in0=mean, in1=mean, op=ALU.mult), reads=[(R, 0)], writes=[(R, 1)])
    P.dve(lambda e: e.tensor_tensor(out=var, in0=kb.pb[7][:, :n], in1=msq, op=ALU.subtract),
          reads=[("pb", 7), (R, 1)], writes=[(R, 1)])
    P.act(lambda e: e.activation(out=std, in_=var, func=AF.Sqrt, bias=kb.epsc[:, 0:1]), reads=[(R, 1), "epsc"],
          writes=[(R, 2)])
    P.dve(lambda e: e.reciprocal(out=rstd, in_=std), reads=[(R, 2)], writes=[(R, 2)])
    P.dve(lambda e: e.scalar_tensor_tensor(out=nmr, in0=mean, scalar=-1.0, in1=rstd, op0=ALU.mult, op1=ALU.mult),
          reads=[(R, 0), (R, 2)], writes=[(R, 0)])
    for o in range(8):
        zo = lb.z[:, o, :n]
        P.dve(lambda e, zo=zo: e.tensor_tensor(out=zo, in0=zo, in1=rstd, op=ALU.mult),
              reads=[("lnz" + t, o), (R, 2)], writes=[("lnz" + t, o)])
        P.pool(lambda e, zo=zo: e.tensor_tensor(out=zo, in0=zo, in1=nmr, op=ALU.add),
               reads=[("lnz" + t, o), (R, 0)], writes=[("lnz" + t, o)])
        ho = lb.zq[:, o, :n]
        g_ap = kb.v("ln_g", (li * 3 + k) * 8 + o)
        b_ap = kb.v("ln_b", (li * 3 + k) * 8 + o)
        P.act(lambda e, zo=zo, ho=ho, g_ap=g_ap, b_ap=b_ap: e.activation(out=ho, in_=zo, func=AF.Identity,
                                                                         bias=b_ap, scale=g_ap),
              reads=[("lnz" + t, o), "vecs"], writes=[("lnzq" + t, o)])
    ops = []
    ops.append(P.dma(dst_ap_fn(), lb.zq[:, :, :n], reads=[("lnzq" + t, o) for o in range(8)], writes=list(dst_res),
                     q="act"))
    return ops


def hres(name, t0, n):
    return [(name, c) for c in range(t0 // 128, (t0 + n + 127) // 128)]


def hT_tile(h_ap, t0, n):
    return h_ap.rearrange("(c p) t -> p c t", p=128)[:, :, t0:t0 + n]


def ffn_sublayer(kb, li, which, h_in, h_in_res, h_out, h_out_res, w_in_d, w_out_d, tiles, out_col0=None):
    import contextlib
    nc, P = kb.nc, kb.P
    slot = 0 if which == 0 else 2
    k = slot
    w_in = w_in_d[li, which]
    w_out = w_out_d[li, which]
    w_in_v = w_in.rearrange("(kc p) n -> p kc n", p=128)
    w_out_v = w_out.rearrange("(j p) n -> p j n", p=128)
    tag = "f"
    final_ops = []
    with contextlib.ExitStack() as es:
        A = lambda name, shape, dt: es.enter_context(nc.sbuf_tensor(UN(name), shape, dt))
        stg = A("ff_stg", [128, 2, 8, 512], F32)
        wb = A("ff_wb", [128, 2, 8, 512], BF16)
        stgo = A("ff_stgo", [128, 22, 128], F32)
        wob = A("ff_wob", [128, 2, 22, 128], BF16)
        ht = A("ff_ht", [128, 2, 8, 512], F32)
        u = A("ff_u", [128, 8, 512], BF16)
        a = A("ff_a", [128, 22, 512], BF16)
        sg = A("ff_sg", [128, 2, 512], F32)
        lb = LNBufs(nc, es, 512, tag)
        wcnt = 0
        ocnt = 0
        gcnt = 0
        for ti, (t0, n, col) in enumerate(tiles):
            hs = ti % 2
            P.dma(ht[:, hs, :, :n], hT_tile(h_in, t0, n), reads=hres(h_in_res, t0, n), writes=[("ff_ht", hs)])
            for c in range(8):
                P.act(lambda e, c=c, hs=hs, n=n, col=col: e.activation(
                    out=u[:, c, :n], in_=ht[:, hs, c, :n], func=AF.Identity,
                    bias=kb.mv(li, 3 * slot, c, col), scale=kb.mv(li, 3 * slot + 1, c, col)),
                    reads=[("ff_ht", hs), ("mod", li)], writes=[("ff_u", c)])
            for jp in range(NJ // 2):
                ws = wcnt % 2
                wcnt += 1
                P.dma(stg[:, ws, :, 0:256], w_in_v[:, :, jp * 256:(jp + 1) * 256], writes=[("ff_stg", ws, 0)])
                P.dma(stg[:, ws, :, 256:512], w_in_v[:, :, DFF + jp * 256:DFF + (jp + 1) * 256],
                      writes=[("ff_stg", ws, 1)])
                for hf in range(2):
                    P.pool(lambda e, ws=ws, hf=hf: e.tensor_copy(out=wb[:, ws, :, hf * 256:(hf + 1) * 256],
                                                                 in_=stg[:, ws, :, hf * 256:(hf + 1) * 256]),
                           reads=[("ff_stg", ws, hf)], writes=[("ff_wb", ws, hf)])
                for jj in range(2):
                    j = jp * 2 + jj
                    gs_ = gcnt % 2
                    gcnt += 1
                    pg, pv = kb.pb[gs_], kb.pb[2 + gs_]
                    for kc in range(8):
                        P.pe(lambda e, kc=kc, ws=ws, jj=jj, pg=pg, n=n: e.matmul(
                            pg[:, :n], lhsT=wb[:, ws, kc, jj * 128:(jj + 1) * 128], rhs=u[:, kc, :n],
                            start=(kc == 0), stop=(kc == 7)),
                            reads=[("ff_wb", ws, 0), ("ff_u", kc)], writes=[("pb", gs_)])
                    for kc in range(8):
                        P.pe(lambda e, kc=kc, ws=ws, jj=jj, pv=pv, n=n: e.matmul(
                            pv[:, :n], lhsT=wb[:, ws, kc, 256 + jj * 128:256 + (jj + 1) * 128], rhs=u[:, kc, :n],
                            start=(kc == 0), stop=(kc == 7)),
                            reads=[("ff_wb", ws, 1), ("ff_u", kc)], writes=[("pb", 2 + gs_)])
                    P.act(lambda e, gs_=gs_, pg=pg, n=n: e.activation(out=sg[:, gs_, :n], in_=pg[:, :n], func=AF.Silu),
                          reads=[("pb", gs_)], writes=[("ff_sg", gs_)])
                    P.dve(lambda e, gs_=gs_, pv=pv, j=j, n=n: e.tensor_tensor(out=a[:, j, :n], in0=pv[:, :n],
                                                                              in1=sg[:, gs_, :n], op=ALU.mult),
                          reads=[("pb", 2 + gs_), ("ff_sg", gs_)], writes=[("ff_a", j)])
            for o in range(8):
                os_ = ocnt % 2
                ocnt += 1
                P.dma(stgo[:], w_out_v[:, :, o * 128:(o + 1) * 128], writes=["ff_stgo"])
                P.pool(lambda e, os_=os_: e.tensor_copy(out=wob[:, os_], in_=stgo[:]),
                       reads=["ff_stgo"], writes=[("ff_wob", os_)])
                py = kb.pb[4 + os_]
                for j in range(NJ):
                    P.pe(lambda e, j=j, os_=os_, py=py, n=n: e.matmul(
                        py[:, :n], lhsT=wob[:, os_, j, :], rhs=a[:, j, :n], start=(j == 0), stop=(j == NJ - 1)),
                        reads=[("ff_wob", os_), ("ff_a", j)], writes=[("pb", 4 + os_)])
                ln_accum(kb, lb, o, py[:, :n], ("pb", 4 + os_), ht[:, hs, o, :n], ("ff_ht", hs),
                         kb.mv(li, 3 * slot + 2, o, col), n, extra_reads=[("mod", li)])
            oc = t0 - (out_col0 or 0)
            final_ops += ln_finish(kb, lb, n, li, k, lambda oc=oc, n=n: hT_tile(h_out, oc, n), hres(h_out_res, t0, n))
    kb.P.barrier()
    return final_ops

import contextlib

NEG = -30000.0


def load_weight_bf16(kb, stage, wb, res, w_ap, KC, N, col0=0, eng="pool"):
    nc, P = kb.nc, kb.P
    wv = w_ap.rearrange("(kc p) n -> p kc n", p=128)
    nb = min(N, stage.shape[-1])
    for kc in range(KC):
        for n0 in range(0, N, nb):
            n1 = min(N, n0 + nb)
            s = kb.uid % 2
            kb.uid += 1
            P.dma(stage[:, s, :n1 - n0], wv[:, kc, col0 + n0:col0 + n1], writes=[("wstage", s)])
            dst = wb[:, kc, n0:n1]
            src = stage[:, s, :n1 - n0]
            if eng == "act":
                P.act(lambda e, dst=dst, src=src: e.activation(out=dst, in_=src, func=AF.Identity),
                      reads=[("wstage", s)], writes=[res])
            else:
                P.add(eng, lambda e, dst=dst, src=src: e.tensor_copy(out=dst, in_=src),
                      reads=[("wstage", s)], writes=[res])


def load_mod_tile(kb, ht, u, rname, h_in, h_in_res, li, slot, t0, n, col, s0, s1, halo):
    P = kb.P
    a0 = max(t0 - halo, s0)
    a1 = min(t0 + n + halo, s1)
    off = a0 - (t0 - halo)
    w = a1 - a0
    P.dma(ht[:, :, off:off + w], hT_tile(h_in, a0, w), reads=hres(h_in_res, a0, w), writes=[rname + "_ht"])
    for c in range(8):
        P.act(lambda e, c=c: e.activation(out=u[:, c, off:off + w], in_=ht[:, c, off:off + w], func=AF.Identity,
                                          bias=kb.mv(li, 3 * slot, c, col), scale=kb.mv(li, 3 * slot + 1, c, col)),
              reads=[rname + "_ht", ("mod", li)], writes=[(rname + "_u", c)])
        tot = n + 2 * halo
        for (z0, z1) in ((0, off), (off + w, tot)):
            if z1 > z0:
                P.act(lambda e, c=c, z0=z0, z1=z1: e.activation(out=u[:, c, z0:z1], in_=ht[:, c, off:off + z1 - z0],
                                                                func=AF.Identity, scale=0.0),
                      reads=[rname + "_ht"], writes=[(rname + "_u", c)])
    return off, w


def proj_ln(kb, lb, xin, xin_res, KC, wob, wob_res, hsrc_fn, hsrc_res, li, n, col, dst_fn, dst_res):
    P = kb.P
    for o in range(8):
        py = kb.pb[4 + (o % 2)]
        for kc in range(KC):
            P.pe(lambda e, o=o, kc=kc, py=py: e.matmul(py[:, :n], lhsT=wob[:, kc, o * 128:(o + 1) * 128],
                                                        rhs=xin[:, kc, :n], start=(kc == 0), stop=(kc == KC - 1)),
                 reads=[wob_res, xin_res(kc)], writes=[("pb", 4 + (o % 2))])
        ln_accum(kb, lb, o, py[:, :n], ("pb", 4 + (o % 2)), hsrc_fn(o), hsrc_res, kb.mv(li, 5, o, col), n,
                 extra_reads=[("mod", li)])
    return ln_finish(kb, lb, n, li, 1, dst_fn, dst_res)


def split_cols(m):
    if m <= 512:
        return [(0, m)]
    h = m // 2
    return [(0, h), (h, m)]


def shortconv_mixer(kb, li, h_in, h_in_res, h_out, h_out_res, w_in_d, w_out_d, tiles):
    nc, P = kb.nc, kb.P
    finals = []
    with contextlib.ExitStack() as es:
        A = lambda name, shape, dt: es.enter_context(nc.sbuf_tensor(UN(name), shape, dt))
        win = A("sc_win", [128, 8, 3072], BF16)
        wout = A("sc_wout", [128, 8, 1024], BF16)
        wst = A("sc_wst", [128, 2, 2048], F32)
        load_weight_bf16(kb, wst, win, "sc_win", w_in_d[0], 8, 3072)
        load_weight_bf16(kb, wst, wout, "sc_wout", w_out_d[0], 8, 1024)
        ht = A("sc_ht", [128, 8, 514], F32)
        u = A("sc_u", [128, 8, 514], BF16)
        cgs = A("sc_cg", [128, 514], F32)
        T = A("sc_T", [128, 514], F32)
        cv = A("sc_cv", [128, 512], F32)
        pin = A("sc_pin", [128, 8, 512], BF16)
        lb = LNBufs(nc, es, 512, "s")
        for (t0, n, col) in tiles:
            s0, s1 = (0, NCTX) if col == 1 else (NCTX, LT)
            off, w = load_mod_tile(kb, ht, u, "sc", h_in, h_in_res, li, 1, t0, n, col, s0, s1, 1)
            m = n + 2
            for e_ in range(8):
                for (hi, (c0, c1)) in enumerate(split_cols(m)):
                    pc, px = kb.pb[hi], kb.pb[2 + hi]
                    for kc in range(8):
                        P.pe(lambda e, kc=kc, e_=e_, pc=pc, c0=c0, c1=c1: e.matmul(
                            pc[:, :c1 - c0], lhsT=win[:, kc, 1024 + e_ * 128:1024 + (e_ + 1) * 128],
                            rhs=u[:, kc, c0:c1], start=(kc == 0), stop=(kc == 7)),
                            reads=["sc_win", ("sc_u", kc)], writes=[("pb", hi)])
                    for kc in range(8):
                        P.pe(lambda e, kc=kc, e_=e_, px=px, c0=c0, c1=c1: e.matmul(
                            px[:, :c1 - c0], lhsT=win[:, kc, 2048 + e_ * 128:2048 + (e_ + 1) * 128],
                            rhs=u[:, kc, c0:c1], start=(kc == 0), stop=(kc == 7)),
                            reads=["sc_win", ("sc_u", kc)], writes=[("pb", 2 + hi)])
                    P.act(lambda e, pc=pc, c0=c0, c1=c1: e.activation(out=cgs[:, c0:c1], in_=pc[:, :c1 - c0],
                                                                      func=AF.Identity),
                          reads=[("pb", hi)], writes=["sc_cg"])
                    P.dve(lambda e, px=px, c0=c0, c1=c1: e.tensor_tensor(out=T[:, c0:c1], in0=px[:, :c1 - c0],
                                                                         in1=cgs[:, c0:c1], op=ALU.mult),
                          reads=[("pb", 2 + hi), "sc_cg"], writes=["sc_T"])
                w0, w1, w2 = (kb.v("sc_conv_w", k_ * 8 + e_) for k_ in range(3))
                P.dve(lambda e, n=n, w1=w1: e.tensor_scalar(out=cv[:, :n], in0=T[:, 1:n + 1], scalar1=w1, scalar2=None,
                                                            op0=ALU.mult),
                      reads=["sc_T", "vecs"], writes=["sc_cv"])
                P.dve(lambda e, n=n, w0=w0: e.scalar_tensor_tensor(out=cv[:, :n], in0=T[:, 0:n], scalar=w0,
                                                                   in1=cv[:, :n], op0=ALU.mult, op1=ALU.add),
                      reads=["sc_T", "sc_cv"], writes=["sc_cv"])
                P.dve(lambda e, n=n, w2=w2: e.scalar_tensor_tensor(out=cv[:, :n], in0=T[:, 2:n + 2], scalar=w2,
                                                                   in1=cv[:, :n], op0=ALU.mult, op1=ALU.add),
                      reads=["sc_T", "sc_cv"], writes=["sc_cv"])
                pbg = kb.pb[4 + (e_ % 2)]
                for kc in range(8):
                    P.pe(lambda e, kc=kc, e_=e_, pbg=pbg, n=n: e.matmul(
                        pbg[:, :n], lhsT=win[:, kc, e_ * 128:(e_ + 1) * 128], rhs=u[:, kc, 1:n + 1],
                        start=(kc == 0), stop=(kc == 7)),
                        reads=["sc_win", ("sc_u", kc)], writes=[("pb", 4 + (e_ % 2))])
                P.dve(lambda e, pbg=pbg, e_=e_, n=n: e.tensor_tensor(out=pin[:, e_, :n], in0=pbg[:, :n], in1=cv[:, :n],
                                                                     op=ALU.mult),
                      reads=[("pb", 4 + (e_ % 2)), "sc_cv"], writes=[("sc_pin", e_)])
            finals += proj_ln(kb, lb, pin, lambda kc: ("sc_pin", kc), 8, wout, "sc_wout",
                              lambda o, n=n: ht[:, o, 1:n + 1], "sc_ht", li, n, col,
                              lambda t0=t0, n=n: hT_tile(h_out, t0, n), hres(h_out_res, t0, n))
    kb.P.barrier()
    return finals


def attention_mixer(kb, li, h_in, h_in_res, h_out, h_out_res, wqkv_d, wsw_d, wo_d, rope_d, tiles):
    nc, P = kb.nc, kb.P
    finals = []
    with contextlib.ExitStack() as es0:
      A0 = lambda name, shape, dt: es0.enter_context(nc.sbuf_tensor(UN(name), shape, dt))
      KT = A0("at_KT", [128, 4, LT], BF16)
      Vt = A0("at_Vt", [128, LT // 128, 256], BF16)
      maskb = A0("at_mask", [128, 384], F32)
      P.pool(lambda e: e.memset(maskb[:], 0.0), writes=["at_mask"])
      P.pool(lambda e: e.affine_select(out=maskb[:], in_=maskb[:], pattern=[[1, 384]], compare_op=ALU.is_ge,
                                       fill=NEG, base=0, channel_multiplier=-1),
             reads=["at_mask"], writes=["at_mask"])
      P.pool(lambda e: e.affine_select(out=maskb[:], in_=maskb[:], pattern=[[-1, 384]], compare_op=ALU.is_ge,
                                       fill=NEG, base=256, channel_multiplier=1),
             reads=["at_mask"], writes=["at_mask"])
      def phase1():
          with contextlib.ExitStack() as es:
            A = lambda name, shape, dt: es.enter_context(nc.sbuf_tensor(UN(name), shape, dt))
            wkd = A("at_wkd", [128, 8, 4, 128], BF16)
            wksd = A("at_wksd", [128, 8, 4, 128], BF16)
            wv = A("at_wv", [128, 8, 256], BF16)
            wst = A("at_wst", [128, 2, 512], F32)
            load_weight_bf16(kb, wst, wv, "at_wv", wqkv_d[0], 8, 256, 1280)
            for (w_ap, dst, rd) in ((wqkv_d[0], wkd, "at_wkd"), (wsw_d, wksd, "at_wksd")):
                wvw = w_ap.rearrange("(kc p) n -> p kc n", p=128)
                for kc in range(8):
                    s = kb.uid % 2
                    kb.uid += 1
                    P.dma(wst[:, s, :256], wvw[:, kc, 1024:1280], writes=[("wstage", s)])
                    for half in range(2):
                        P.pool(lambda e, dst=dst, kc=kc, half=half, s=s: e.tensor_copy(
                            out=dst[:, kc, :, half * 64:(half + 1) * 64],
                            in_=wst[:, s, :256].rearrange("p (g d) -> p g d", g=4)),
                            reads=[("wstage", s)], writes=[rd])
            ht = A("at_ht", [128, 8, 512], F32)
            u = A("at_u", [128, 8, 512], BF16)
            rp = A("at_rp", [128, 4, 512], F32)
            t1 = A("at_t1", [128, 2, 512], F32)
            t2 = A("at_t2", [128, 2, 512], F32)
            cnt = 0
            for (t0, n, col) in tiles:
                s0, s1 = (0, NCTX) if col == 1 else (NCTX, LT)
                load_mod_tile(kb, ht, u, "at", h_in, h_in_res, li, 1, t0, n, col, s0, s1, 0)
                if col == 0:
                    P.dma(rp[:, 2:4, :n], rope_d[2:4, :, t0 - NCTX:t0 - NCTX + n].rearrange("a p t -> p a t"),
                          writes=["at_rp"])
                for g in range(4):
                    s = cnt % 2
                    cnt += 1
                    pk, pks = kb.pb[s], kb.pb[2 + s]
                    for kc in range(8):
                        P.pe(lambda e, kc=kc, g=g, pk=pk, n=n: e.matmul(pk[:, :n], lhsT=wkd[:, kc, g, :], rhs=u[:, kc, :n],
                                                                        start=(kc == 0), stop=(kc == 7)),
                             reads=["at_wkd", ("at_u", kc)], writes=[("pb", s)])
                    if col == 0:
                        for kc in range(8):
                            P.pe(lambda e, kc=kc, g=g, pks=pks, n=n: e.matmul(pks[:, :n], lhsT=wksd[:, kc, g, :],
                                                                              rhs=u[:, kc, :n], start=(kc == 0),
                                                                              stop=(kc == 7)),
                                 reads=["at_wksd", ("at_u", kc)], writes=[("pb", 2 + s)])
                        P.dve(lambda e, pk=pk, s=s, n=n: e.tensor_tensor(out=t1[:, s, :n], in0=pk[:, :n], in1=rp[:, 2, :n],
                                                                         op=ALU.mult),
                              reads=[("pb", s), "at_rp"], writes=[("at_t1", s)])
                        P.dve(lambda e, pks=pks, s=s, n=n: e.tensor_tensor(out=t2[:, s, :n], in0=pks[:, :n],
                                                                           in1=rp[:, 3, :n], op=ALU.mult),
                              reads=[("pb", 2 + s), "at_rp"], writes=[("at_t2", s)])
                        P.pool(lambda e, g=g, s=s, t0=t0, n=n: e.tensor_tensor(out=KT[:, g, t0:t0 + n], in0=t1[:, s, :n],
                                                                              in1=t2[:, s, :n], op=ALU.add),
                               reads=[("at_t1", s), ("at_t2", s)], writes=[("at_KT", t0 // 512 if col == 0 else "c")])
                    else:
                        P.act(lambda e, pk=pk, g=g, t0=t0, n=n: e.activation(out=KT[:, g, t0:t0 + n], in_=pk[:, :n],
                                                                             func=AF.Identity),
                              reads=[("pb", s)], writes=[("at_KT", "c")])
                for bi in range(n // 128):
                    pv = kb.pb[4 + (bi % 2)]
                    for kc in range(8):
                        P.pe(lambda e, kc=kc, bi=bi, pv=pv: e.matmul(pv[:, :256], lhsT=u[:, kc, bi * 128:(bi + 1) * 128],
                                                                     rhs=wv[:, kc, :], start=(kc == 0), stop=(kc == 7)),
                             reads=["at_wv", ("at_u", kc)], writes=[("pb", 4 + (bi % 2))])
                    blk = t0 // 128 + bi
                    P.act(lambda e, pv=pv, blk=blk: e.activation(out=Vt[:, blk, :], in_=pv[:, :256], func=AF.Identity),
                          reads=[("pb", 4 + (bi % 2))], writes=[("at_Vt", blk)])
      phase1()
      P.barrier()
      tiles = [(0, 256, 1)] + [(NCTX + 256 * i, 256, 0) for i in range(SEQ // 256)] if len(tiles) == 9 else tiles
      def phase2(tiles=tiles):
          with contextlib.ExitStack() as es:
            A = lambda name, shape, dt: es.enter_context(nc.sbuf_tensor(UN(name), shape, dt))
            NQ = 256
            wq = A("at_wq", [128, 8, 1024], BF16)
            wqs = A("at_wqs", [128, 8, 1024], BF16)
            wo = A("at_wo", [128, 8, 1024], BF16)
            wst = A("at_wst2", [128, 2, 1024], F32)
            load_weight_bf16(kb, wst, wq, "at_wq", wqkv_d[0], 8, 1024, 0)
            load_weight_bf16(kb, wst, wqs, "at_wqs", wsw_d, 8, 1024, 0)
            load_weight_bf16(kb, wst, wo, "at_wo", wo_d[0], 8, 1024, 0)
            ht = A("at_ht2", [128, 8, NQ], F32)
            u = A("at_u2", [128, 8, NQ], BF16)
            rp = A("at_rp2", [128, 4, NQ], F32)
            t1 = A("at_t12", [128, 2, NQ], F32)
            t2 = A("at_t22", [128, 2, NQ], F32)
            QT = A("at_QT", [128, 8, NQ], BF16)
            Sm = A("at_Sm", [128, 2, 384], F32)
            pbuf = A("at_p", [128, 2, 640], BF16)
            PT = A("at_PT", [128, 2, 5, 128], BF16)
            Osb = A("at_O", [128, 1024], BF16)
            OT = A("at_OT", [128, 8, NQ], BF16)
            st = A("at_st", [128, 2, 8], F32)
            lb = LNBufs(nc, es, NQ, "a")
            hcnt = 0
            for (t0, n, col) in tiles:
                s0, s1 = (0, NCTX) if col == 1 else (NCTX, LT)
                load_mod_tile(kb, ht, u, "at", h_in, h_in_res, li, 1, t0, n, col, s0, s1, 0)
                if col == 0:
                    P.dma(rp[:, 0:2, :n], rope_d[0:2, :, t0 - NCTX:t0 - NCTX + n].rearrange("a p t -> p a t"),
                          writes=["at_rp"])
                for c in range(8):
                    s = c % 2
                    pq, pqs = kb.pb[4 + s], kb.pb[6 + s]
                    for kc in range(8):
                        P.pe(lambda e, kc=kc, c=c, pq=pq, n=n: e.matmul(pq[:, :n], lhsT=wq[:, kc, c * 128:(c + 1) * 128],
                                                                        rhs=u[:, kc, :n], start=(kc == 0), stop=(kc == 7)),
                             reads=["at_wq", ("at_u", kc)], writes=[("pb", 4 + s)])
                    if col == 0:
                        for kc in range(8):
                            P.pe(lambda e, kc=kc, c=c, pqs=pqs, n=n: e.matmul(pqs[:, :n],
                                                                              lhsT=wqs[:, kc, c * 128:(c + 1) * 128],
                                                                              rhs=u[:, kc, :n], start=(kc == 0),
                                                                              stop=(kc == 7)),
                                 reads=["at_wqs", ("at_u", kc)], writes=[("pb", 6 + s)])
                        P.dve(lambda e, pq=pq, s=s, n=n: e.tensor_tensor(out=t1[:, s, :n], in0=pq[:, :n], in1=rp[:, 0, :n],
                                                                         op=ALU.mult),
                              reads=[("pb", 4 + s), "at_rp"], writes=[("at_t1", s)])
                        P.dve(lambda e, pqs=pqs, s=s, n=n: e.tensor_tensor(out=t2[:, s, :n], in0=pqs[:, :n],
                                                                           in1=rp[:, 1, :n], op=ALU.mult),
                              reads=[("pb", 6 + s), "at_rp"], writes=[("at_t2", s)])
                        P.pool(lambda e, c=c, s=s, n=n: e.tensor_tensor(out=QT[:, c, :n], in0=t1[:, s, :n],
                                                                       in1=t2[:, s, :n], op=ALU.add),
                               reads=[("at_t1", s), ("at_t2", s)], writes=[("at_QT", c)])
                    else:
                        P.act(lambda e, pq=pq, c=c, n=n: e.activation(out=QT[:, c, :n], in_=pq[:, :n], func=AF.Identity,
                                                                      scale=0.125),
                              reads=[("pb", 4 + s)], writes=[("at_QT", c)])
                for qb in range(n // 128):
                    if col == 0:
                        L = (t0 - NCTX) // 128 + qb
                        b0, b1 = max(L - 1, 0), min(L + 2, SEQ // 128)
                        kw0, kw1 = NCTX + 128 * b0, NCTX + 128 * b1
                        nw = kw1 - kw0
                        moff = 128 * (b0 - (L - 1))
                        kres = sorted(set(("at_KT", (kk - NCTX) // 512) for kk in (kw0, kw1 - 1))) + [("at_KT", "c")]
                        vblocks = list(range(kw0 // 128, kw1 // 128)) + [0, 1]
                    else:
                        nw = 0
                        kres = [("at_KT", "c")]
                        vblocks = [0, 1]
                    nk = nw + 256
                    for h in range(16):
                        c, base, g = h // 2, (h % 2) * 64, h // 4
                        s = hcnt % 2
                        hcnt += 1
                        pS, pS2, pTO = kb.pb[2 * s], kb.pb[2 * s + 1], kb.pb[4 + s]
                        q_ap = QT[base:base + 64, c, qb * 128:(qb + 1) * 128]
                        if nw:
                            P.pe(lambda e, pS=pS, q_ap=q_ap, base=base, g=g, kw0=kw0, kw1=kw1, nw=nw: e.matmul(
                                pS[:, :nw], lhsT=q_ap, rhs=KT[base:base + 64, g, kw0:kw1], start=True, stop=True),
                                reads=[("at_QT", c)] + kres, writes=[("pb", 2 * s)])
                        P.pe(lambda e, pS2=pS2, q_ap=q_ap, base=base, g=g: e.matmul(
                            pS2[:, :256], lhsT=q_ap, rhs=KT[base:base + 64, g, 0:256], start=True, stop=True),
                            reads=[("at_QT", c), ("at_KT", "c")], writes=[("pb", 2 * s + 1)])
                        m1, m2, mx, negm, rs1, rs2, esk, den = (st[:, s, i:i + 1] for i in range(8))
                        sres = ("at_st", s)
                        if nw:
                            P.dve(lambda e, pS=pS, s=s, nw=nw, moff=moff: e.tensor_tensor(
                                out=Sm[:, s, :nw], in0=pS[:, :nw], in1=maskb[:, moff:moff + nw], op=ALU.add),
                                reads=[("pb", 2 * s), "at_mask"], writes=[("at_Sm", s)])
                            P.dve(lambda e, s=s, nw=nw, m1=m1: e.reduce_max(out=m1, in_=Sm[:, s, :nw], axis=AX.X),
                                  reads=[("at_Sm", s)], writes=[sres])
                        P.dve(lambda e, pS2=pS2, m2=m2: e.reduce_max(out=m2, in_=pS2[:, :256], axis=AX.X),
                              reads=[("pb", 2 * s + 1)], writes=[sres])
                        sink_ap = kb.v("at_sink", h)
                        if nw:
                            P.dve(lambda e, m1=m1, m2=m2, mx=mx, sink_ap=sink_ap: e.tensor_scalar(
                                out=mx, in0=m1, scalar1=m2, scalar2=sink_ap, op0=ALU.max, op1=ALU.max),
                                reads=[sres, "vecs"], writes=[sres])
                        else:
                            P.dve(lambda e, m2=m2, mx=mx, sink_ap=sink_ap: e.tensor_scalar(
                                out=mx, in0=m2, scalar1=sink_ap, scalar2=None, op0=ALU.max),
                                reads=[sres, "vecs"], writes=[sres])
                        P.dve(lambda e, mx=mx, negm=negm: e.tensor_scalar(out=negm, in0=mx, scalar1=-1.0, scalar2=None,
                                                                          op0=ALU.mult),
                              reads=[sres], writes=[sres])
                        if nw:
                            P.act(lambda e, s=s, nw=nw, negm=negm, rs1=rs1: e.activation(
                                out=pbuf[:, s, :nw], in_=Sm[:, s, :nw], func=AF.Exp, bias=negm, accum_out=rs1),
                                reads=[("at_Sm", s), sres], writes=[("at_p", s), ("at_st2", s)])
                        P.act(lambda e, s=s, nw=nw, pS2=pS2, negm=negm, rs2=rs2: e.activation(
                            out=pbuf[:, s, nw:nw + 256], in_=pS2[:, :256], func=AF.Exp, bias=negm, accum_out=rs2),
                            reads=[("pb", 2 * s + 1), sres], writes=[("at_p", s), ("at_st2", s)])
                        P.act(lambda e, negm=negm, esk=esk, sink_ap=sink_ap: e.activation(
                            out=esk, in_=negm, func=AF.Exp, bias=sink_ap),
                            reads=[sres, "vecs"], writes=[("at_st2", s)])
                        if nw:
                            P.dve(lambda e, rs1=rs1, rs2=rs2, esk=esk, den=den: e.tensor_scalar(
                                out=den, in0=rs1, scalar1=rs2, scalar2=esk, op0=ALU.add, op1=ALU.add),
                                reads=[("at_st2", s)], writes=[("at_st3", s)])
                        else:
                            P.dve(lambda e, rs2=rs2, esk=esk, den=den: e.tensor_scalar(
                                out=den, in0=rs2, scalar1=esk, scalar2=None, op0=ALU.add),
                                reads=[("at_st2", s)], writes=[("at_st3", s)])
                        P.dve(lambda e, den=den: e.reciprocal(out=den, in_=den), reads=[("at_st3", s)],
                              writes=[("at_st3", s)])
                        nkb = nk // 128
                        ptp = pTO[:].bitcast(BF16)
                        for kk in range(nkb):
                            P.pe(lambda e, kk=kk, s=s, ptp=ptp: e.transpose(
                                out=ptp[:, kk * 128:(kk + 1) * 128], in_=pbuf[:, s, kk * 128:(kk + 1) * 128],
                                identity=kb.identb[:]),
                                reads=[("at_p", s), "identb"], writes=[("pb", 4 + s)])
                        P.dve(lambda e, s=s, nkb=nkb, ptp=ptp: e.tensor_copy(
                            out=PT[:, s, :nkb, :].rearrange("p a b -> p (a b)"), in_=ptp[:, :nkb * 128]),
                            reads=[("pb", 4 + s)], writes=[("at_PT", s)])
                        po = pTO[:, 384:448]
                        for kk in range(nkb):
                            vb = vblocks[kk]
                            P.pe(lambda e, kk=kk, s=s, po=po, vb=vb, g=g, nkb=nkb: e.matmul(
                                po, lhsT=PT[:, s, kk, :], rhs=Vt[:, vb, g * 64:(g + 1) * 64], start=(kk == 0),
                                stop=(kk == nkb - 1)),
                                reads=[("at_PT", s), ("at_Vt", vb)], writes=[("pb", 4 + s)])
                        P.act(lambda e, po=po, h=h, den=den: e.activation(out=Osb[:, h * 64:(h + 1) * 64], in_=po,
                                                                          func=AF.Identity, scale=den),
                              reads=[("pb", 4 + s), ("at_st3", s)], writes=["at_O"])
                    pot = kb.pb[6][:].bitcast(BF16)
                    for c in range(8):
                        P.pe(lambda e, c=c, pot=pot: e.transpose(out=pot[:, c * 128:(c + 1) * 128],
                                                                 in_=Osb[:, c * 128:(c + 1) * 128], identity=kb.identb[:]),
                             reads=["at_O", "identb"], writes=[("pb", 6)])
                    P.dve(lambda e, qb=qb, pot=pot: e.tensor_copy(
                        out=OT[:, :, qb * 128:(qb + 1) * 128], in_=pot[:].rearrange("p (c t) -> p c t", c=8)),
                        reads=[("pb", 6)], writes=["at_OT"])
                finals.extend(proj_ln(kb, lb, OT, lambda kc: "at_OT", 8, wo, "at_wo",
                                  lambda o, n=n: ht[:, o, :n], "at_ht", li, n, col,
                                  lambda t0=t0, n=n: hT_tile(h_out, t0, n), hres(h_out_res, t0, n)))
      phase2()
    kb.P.barrier()
    return finals

import math

NCH = LT // 128
E2 = 2048
DH = 512
ORDER = {0: list(range(NCH)), 1: [1, 0] + list(range(NCH - 1, 1, -1))}
PT256 = [(0, 256, 1)] + [(NCTX + 256 * i, 256, 0) for i in range(SEQ // 256)]


def host_bd(w_qkv_j):
    out = np.zeros((3, 128, 16, 128), np.float32)
    w = np.asarray(w_qkv_j, np.float32).reshape(3, 16, 32, 4, 4)
    for m in range(32):
        out[:, 4 * m:4 * m + 4, :, 4 * m:4 * m + 4] = np.transpose(w[:, :, m], (0, 2, 1, 3))
    return out


def host_wif(w_if_j):
    w = np.asarray(w_if_j, np.float32).reshape(2, 48, 128, 8)
    return np.ascontiguousarray(np.transpose(w, (2, 1, 0, 3)).reshape(128, 48, 16))


def mlstm_scratch(kb):
    if hasattr(kb, "ml_scr"):
        return kb.ml_scr
    s = {}
    s["qTb"] = kb.dscr("ml_qTb", [NCH, 128, 16, 128], BF16)
    s["kTb"] = kb.dscr("ml_kTb", [NCH, 128, 16, 128], BF16)
    s["xcTb"] = kb.dscr("ml_xcTb", [NCH, 128, 16, 128], BF16)
    s["ktok"] = kb.dscr("ml_ktok", [LT, E2], BF16)
    s["vtok"] = kb.dscr("ml_vtok", [LT, E2], BF16)
    s["sztok"] = kb.dscr("ml_sztok", [LT, E2], F32)
    s["gtok"] = kb.dscr("ml_gtok", [LT, 16], F32)
    s["hftok"] = kb.dscr("ml_hftok", [LT, E2], F32)
    kb.ml_scr = s
    return s


def mlstm_prep(kb, li, j, h_in, h_in_res, wup_d, bd_d, wif_d):
    nc, P = kb.nc, kb.P
    S = mlstm_scratch(kb)
    with contextlib.ExitStack() as es:
        A = lambda name, shape, dt: es.enter_context(nc.sbuf_tensor(UN(name), shape, dt))
        wst = A("mp_wst", [128, 2, 2048], F32)
        wup = A("mp_wup", [128, 8, 4096], BF16)
        load_weight_bf16(kb, wst, wup, "mp_wup", wup_d[j], 8, 4096)
        bd = A("mp_bd", [128, 3, 16, 128], BF16)
        for x in range(3):
            s = kb.uid % 2
            kb.uid += 1
            P.dma(wst[:, s, :], bd_d[j, x].rearrange("p e o -> p (e o)"), writes=[("wstage", s)])
            P.pool(lambda e, x=x, s=s: e.tensor_copy(out=bd[:, x].rearrange("p e o -> p (e o)"), in_=wst[:, s, :]),
                   reads=[("wstage", s)], writes=["mp_bd"])
        wif = A("mp_wif", [128, 48, 16], BF16)
        s = kb.uid % 2
        kb.uid += 1
        P.dma(wst[:, s, :768], wif_d[j].rearrange("p r g -> p (r g)"), writes=[("wstage", s)])
        P.pool(lambda e, s=s: e.tensor_copy(out=wif[:].rearrange("p r g -> p (r g)"), in_=wst[:, s, :768]),
               reads=[("wstage", s)], writes=["mp_wif"])
        n = 256
        ht = A("mp_ht", [128, 8, n + 2], F32)
        u = A("mp_u", [128, 8, n + 2], BF16)
        xms = A("mp_xm", [128, 2, n + 2], F32)
        cv = A("mp_cv", [128, 2, n], F32)
        xmb = A("mp_xmb", [128, 16, n], BF16)
        xc = A("mp_xc", [128, 2, 16, 128], BF16)
        qTs = A("mp_qT", [128, 2, 16, 128], BF16)
        kTs = A("mp_kT", [128, 2, 16, 128], BF16)
        vTs = A("mp_vT", [128, 2, n], BF16)
        kst = A("mp_kst", [128, 2, E2], BF16)
        szs = A("mp_szs", [128, 2, E2], F32)
        gst = A("mp_gst", [128, 2, 16], F32)
        cnt = 0
        for (t0, n_, col) in PT256:
            s0, s1 = (0, NCTX) if col == 1 else (NCTX, LT)
            load_mod_tile(kb, ht, u, "mp", h_in, h_in_res, li, 1, t0, n, col, s0, s1, 1)
            c0 = t0 // 128
            for e_ in range(16):
                s = e_ % 2
                px = kb.pb[s]
                for kc in range(8):
                    P.pe(lambda e, kc=kc, e_=e_, px=px: e.matmul(px[:, :n + 2], lhsT=wup[:, kc, e_ * 128:(e_ + 1) * 128],
                                                                  rhs=u[:, kc, :], start=(kc == 0), stop=(kc == 7)),
                         reads=["mp_wup", ("mp_u", kc)], writes=[("pb", s)])
                P.act(lambda e, px=px, s=s: e.activation(out=xms[:, s, :], in_=px[:, :n + 2], func=AF.Identity),
                      reads=[("pb", s)], writes=[("mp_xm", s)])
                P.pool(lambda e, s=s, e_=e_: e.tensor_copy(out=xmb[:, e_, :], in_=xms[:, s, 1:n + 1]),
                       reads=[("mp_xm", s)], writes=[("mp_xmb", e_)])
                w0, w1, w2 = (kb.v("ml_conv_w", (j * 3 + k_) * 16 + e_) for k_ in range(3))
                bcv = kb.v("ml_conv_b", j * 16 + e_)
                P.dve(lambda e, s=s, w1=w1, bcv=bcv: e.tensor_scalar(out=cv[:, s, :], in0=xms[:, s, 1:n + 1], scalar1=w1,
                                                                     scalar2=bcv, op0=ALU.mult, op1=ALU.add),
                      reads=[("mp_xm", s), "vecs"], writes=[("mp_cv", s)])
                P.dve(lambda e, s=s, w0=w0: e.scalar_tensor_tensor(out=cv[:, s, :], in0=xms[:, s, 0:n], scalar=w0,
                                                                   in1=cv[:, s, :], op0=ALU.mult, op1=ALU.add),
                      reads=[("mp_xm", s), ("mp_cv", s)], writes=[("mp_cv", s)])
                P.dve(lambda e, s=s, w2=w2: e.scalar_tensor_tensor(out=cv[:, s, :], in0=xms[:, s, 2:n + 2], scalar=w2,
                                                                   in1=cv[:, s, :], op0=ALU.mult, op1=ALU.add),
                      reads=[("mp_xm", s), ("mp_cv", s)], writes=[("mp_cv", s)])
                P.act(lambda e, s=s, e_=e_: e.activation(out=xc[:, :, e_, :],
                                                         in_=cv[:, s, :].rearrange("p (c t) -> p c t", c=2),
                                                         func=AF.Silu),
                      reads=[("mp_cv", s)], writes=[("mp_xc", e_)])
                for x, (dst, dres) in enumerate(((qTs, "mp_qT"), (kTs, "mp_kT"), (vTs, "mp_vT"))):
                    pq = kb.pb[2 + (cnt % 2)]
                    pqi = 2 + (cnt % 2)
                    cnt += 1
                    if x < 2:
                        P.pe(lambda e, x=x, e_=e_, pq=pq: e.matmul(pq[:, :n], lhsT=bd[:, x, e_, :], rhs=xc[:, :, e_, :],
                                                                   start=True, stop=True),
                             reads=["mp_bd", ("mp_xc", e_)], writes=[("pb", pqi)])
                        P.act(lambda e, dst=dst, e_=e_, pq=pq: e.activation(
                            out=dst[:, :, e_, :], in_=pq[:, :n].rearrange("p (c t) -> p c t", c=2), func=AF.Identity),
                            reads=[("pb", pqi)], writes=[(dres, e_)])
                        src_fn = lambda cb, dst=dst, e_=e_: dst[:, cb, e_, :]
                        src_res = (dres, e_)
                    else:
                        P.pe(lambda e, e_=e_, pq=pq: e.matmul(pq[:, :n], lhsT=bd[:, 2, e_, :], rhs=xmb[:, e_, :],
                                                              start=True, stop=True),
                             reads=["mp_bd", ("mp_xmb", e_)], writes=[("pb", pqi)])
                        vs = e_ % 2
                        P.act(lambda e, vs=vs, pq=pq: e.activation(out=vTs[:, vs, :], in_=pq[:, :n], func=AF.Identity),
                              reads=[("pb", pqi)], writes=[("mp_vT", vs)])
                        src_fn = lambda cb, vs=vs: vTs[:, vs, cb * 128:(cb + 1) * 128]
                        src_res = ("mp_vT", vs)
                    for cb in range(2):
                        P.pe(lambda e, cb=cb, x=x, e_=e_, src_fn=src_fn: e.matmul(
                            kb.pb[6 + cb][:, :16], lhsT=src_fn(cb), rhs=wif[:, x * 16 + e_, :],
                            start=(x == 0 and e_ == 0), stop=(x == 2 and e_ == 15)),
                            reads=[src_res, "mp_wif"], writes=[("pb", 6 + cb)])
            for cb in range(2):
                P.dve(lambda e, cb=cb: e.tensor_tensor(out=gst[:, cb, :], in0=kb.pb[6 + cb][:, :16],
                                                       in1=kb.v("ml_b_if", j * 16, 16), op=ALU.add),
                      reads=[("pb", 6 + cb), "vecs"], writes=[("mp_gst", cb)])
                P.dma(S["gtok"][t0 + cb * 128:t0 + (cb + 1) * 128, :], gst[:, cb, :], reads=[("mp_gst", cb)],
                      writes=[("gtok", c0 + cb)], q="act")
            for cb in range(2):
                for (x, dname) in ((1, "ktok"), (2, "vtok")):
                    ks = x - 1
                    for hh in range(4):
                        pk = kb.pb[4 + (hh % 2)]
                        for jj in range(4):
                            e_ = 4 * hh + jj
                            lhs = xc[:, cb, e_, :] if x == 1 else xmb[:, e_, cb * 128:(cb + 1) * 128]
                            lres = ("mp_xc", e_) if x == 1 else ("mp_xmb", e_)
                            P.pe(lambda e, pk=pk, jj=jj, lhs=lhs, x=x, e_=e_: e.matmul(
                                pk[:, jj * 128:(jj + 1) * 128], lhsT=lhs, rhs=bd[:, x, e_, :], start=True, stop=True),
                                reads=[lres, "mp_bd"], writes=[("pb", 4 + (hh % 2))])
                        P.act(lambda e, pk=pk, ks=ks, hh=hh: e.activation(out=kst[:, ks, hh * 512:(hh + 1) * 512],
                                                                          in_=pk[:, :512], func=AF.Identity),
                              reads=[("pb", 4 + (hh % 2))], writes=[("mp_kst", ks)])
                    P.dma(S[dname][t0 + cb * 128:t0 + (cb + 1) * 128, :], kst[:, ks, :], reads=[("mp_kst", ks)],
                          writes=[(dname, c0 + cb)], q="act")
                for hh in range(4):
                    pz = kb.pb[hh % 2]
                    for kc in range(8):
                        P.pe(lambda e, pz=pz, kc=kc, cb=cb, hh=hh: e.matmul(
                            pz[:, :512], lhsT=u[:, kc, 1 + cb * 128:1 + (cb + 1) * 128],
                            rhs=wup[:, kc, E2 + hh * 512:E2 + (hh + 1) * 512], start=(kc == 0), stop=(kc == 7)),
                            reads=[("mp_u", kc), "mp_wup"], writes=[("pb", hh % 2)])
                    P.act(lambda e, pz=pz, cb=cb, hh=hh: e.activation(out=szs[:, cb, hh * 512:(hh + 1) * 512],
                                                                      in_=pz[:, :512], func=AF.Sigmoid),
                          reads=[("pb", hh % 2)], writes=[("mp_szs", cb)])
                P.dma(S["sztok"][t0 + cb * 128:t0 + (cb + 1) * 128, :], szs[:, cb, :], reads=[("mp_szs", cb)],
                      writes=[("sztok", c0 + cb)], q="act")
            for (src, dname, rname) in ((qTs, "qTb", "mp_qT"), (kTs, "kTb", "mp_kT"), (xc, "xcTb", "mp_xc")):
                P.dma(S[dname][c0:c0 + 2].rearrange("c p e t -> p c e t"), src[:],
                      reads=[(rname, e_) for e_ in range(16)], writes=[(dname, c0), (dname, c0 + 1)], q="act")
    P.barrier()


def mlstm_gates(kb, es, tag):
    nc, P = kb.nc, kb.P
    S = mlstm_scratch(kb)
    A = lambda name, shape, dt: es.enter_context(nc.sbuf_tensor(UN(name), shape, dt))
    tri = A("mg_tri", [128, 2, 128], F32)
    pers = {}
    for d_ in range(2):
        pers[d_] = {nm: A("mg_%s%d" % (nm, d_), [128, NCH, 4], F32) for nm in ("w", "wa", "e", "am")}
    es_t = contextlib.ExitStack()
    At = lambda name, shape, dt: es_t.enter_context(nc.sbuf_tensor(UN(name), shape, dt))
    G = At("mg_G", [128, NCH, 16], F32)
    P.dma(G[:], S["gtok"].rearrange("(c p) g -> p c g", p=128), reads=[("gtok", c) for c in range(NCH)], writes=["mg_G"])
    ones = At("mg_ones", [128, 128], F32)
    P.pool(lambda e: e.memset(ones[:], 1.0), writes=["mg_ones"])
    P.pool(lambda e: e.memset(tri[:], 1.0), writes=["mg_tri"])
    P.pool(lambda e: e.affine_select(out=tri[:, 0, :], in_=tri[:, 0, :], pattern=[[1, 128]], compare_op=ALU.is_ge,
                                     fill=0.0, base=0, channel_multiplier=-1), reads=["mg_tri"], writes=["mg_tri"])
    P.pool(lambda e: e.affine_select(out=tri[:, 1, :], in_=tri[:, 1, :], pattern=[[-1, 128]], compare_op=ALU.is_ge,
                                     fill=0.0, base=0, channel_multiplier=1), reads=["mg_tri"], writes=["mg_tri"])
    NC4 = NCH * 4
    out = {"tri": tri}
    tmp = At("mg_tmp", [128, 6, NCH, 4], F32)
    X = At("mg_X", [128, NCH, 4], F32)
    dg = At("mg_dg", [68, 68], F32)
    xm = At("mg_xm", [68, 1], F32)
    for d in range(2):
        IG = G[:, :, d * 8:d * 8 + 4]
        FG = G[:, :, d * 8 + 4:d * 8 + 8]
        ab, ex, lf, bsb, gsb, x = (tmp[:, i] for i in range(6))
        R = lambda i: ("mg_tmp", i)
        P.act(lambda e, FG=FG, ab=ab: e.activation(out=ab, in_=FG, func=AF.Abs),
              reads=["mg_G"], writes=[R(0)])
        P.act(lambda e, ab=ab, ex=ex: e.activation(out=ex, in_=ab, func=AF.Exp, scale=-1.0), reads=[R(0)], writes=[R(1)])
        P.act(lambda e, ex=ex: e.activation(out=ex, in_=ex, func=AF.Ln, bias=kb.epsc[:, 3:4]), reads=[R(1), "epsc"],
              writes=[R(1)])
        P.dve(lambda e, FG=FG, ab=ab: e.tensor_single_scalar(out=ab, in_=FG, scalar=0.0, op=ALU.min),
              reads=["mg_G", R(0)], writes=[R(0)])
        P.dve(lambda e, ab=ab, ex=ex, lf=lf: e.tensor_tensor(out=lf, in0=ab, in1=ex, op=ALU.subtract),
              reads=[R(0), R(1)], writes=[R(2)])
        lf2 = lf.rearrange("p c h -> p (c h)")
        P.pe(lambda e, d=d, lf2=lf2: e.matmul(kb.pb[0][:, :NC4], lhsT=tri[:, d, :], rhs=lf2, start=True, stop=True),
             reads=["mg_tri", R(2)], writes=[("pb", 0)])
        P.pe(lambda e, lf2=lf2: e.matmul(kb.pb[1][:, :NC4], lhsT=ones[:], rhs=lf2, start=True, stop=True),
             reads=["mg_ones", R(2)], writes=[("pb", 1)])
        P.act(lambda e, bsb=bsb: e.activation(out=bsb.rearrange("p c h -> p (c h)"), in_=kb.pb[0][:, :NC4],
                                              func=AF.Identity), reads=[("pb", 0)], writes=[R(3)])
        P.act(lambda e, gsb=gsb: e.activation(out=gsb.rearrange("p c h -> p (c h)"), in_=kb.pb[1][:, :NC4],
                                              func=AF.Identity), reads=[("pb", 1)], writes=[R(4)])
        P.dve(lambda e, IG=IG, bsb=bsb, x=x: e.tensor_tensor(out=x, in0=IG, in1=bsb, op=ALU.subtract),
              reads=["mg_G", R(3)], writes=[R(5)])
        x2 = x.rearrange("p c h -> p (c h)")
        X2 = X[:].rearrange("p c h -> p (c h)")
        for half in range(2):
            pt = kb.pb[2 + half]
            P.pe(lambda e, half=half, pt=pt, x2=x2: e.transpose(out=pt[0:68, 0:128], in_=x2[:, half * 68:(half + 1) * 68],
                                                                identity=kb.ident[:]),
                 reads=[R(5), "ident"], writes=[("pb", 2 + half)])
            P.dve(lambda e, pt=pt: e.reduce_max(out=xm[:], in_=pt[0:68, 0:128], axis=AX.X), reads=[("pb", 2 + half)],
                  writes=["mg_xm"])
            P.dve(lambda e: e.tensor_scalar(out=dg[:], in0=kb.ident[0:68, 0:68], scalar1=xm[:, 0:1], scalar2=None,
                                            op0=ALU.mult), reads=["mg_xm", "ident"], writes=["mg_dg"])
            pr = kb.pb[4 + half]
            P.pe(lambda e, pr=pr: e.matmul(pr[:, 0:68], lhsT=ones[0:68, :], rhs=dg[:], start=True, stop=True),
                 reads=["mg_ones", "mg_dg"], writes=[("pb", 4 + half)])
            P.act(lambda e, half=half, pr=pr, X2=X2: e.activation(out=X2[:, half * 68:(half + 1) * 68], in_=pr[:, 0:68],
                                                                  func=AF.Identity),
                  reads=[("pb", 4 + half)], writes=["mg_X"])
        M = At("mg_M%d" % d, [128, NCH, 4], F32)
        mseq = At("mg_ms%d" % d, [128, NCH + 1, 4], F32)
        am = pers[d]["am"]
        P.dve(lambda e, mseq=mseq: e.memset(mseq[:, 0, :], 0.0), writes=["mg_mseq"])
        for k, c in enumerate(ORDER[d]):
            P.dve(lambda e, k=k, c=c, M=M, mseq=mseq: e.tensor_tensor(out=M[:, c, :], in0=mseq[:, k, :], in1=X[:, c, :],
                                                                      op=ALU.max),
                  reads=["mg_mseq", "mg_X"], writes=["mg_M"])
            P.dve(lambda e, k=k, c=c, M=M, mseq=mseq, gsb=gsb: e.tensor_tensor(out=mseq[:, k + 1, :], in0=M[:, c, :],
                                                                               in1=gsb[:, c, :], op=ALU.add),
                  reads=["mg_M", R(4)], writes=["mg_mseq"])
            P.dve(lambda e, k=k, c=c, M=M, mseq=mseq, am=am: e.tensor_tensor(out=am[:, c, :], in0=mseq[:, k, :],
                                                                             in1=M[:, c, :], op=ALU.subtract),
                  reads=["mg_M", "mg_mseq"], writes=["mg_am"])
        w, wa, ee = pers[d]["w"], pers[d]["wa"], pers[d]["e"]
        P.dve(lambda e, x=x, M=M: e.tensor_tensor(out=x, in0=x, in1=M[:], op=ALU.subtract), reads=[R(5), "mg_M"],
              writes=[R(5)])
        P.act(lambda e, x=x, w=w: e.activation(out=w[:], in_=x, func=AF.Exp, bias=kb.epsc[:, 2:3]),
              reads=[R(5), "epsc"], writes=["mg_w"])
        P.dve(lambda e, bsb=bsb, M=M: e.tensor_tensor(out=bsb, in0=bsb, in1=M[:], op=ALU.add), reads=[R(3), "mg_M"],
              writes=[R(3)])
        P.act(lambda e, bsb=bsb, ee=ee: e.activation(out=ee[:], in_=bsb, func=AF.Exp, scale=-1.0), reads=[R(3)],
              writes=["mg_e"])
        P.act(lambda e, am=am: e.activation(out=am[:], in_=am[:], func=AF.Exp), reads=["mg_am"], writes=["mg_am"])
        P.dve(lambda e, w=w, wa=wa: e.tensor_copy(out=wa[:], in_=w[:]), reads=["mg_w"], writes=["mg_wa"])
        if d == 0:
            P.dve(lambda e, w=w, wa=wa, am=am: e.tensor_tensor(out=wa[:, 0:NCH - 1, :], in0=w[:, 0:NCH - 1, :],
                                                               in1=am[:, 1:NCH, :], op=ALU.mult),
                  reads=["mg_w", "mg_am", "mg_wa"], writes=["mg_wa"])
        else:
            P.dve(lambda e, w=w, wa=wa, am=am: e.tensor_tensor(out=wa[:, 3:NCH, :], in0=w[:, 3:NCH, :],
                                                               in1=am[:, 2:NCH - 1, :], op=ALU.mult),
                  reads=["mg_w", "mg_am", "mg_wa"], writes=["mg_wa"])
            P.dve(lambda e, w=w, wa=wa, am=am: e.tensor_tensor(out=wa[:, 1, :], in0=w[:, 1, :], in1=am[:, 0, :],
                                                               op=ALU.mult),
                  reads=["mg_w", "mg_am", "mg_wa"], writes=["mg_wa"])
            P.dve(lambda e, w=w, wa=wa, am=am: e.tensor_tensor(out=wa[:, 0, :], in0=w[:, 0, :], in1=am[:, NCH - 1, :],
                                                               op=ALU.mult),
                  reads=["mg_w", "mg_am", "mg_wa"], writes=["mg_wa"])
        out[d] = {"w": w, "wa": wa, "e": ee, "a": am}
    P.barrier()
    es_t.close()
    return out


def mlstm_scan(kb, li, j, last, gates, h_in, h_in_res, h_out, h_out_res, wdown_d, ngbc_d):
    nc, P = kb.nc, kb.P
    S = mlstm_scratch(kb)
    finals = []
    tri = gates["tri"]
    for d in range(2):
        gd = gates[d]
        order = ORDER[d]

        def run_dir(d=d, gd=gd, order=order):
            with contextlib.ExitStack() as es:
                A = lambda name, shape, dt: es.enter_context(nc.sbuf_tensor(UN(name), shape, dt))
                T = "ms%d_" % d
                Cs = A(T + "Cs", [128, 4, 4, 512], F32)
                Cb = A(T + "Cb", [128, 4, 4, 512], BF16)
                ns = A(T + "ns", [128, 4, 4], F32)
                nb = A(T + "nb", [128, 4, 4], BF16)
                onesb = A(T + "onesb", [128, 2], BF16)
                qT = A(T + "qT", [128, 2, 16, 128], BF16)
                kT = A(T + "kT", [128, 2, 16, 128], BF16)
                ktk = A(T + "ktk", [128, E2], BF16)
                vtk = A(T + "vtk", [128, E2], BF16)
                AT = A(T + "AT", [128, 4, 128], BF16)
                KW = A(T + "KW", [128, 2, 512], BF16)
                hst = A(T + "hst", [128, E2], F32)
                rr = A(T + "rr", [128, 4], F32)
                P.pool(lambda e: e.memset(Cs[:], 0.0), writes=[("Cs", h) for h in range(4)])
                P.pool(lambda e: e.memset(Cb[:], 0.0), writes=[("Cb", h) for h in range(4)])
                P.pool(lambda e: e.memset(ns[:], 0.0), writes=["ns"])
                P.pool(lambda e: e.memset(nb[:], 0.0), writes=["nb"])
                P.pool(lambda e: e.memset(onesb[:], 1.0), writes=["onesb"])
                if d == 1:
                    wst = A(T + "wst", [128, 2, 512], F32)
                    wdn = A(T + "wdn", [128, 16, 1024], BF16)
                    load_weight_bf16(kb, wst, wdn, "wdn", wdown_d[j], 16, 1024)
                    ngbc = A(T + "ngbc", [128, E2], F32)
                    P.dma(ngbc[:], ngbc_d[j], writes=["ngbc"])
                    hf = A(T + "hf", [128, E2], F32)
                    sz = A(T + "sz", [128, E2], F32)
                    xcT = A(T + "xcT", [128, 16, 128], BF16)
                    hnb = A(T + "hnb", [128, E2], BF16)
                    xin = A(T + "xin", [128, 16, 128], BF16)
                    hrs = A(T + "hrs", [128, 8, 128], F32)
                    bst = A(T + "bst", [128, 4, 8], F32)
                    lb = LNBufs(nc, es, 128, "m")
                for k, c in enumerate(order):
                    is_last = (k == len(order) - 1)
                    cn = order[k + 1] if not is_last else None
                    col = 1 if c < 2 else 0
                    need_h = not (last and col == 1)
                    qs = k % 2
                    P.dma(qT[:, qs], S["qTb"][c], reads=[("qTb", c)], writes=[("qT", qs)])
                    P.dma(kT[:, qs], S["kTb"][c], reads=[("kTb", c)], writes=[("kT", qs)])
                    P.dma(ktk[:], S["ktok"][c * 128:(c + 1) * 128, :], reads=[("ktok", c)], writes=["ktk"])
                    P.dma(vtk[:], S["vtok"][c * 128:(c + 1) * 128, :], reads=[("vtok", c)], writes=["vtk"])
                    if need_h:
                        for hh in range(4):
                            for jj in range(4):
                                P.pe(lambda e, hh=hh, jj=jj, qs=qs: e.matmul(
                                    kb.pb[0][:, hh * 128:(hh + 1) * 128], lhsT=kT[:, qs, 4 * hh + jj, :],
                                    rhs=qT[:, qs, 4 * hh + jj, :], start=(jj == 0), stop=(jj == 3)),
                                    reads=[("kT", qs), ("qT", qs)], writes=[("pb", 0)])
                        for hh in range(4):
                            P.dve(lambda e, hh=hh, c=c: e.scalar_tensor_tensor(
                                out=AT[:, hh, :], in0=kb.pb[0][:, hh * 128:(hh + 1) * 128], scalar=gd["w"][:, c, hh:hh + 1],
                                in1=tri[:, d, :], op0=ALU.mult, op1=ALU.mult),
                                reads=[("pb", 0), "mg_w", "mg_tri"], writes=[("AT", hh)])
                    for hh in range(4):
                        if need_h:
                            pn = kb.pb[2 + (hh % 2)]
                            P.pe(lambda e, hh=hh, pn=pn: e.matmul(pn[:, :512], lhsT=AT[:, hh, :],
                                                                  rhs=vtk[:, hh * 512:(hh + 1) * 512], start=True,
                                                                  stop=False),
                                 reads=[("AT", hh), "vtk"], writes=[("pb", 2 + (hh % 2))])
                            for jj in range(4):
                                P.pe(lambda e, hh=hh, jj=jj, pn=pn, qs=qs: e.matmul(
                                    pn[:, :512], lhsT=qT[:, qs, 4 * hh + jj, :], rhs=Cb[:, hh, jj, :], start=False,
                                    stop=(jj == 3)),
                                    reads=[("qT", qs), ("Cb", hh)], writes=[("pb", 2 + (hh % 2))])
                            pd = kb.pb[1][:, hh:hh + 1]
                            P.pe(lambda e, hh=hh, pd=pd: e.matmul(pd, lhsT=AT[:, hh, :], rhs=onesb[:, 0:1], start=True,
                                                                  stop=False),
                                 reads=[("AT", hh), "onesb"], writes=[("pb", 1)])
                            for jj in range(4):
                                P.pe(lambda e, hh=hh, jj=jj, pd=pd, qs=qs: e.matmul(
                                    pd, lhsT=qT[:, qs, 4 * hh + jj, :], rhs=nb[:, hh, jj:jj + 1], start=False,
                                    stop=(jj == 3)),
                                    reads=[("qT", qs), "nb"], writes=[("pb", 1)])
                            P.act(lambda e, hh=hh, pd=pd: e.activation(out=rr[:, hh:hh + 1], in_=pd, func=AF.Abs),
                                  reads=[("pb", 1)], writes=[("rr", hh)])
                            P.dve(lambda e, hh=hh, c=c: e.tensor_tensor(out=rr[:, hh:hh + 1], in0=rr[:, hh:hh + 1],
                                                                        in1=gd["e"][:, c, hh:hh + 1], op=ALU.max),
                                  reads=[("rr", hh), "mg_e"], writes=[("rr", hh)])
                            P.dve(lambda e, hh=hh: e.reciprocal(out=rr[:, hh:hh + 1], in_=rr[:, hh:hh + 1]),
                                  reads=[("rr", hh)], writes=[("rr", hh)])
                            P.act(lambda e, hh=hh, pn=pn: e.activation(out=hst[:, hh * 512:(hh + 1) * 512], in_=pn[:, :512],
                                                                       func=AF.Identity, scale=rr[:, hh:hh + 1]),
                                  reads=[("pb", 2 + (hh % 2)), ("rr", hh)], writes=[("hst", hh)])
                        if not is_last:
                            ks = hh % 2
                            P.act(lambda e, hh=hh, ks=ks, c=c: e.activation(out=KW[:, ks, :],
                                                                            in_=ktk[:, hh * 512:(hh + 1) * 512],
                                                                            func=AF.Identity,
                                                                            scale=gd["wa"][:, c, hh:hh + 1]),
                                  reads=["ktk", "mg_wa"], writes=[("KW", ks)])
                            a_ap = gd["a"][:, cn, hh:hh + 1]
                            for jj in range(4):
                                P.pe(lambda e, hh=hh, jj=jj, ks=ks: e.matmul(
                                    kb.pb[4 + jj][:, :512], lhsT=KW[:, ks, jj * 128:(jj + 1) * 128],
                                    rhs=vtk[:, hh * 512:(hh + 1) * 512], start=True, stop=True),
                                    reads=[("KW", ks), "vtk"], writes=[("pb", 4 + jj)])
                                P.dve(lambda e, hh=hh, jj=jj, a_ap=a_ap: e.scalar_tensor_tensor(
                                    out=Cs[:, hh, jj, :], in0=Cs[:, hh, jj, :], scalar=a_ap, in1=kb.pb[4 + jj][:, :512],
                                    op0=ALU.mult, op1=ALU.add),
                                    reads=[("Cs", hh), ("pb", 4 + jj), "mg_am"], writes=[("Cs", hh)])
                            P.pool(lambda e, hh=hh: e.tensor_copy(out=Cb[:, hh], in_=Cs[:, hh]), reads=[("Cs", hh)],
                                   writes=[("Cb", hh)])
                            pnu = kb.pb[1][:, 16 + hh * 4:16 + hh * 4 + 4]
                            for jj in range(4):
                                P.pe(lambda e, hh=hh, jj=jj, ks=ks: e.matmul(
                                    kb.pb[1][:, 16 + hh * 4 + jj:16 + hh * 4 + jj + 1],
                                    lhsT=KW[:, ks, jj * 128:(jj + 1) * 128], rhs=onesb[:, 0:1], start=True, stop=True),
                                    reads=[("KW", ks), "onesb"], writes=[("pb", 1)])
                            P.dve(lambda e, hh=hh, pnu=pnu, a_ap=a_ap: e.scalar_tensor_tensor(
                                out=ns[:, hh, :], in0=ns[:, hh, :], scalar=a_ap, in1=pnu, op0=ALU.mult, op1=ALU.add),
                                reads=["ns", ("pb", 1), "mg_am"], writes=["ns"])
                            P.dve(lambda e, hh=hh: e.tensor_copy(out=nb[:, hh, :], in_=ns[:, hh, :]), reads=["ns"],
                                  writes=["nb"])
                    if not need_h:
                        continue
                    hst_res = [("hst", hh) for hh in range(4)]
                    if d == 0:
                        P.dma(S["hftok"][c * 128:(c + 1) * 128, :], hst[:], reads=hst_res, writes=[("hftok", c)],
                              q="act")
                        continue
                    P.dma(hf[:], S["hftok"][c * 128:(c + 1) * 128, :], reads=[("hftok", c)], writes=["hf"])
                    P.dma(sz[:], S["sztok"][c * 128:(c + 1) * 128, :], reads=[("sztok", c)], writes=["sz"])
                    P.dma(xcT[:], S["xcTb"][c], reads=[("xcTb", c)], writes=["xcT"])
                    t0 = c * 128
                    P.dma(hrs[:], hT_tile(h_in, t0, 128), reads=hres(h_in_res, t0, 128), writes=["hrs"])
                    P.pool(lambda e: e.tensor_tensor(out=hf[:], in0=hf[:], in1=hst[:], op=ALU.add),
                           reads=["hf"] + hst_res, writes=["hf"])
                    for hh in range(4):
                        hsl = slice(hh * 512, (hh + 1) * 512)
                        P.dve(lambda e, hsl=hsl: e.tensor_tensor(out=hf[:, hsl], in0=hf[:, hsl], in1=sz[:, hsl],
                                                                 op=ALU.mult),
                              reads=["hf", "sz"], writes=["hf"])
                        P.dve(lambda e, hh=hh, hsl=hsl: e.bn_stats(out=bst[:, hh, 0:6], in_=hf[:, hsl]), reads=["hf"],
                              writes=[("bst", hh)])
                        P.dve(lambda e, hh=hh: e.bn_aggr(out=bst[:, hh, 6:8], in_=bst[:, hh, 0:6]), reads=[("bst", hh)],
                              writes=[("bst", hh)])
                        P.act(lambda e, hh=hh: e.activation(out=bst[:, hh, 7:8], in_=bst[:, hh, 7:8], func=AF.Sqrt,
                                                            bias=kb.epsc[:, 1:2]),
                              reads=[("bst", hh), "epsc"], writes=[("bst", hh)])
                        P.dve(lambda e, hh=hh: e.reciprocal(out=bst[:, hh, 7:8], in_=bst[:, hh, 7:8]),
                              reads=[("bst", hh)], writes=[("bst", hh)])
                        P.dve(lambda e, hh=hh, hsl=hsl: e.tensor_scalar(out=hf[:, hsl], in0=hf[:, hsl],
                                                                        scalar1=bst[:, hh, 6:7], scalar2=bst[:, hh, 7:8],
                                                                        op0=ALU.subtract, op1=ALU.mult),
                              reads=["hf", ("bst", hh)], writes=["hf"])
                        P.pool(lambda e, hsl=hsl: e.tensor_tensor(out=hnb[:, hsl], in0=hf[:, hsl], in1=ngbc[:, hsl],
                                                                  op=ALU.mult),
                               reads=["hf", "ngbc"], writes=["hnb"])
                    for half in range(2):
                        ptv = kb.pb[2 + half][:].bitcast(BF16)
                        for e8 in range(8):
                            e_ = half * 8 + e8
                            P.pe(lambda e, e_=e_, e8=e8, ptv=ptv: e.transpose(out=ptv[:, e8 * 128:(e8 + 1) * 128],
                                                                              in_=hnb[:, e_ * 128:(e_ + 1) * 128],
                                                                              identity=kb.identb[:]),
                                 reads=["hnb", "identb"], writes=[("pb", 2 + half)])
                        for e8 in range(8):
                            e_ = half * 8 + e8
                            P.dve(lambda e, e_=e_, e8=e8, ptv=ptv: e.scalar_tensor_tensor(
                                out=xin[:, e_, :], in0=xcT[:, e_, :], scalar=kb.v("ml_skip", j * 16 + e_),
                                in1=ptv[:, e8 * 128:(e8 + 1) * 128], op0=ALU.mult, op1=ALU.add),
                                reads=["xcT", ("pb", 2 + half), "vecs"], writes=[("xin", e_)])
                    finals.extend(proj_ln(kb, lb, xin, lambda kc: ("xin", kc), 16, wdn, "wdn",
                                          lambda o: hrs[:, o, :], "hrs", li, 128, col,
                                          lambda t0=t0: hT_tile(h_out, t0, 128), hres(h_out_res, t0, 128)))
            P.barrier()

        run_dir()
    return finals


def mlstm_mixer(kb, li, j, last, h_in, h_in_res, h_out, h_out_res, wup_d, bd_d, wif_d, wdown_d, ngbc_d):
    mlstm_prep(kb, li, j, h_in, h_in_res, wup_d, bd_d, wif_d)
    with contextlib.ExitStack() as es:
        gates = mlstm_gates(kb, es, "g%d" % li)
        finals = mlstm_scan(kb, li, j, last, gates, h_in, h_in_res, h_out, h_out_res, wdown_d, ngbc_d)
    kb.P.barrier()
    return finals


def rope_tables_host():
    axis_dim = 32
    freqs = (10000.0 ** (-np.arange(0, axis_dim, 2, dtype=np.float32) / axis_dim)).astype(np.float32)
    t = np.arange(SEQ)
    rows = (t // 64).astype(np.float32)
    cols = (t % 64).astype(np.float32)
    ang = np.concatenate([rows[:, None] * freqs, cols[:, None] * freqs], axis=-1).astype(np.float32)
    cos = np.cos(ang).astype(np.float32)
    sin = np.sin(ang).astype(np.float32)
    p = np.arange(128)
    i = (p % 64) // 2
    par = p % 2
    c = cos[:, i].T
    s = sin[:, i].T * np.where(par == 0, -1.0, 1.0)[:, None]
    return np.ascontiguousarray(np.stack([c * 0.125, s * 0.125, c, s]).astype(np.float32))


def swap_pairs(w):
    w = w.reshape(w.shape[0], -1, 2)[:, :, ::-1]
    return np.ascontiguousarray(w.reshape(w.shape[0], -1))


def build_program(n_layers=DEPTH):
    nc = bass.Bass("TRN2", target_bir_lowering=False)
    kb = KB(nc)
    h0 = kb.din("h0", [D, LT])
    vecs = kb.din("vecs", [128, NV])
    ident = kb.din("ident", [128, 128])
    modw = kb.din("mod_w", [DEPTH, D, 9216])
    w_in = kb.din("ffn_w_in", [DEPTH, 2, D, 2 * DFF])
    w_out = kb.din("ffn_w_out", [DEPTH, 2, DFF, D])
    wup = kb.din("ml_w_up", [2, D, 4096])
    bd = kb.din("ml_bd", [2, 3, 128, 16, 128])
    wif = kb.din("ml_wif", [2, 128, 48, 16])
    wdn = kb.din("ml_w_down", [2, E2, D])
    ngbc = kb.din("ml_ngbc", [2, 128, E2])
    wqkv = kb.din("at_w_qkv", [1, D, 1536])
    wsw = kb.din("at_wsw", [D, 1280])
    wo = kb.din("at_w_o", [1, D, D])
    rope = kb.din("rope", [4, 128, SEQ])
    scw_in = kb.din("sc_w_in", [1, D, 3072])
    scw_out = kb.din("sc_w_out", [1, D, D])
    out = kb.dout("out", [D, SEQ])
    hA = kb.dscr("hA", [D, LT])
    hB = kb.dscr("hB", [D, LT])
    setup_consts(kb, vecs, ident)
    compute_mod(kb, modw, list(range(n_layers)))
    bufs = [(hA, "hA"), (hB, "hB")]
    cur = (h0, "h0")
    nxt_i = 0
    finals = []
    lat_tiles = [t for t in TILES if t[2] == 0]
    for i in range(n_layers):
        last = (i == DEPTH - 1)
        kind, j = i % 3, i // 3
        nxt = bufs[nxt_i]
        ffn_sublayer(kb, i, 0, cur[0], cur[1], nxt[0], nxt[1], w_in, w_out, TILES)
        cur, nxt_i = nxt, 1 - nxt_i
        nxt = bufs[nxt_i]
        if kind == 0:
            mlstm_mixer(kb, i, j, last, cur[0], cur[1], nxt[0], nxt[1], wup, bd, wif, wdn, ngbc)
        elif kind == 1:
            attention_mixer(kb, i, cur[0], cur[1], nxt[0], nxt[1], wqkv, wsw, wo, rope, TILES)
        else:
            shortconv_mixer(kb, i, cur[0], cur[1], nxt[0], nxt[1], scw_in, scw_out, TILES)
        cur, nxt_i = nxt, 1 - nxt_i
        nxt = bufs[nxt_i]
        if last:
            finals = ffn_sublayer(kb, i, 1, cur[0], cur[1], out, "out", w_in, w_out, lat_tiles, out_col0=NCTX)
        else:
            finals = ffn_sublayer(kb, i, 1, cur[0], cur[1], nxt[0], nxt[1], w_in, w_out, TILES)
            cur, nxt_i = nxt, 1 - nxt_i
    if n_layers < DEPTH:
        finals = [kb.P.dma(out, cur[0][:, NCTX:LT], reads=hres(cur[1], NCTX, SEQ), writes=["out"])]
    stats = kb.P.emit(final_waits=[o.idx for o in finals])
    return nc, stats


def host_inputs(inputs, n_cores=8):
    I = {k: np.asarray(v) for k, v in inputs.items()}
    wqkv = np.ascontiguousarray(I["at_w_qkv"], np.float32)
    shared = {
        "ident": np.eye(128, dtype=np.float32),
        "mod_w": np.ascontiguousarray(I["mod_w"], np.float32),
        "ffn_w_in": np.ascontiguousarray(I["ffn_w_in"], np.float32),
        "ffn_w_out": np.ascontiguousarray(I["ffn_w_out"], np.float32),
        "ml_w_up": np.ascontiguousarray(I["ml_w_up"], np.float32),
        "ml_bd": np.stack([host_bd(I["ml_w_qkv"][j]) for j in range(2)]),
        "ml_wif": np.stack([host_wif(I["ml_w_if"][j]) for j in range(2)]),
        "ml_w_down": np.ascontiguousarray(I["ml_w_down"], np.float32),
        "ml_ngbc": np.ascontiguousarray(np.broadcast_to(I["ml_norm_g"].astype(np.float32)[:, None, :], (2, 128, E2))),
        "at_w_qkv": wqkv,
        "at_wsw": swap_pairs(wqkv[0][:, :1280]),
        "at_w_o": np.ascontiguousarray(I["at_w_o"], np.float32),
        "rope": rope_tables_host(),
        "sc_w_in": np.ascontiguousarray(I["sc_w_in"], np.float32),
        "sc_w_out": np.ascontiguousarray(I["sc_w_out"], np.float32),
    }
    maps = []
    for b in range(n_cores):
        m = dict(shared)
        m["h0"] = np.ascontiguousarray(np.concatenate([I["ctx"][b].T, I["x"][b].T], axis=1), np.float32)
        m["vecs"] = pack_vecs(I, b)
        maps.append(m)
    return maps


def kernel(**inputs):
    nc, _ = build_program()
    maps = host_inputs(inputs, 8)
    res = run_bass_kernel_spmd(nc, maps, core_ids=list(range(8)))
    outs = [np.asarray(r["out"], np.float32).T for r in res.results]
    return np.ascontiguousarray(np.stack(outs, axis=0))
```

```python
import contextlib
import numpy as np
import concourse.bass as bass
import concourse.mybir as mybir
from concourse.bass_utils import run_bass_kernel_spmd

F32 = mybir.dt.float32
BF16 = mybir.dt.bfloat16
I32 = mybir.dt.int32
AF = mybir.ActivationFunctionType
ALU = mybir.AluOpType
AX = mybir.AxisListType


ATTACH_WAITS = True


class Op:
    __slots__ = ("idx", "eng", "fn", "dma", "deps", "signal", "sem", "val", "prewait")

    def __init__(self, idx, eng, fn, dma):
        self.idx = idx
        self.eng = eng
        self.fn = fn
        self.dma = dma
        self.deps = set()
        self.signal = False
        self.sem = None
        self.val = 0
        self.prewait = None


class Prog:
    ENGS = ("pe", "act", "dve", "pool", "sp")

    def __init__(self, nc, n_dma_sems=6):
        self.nc = nc
        self.ops = []
        self.lastw = {}
        self.readers = {}
        self.n_dma_sems = n_dma_sems

    def add(self, eng, fn, reads=(), writes=(), dma=False):
        op = Op(len(self.ops), eng, fn, dma)
        raw = set()
        for r in reads:
            w = self.lastw.get(r)
            if w is not None:
                raw.add(w)
        other = set()
        for r in writes:
            w = self.lastw.get(r)
            if w is not None:
                other.add(w)
            for q in self.readers.get(r, ()):
                other.add(q)
        op.deps = set(raw)
        for d in other:
            dop = self.ops[d]
            if (not dma) and (not dop.dma) and dop.eng == eng and eng != "pool":
                continue
            op.deps.add(d)
        for r in reads:
            self.readers.setdefault(r, []).append(op.idx)
        for r in writes:
            self.lastw[r] = op.idx
            self.readers[r] = []
        op.deps.discard(op.idx)
        self.ops.append(op)
        return op

    def pe(self, fn, reads=(), writes=()):
        return self.add("pe", fn, reads, writes)

    def act(self, fn, reads=(), writes=()):
        return self.add("act", fn, reads, writes)

    def dve(self, fn, reads=(), writes=()):
        return self.add("dve", fn, reads, writes)

    def pool(self, fn, reads=(), writes=()):
        return self.add("pool", fn, reads, writes)

    def dma(self, out, in_, reads=(), writes=(), q="sp"):
        return self.add(q, lambda e: e.dma_start(out=out, in_=in_), reads, writes, dma=True)

    def barrier(self):
        last = {}
        dmas = {e: [] for e in self.ENGS}
        for op in self.ops:
            if op.fn is None:
                continue
            if op.dma:
                dmas[op.eng].append(op.idx)
            else:
                last[op.eng] = op.idx
        deps = set(last.values())
        for e in self.ENGS:
            deps.update(dmas[e][-self.n_dma_sems:])
        for e in self.ENGS:
            op = Op(len(self.ops), e, None, False)
            op.deps = set(deps)
            self.ops.append(op)
        self.lastw = {}
        self.readers = {}

    def emit(self, final_waits=()):
        nc = self.nc
        ops = self.ops
        for op in ops:
            if op.eng == "pe" and not op.dma:
                op.deps = {d for d in op.deps if not (ops[d].eng == "pe" and not ops[d].dma)}
            op.deps = {d for d in op.deps if ops[d].fn is not None}
            for d in op.deps:
                ops[d].signal = True
        for idx in final_waits:
            ops[idx].signal = True
        esem = {e: nc.alloc_semaphore("s_" + e) for e in self.ENGS}
        dsem = {e: [nc.alloc_semaphore("d_%s%d" % (e, i)) for i in range(self.n_dma_sems)]
                for e in ("sp", "act", "pool")}
        ecount = {e: 0 for e in self.ENGS}
        dcount = {e: 0 for e in self.ENGS}
        P = self.n_dma_sems
        for op in ops:
            if op.dma:
                i = dcount[op.eng]
                dcount[op.eng] += 1
                op.sem = dsem[op.eng][i % P]
                op.val = 16 * (i // P + 1)
                op.signal = True
                if i >= P:
                    op.prewait = (op.sem, 16 * (i // P))
            elif op.signal:
                ecount[op.eng] += 1
                op.sem = esem[op.eng]
                op.val = ecount[op.eng]
        per_eng = {e: [] for e in self.ENGS}
        for op in ops:
            per_eng[op.eng].append(op)
        self.stats = {e: len(v) for e, v in per_eng.items()}
        nwaits = [0]

        def run(e, eng_handle, extra_final):
            waited = {}

            def wait(sem, val):
                k = id(sem)
                if waited.get(k, 0) >= val:
                    return
                waited[k] = val
                eng_handle.wait_ge(sem, val)
                nwaits[0] += 1

            for op in per_eng[e]:
                if op.prewait is not None:
                    wait(*op.prewait)
                need = {}
                for d in op.deps:
                    dop = ops[d]
                    k = id(dop.sem)
                    if k not in need or need[k][1] < dop.val:
                        need[k] = (dop.sem, dop.val)
                pend = [(sem, val) for sem, val in need.values() if waited.get(id(sem), 0) < val]
                if op.fn is None:
                    for sem, val in pend:
                        wait(sem, val)
                    continue
                attach = None
                if pend and ATTACH_WAITS:
                    attach = pend.pop()
                for sem, val in pend:
                    wait(sem, val)
                ins = op.fn(eng_handle)
                if attach is not None:
                    ins._wait_ge(attach[0], attach[1])
                    waited[id(attach[0])] = attach[1]
                if op.signal:
                    ins.then_inc(op.sem, 16 if op.dma else 1)
            if extra_final:
                for idx in final_waits:
                    wait(ops[idx].sem, ops[idx].val)
                for q in ("sp", "act", "pool"):
                    n = dcount[q]
                    for s in range(min(P, n)):
                        last_i = ((n - 1 - s) // P) * P + s
                        wait(dsem[q][s], 16 * (last_i // P + 1))

        with nc.Block() as block:
            @block.tensor
            def _(eng):
                run("pe", eng, False)

            @block.scalar
            def _(eng):
                run("act", eng, False)

            @block.vector
            def _(eng):
                run("dve", eng, False)

            @block.gpsimd
            def _(eng):
                run("pool", eng, False)

            @block.sync
            def _(eng):
                run("sp", eng, True)
        self.stats["waits"] = nwaits[0]
        return self.stats

import numpy as np

D = 1024
NCTX = 256
SEQ = 4096
LT = NCTX + SEQ
DEPTH = 4
DFF = 2816
NJ = DFF // 128
ALPHA = (2 * DEPTH) ** 0.25
LN_EPS = 1e-5
TILES = [(0, 256, 1)] + [(256 + 512 * i, 512, 0) for i in range(8)]

VEC_SPEC = [("cond", 16), ("mod_b", 4 * 72), ("ln_g", 4 * 3 * 8), ("ln_b", 4 * 3 * 8),
            ("ml_conv_w", 2 * 3 * 16), ("ml_conv_b", 2 * 16), ("ml_skip", 2 * 16),
            ("sc_conv_w", 3 * 8), ("at_sink", 16), ("ml_b_if", 2 * 16)]
VOFF = {}
_o = 0
for _n, _c in VEC_SPEC:
    VOFF[_n] = _o
    _o += _c
NV = _o


_UNC = [0]


def UN(name):
    _UNC[0] += 1
    return "%s_%d" % (name, _UNC[0])


def fm(v):
    v = np.asarray(v, np.float32)
    F = v.shape[-1]
    lead = v.shape[:-1]
    a = v.reshape(lead + (F // 128, 128))
    a = np.moveaxis(a, -1, 0)
    return a.reshape(128, -1)


def pack_vecs(inp, b):
    out = np.zeros((128, NV), np.float32)

    def put(name, arr):
        arr = np.asarray(arr, np.float32)
        out[:, VOFF[name]:VOFF[name] + arr.shape[1]] = arr

    cond = np.stack([inp["c"][b], inp["c_ctx"]], axis=-1)
    put("cond", cond.reshape(8, 128, 2).transpose(1, 0, 2).reshape(128, 16))
    put("mod_b", fm(inp["mod_b"]))
    put("ln_g", fm(inp["ln_g"]))
    put("ln_b", fm(inp["ln_b"]))
    put("ml_conv_w", fm(inp["ml_conv_w"]))
    put("ml_conv_b", fm(inp["ml_conv_b"]))
    put("ml_skip", fm(inp["ml_skip"]))
    put("sc_conv_w", fm(inp["sc_conv_w"]))
    put("at_sink", np.broadcast_to(inp["at_sink"].reshape(1, 16), (128, 16)))
    put("ml_b_if", np.broadcast_to(inp["ml_b_if"].reshape(1, 32), (128, 32)))
    return out


class KB:
    def __init__(self, nc):
        self.nc = nc
        self.P = Prog(nc)
        self.uid = 0
        nc_ = nc
        self.pb = [nc_.alloc_psum_tensor("pb%d" % i, [128, 512], F32) for i in range(8)]
        self.vecs = nc_.alloc_sbuf_tensor("sb_vecs", [128, NV], F32)
        self.mod = nc_.alloc_sbuf_tensor("sb_mod", [128, DEPTH, 72, 2], F32)
        self.ident = nc_.alloc_sbuf_tensor("sb_ident", [128, 128], F32)
        self.identb = nc_.alloc_sbuf_tensor("identb", [128, 128], BF16)
        self.onesm = nc_.alloc_sbuf_tensor("onesm", [128, 128], F32)
        self.epsc = nc_.alloc_sbuf_tensor("epsc", [128, 4], F32)
        self.dram = {}
        self.wbf_in = nc_.dram_tensor("wbf_in", [NJ // 2, 128, 8, 512], BF16).ap()
        self.wbf_out = nc_.dram_tensor("wbf_out", [8, 128, NJ, 128], BF16).ap()

    def din(self, name, shape, dt=F32):
        t = self.nc.dram_tensor(name, list(shape), dt, kind="ExternalInput")
        self.dram[name] = t
        return t.ap()

    def dout(self, name, shape, dt=F32):
        t = self.nc.dram_tensor(name, list(shape), dt, kind="ExternalOutput")
        self.dram[name] = t
        return t.ap()

    def dscr(self, name, shape, dt=F32):
        t = self.nc.dram_tensor(name, list(shape), dt)
        self.dram[name] = t
        return t.ap()

    def v(self, name, idx=0, n=1):
        o = VOFF[name] + idx
        return self.vecs[:, o:o + n]

    def mv(self, i, slot, c, col):
        return self.mod[:, i, slot * 8 + c, col:col + 1]


def setup_consts(kb, vecs_d, ident_d):
    P = kb.P
    P.dma(kb.vecs[:], vecs_d, writes=["vecs"])
    P.dma(kb.ident[:], ident_d, writes=["ident"])
    P.dve(lambda e: e.tensor_copy(out=kb.identb[:], in_=kb.ident[:]), reads=["ident"], writes=["identb"])
    P.pool(lambda e: e.memset(kb.onesm[:], 1.0 / 1024.0), writes=["onesm"])
    P.pool(lambda e: e.memset(kb.epsc[:, 0:1], LN_EPS / (ALPHA * ALPHA)), writes=["epsc"])
    P.pool(lambda e: e.memset(kb.epsc[:, 1:2], LN_EPS), writes=["epsc"])
    P.pool(lambda e: e.memset(kb.epsc[:, 2:3], -0.5 * float(np.log(512.0))), writes=["epsc"])
    P.pool(lambda e: e.memset(kb.epsc[:, 3:4], 1.0), writes=["epsc"])


def compute_mod(kb, modw_d, layers, dmap=None):
    nc, P = kb.nc, kb.P
    condT = kb.v("cond", 0, 16)
    P.act(lambda e: e.activation(out=condT, in_=condT, func=AF.Silu), reads=["vecs"], writes=["vecs"])
    with nc.sbuf_tensor(UN("mw_stage"), [128, 2, 8, 512], F32) as stage, \
            nc.sbuf_tensor(UN("mrow"), [2, 9216], F32) as mrow:
        cnt = 0
        for i in layers:
            for nb in range(18):
                s = cnt % 2
                cnt += 1
                src = modw_d[i if dmap is None else dmap[i]].rearrange("(kc p) n -> p kc n", p=128)[:, :, nb * 512:(nb + 1) * 512]
                P.dma(stage[:, s], src, writes=[("mws", s)])
                ps = kb.pb[s]
                for kc in range(8):
                    P.pe(lambda e, kc=kc, s=s, ps=ps: e.matmul(
                        ps[0:2, :], lhsT=kb.vecs[:, VOFF["cond"] + 2 * kc:VOFF["cond"] + 2 * kc + 2],
                        rhs=stage[:, s, kc, :], start=(kc == 0), stop=(kc == 7)),
                        reads=["vecs", ("mws", s)], writes=[("pb", s)])
                P.act(lambda e, nb=nb, ps=ps: e.activation(out=mrow[0:2, nb * 512:(nb + 1) * 512], in_=ps[0:2, :],
                                                            func=AF.Identity),
                      reads=[("pb", s)], writes=["mrow"])
            pst = kb.pb[2]
            for k in range(72):
                P.pe(lambda e, k=k: e.transpose(out=pst[:, 2 * k:2 * k + 2], in_=mrow[0:2, k * 128:(k + 1) * 128],
                                                identity=kb.ident[0:2, 0:2]),
                     reads=["mrow", "ident"], writes=[("pb", 2)])
            for j in range(2):
                P.dve(lambda e, i=i, j=j: e.tensor_tensor(
                    out=kb.mod[:, i, :, j], in0=pst[:, j:144:2], in1=kb.v("mod_b", i * 72, 72), op=ALU.add),
                    reads=[("pb", 2), "vecs"], writes=[("mod", i)])
            for s3 in range(3):
                w = 1.0 if s3 == 1 else 0.5
                sl = kb.mod[:, i, (3 * s3 + 1) * 8:(3 * s3 + 2) * 8, :]
                P.dve(lambda e, sl=sl: e.tensor_scalar(out=sl, in0=sl, scalar1=1.0, scalar2=None, op0=ALU.add),
                      reads=[("mod", i)], writes=[("mod", i)])
                gl = kb.mod[:, i, (3 * s3 + 2) * 8:(3 * s3 + 3) * 8, :]
                P.dve(lambda e, gl=gl, w=w: e.tensor_scalar(out=gl, in0=gl, scalar1=w / ALPHA, scalar2=None,
                                                            op0=ALU.mult),
                      reads=[("mod", i)], writes=[("mod", i)])
    P.barrier()


class LNBufs:
    def __init__(self, nc, es, n, tag):
        self.n = n
        self.z = es.enter_context(nc.sbuf_tensor(UN("lnz" + tag), [128, 8, n], F32))
        self.zq = es.enter_context(nc.sbuf_tensor(UN("lnzq" + tag), [128, 8, n], F32))
        self.sm = es.enter_context(nc.sbuf_tensor(UN("lnsm" + tag), [128, 3, n], F32))
        self.tag = tag


def ln_accum(kb, lb, o, y_ps, y_res, hsrc, hres, gs_ap, n, extra_reads=()):
    P = kb.P
    t = lb.tag
    P.dve(lambda e: e.scalar_tensor_tensor(out=lb.z[:, o, :n], in0=y_ps, scalar=gs_ap, in1=hsrc,
                                           op0=ALU.mult, op1=ALU.add),
          reads=[y_res, hres] + list(extra_reads), writes=[("lnz" + t, o)])
    P.act(lambda e: e.activation(out=lb.zq[:, o, :n], in_=lb.z[:, o, :n], func=AF.Square),
          reads=[("lnz" + t, o)], writes=[("lnzq" + t, o)])
    P.pe(lambda e: e.matmul(kb.pb[6][:, :n], lhsT=kb.onesm[:], rhs=lb.z[:, o, :n], start=(o == 0), stop=(o == 7)),
         reads=[("lnz" + t, o), "onesm"], writes=[("pb", 6)])
    P.pe(lambda e: e.matmul(kb.pb[7][:, :n], lhsT=kb.onesm[:], rhs=lb.zq[:, o, :n], start=(o == 0), stop=(o == 7)),
         reads=[("lnzq" + t, o), "onesm"], writes=[("pb", 7)])


def ln_finish(kb, lb, n, li, k, dst_ap_fn, dst_res):
    P = kb.P
    t = lb.tag
    sm = lb.sm
    mean, msq, std = (sm[:, i, :n] for i in range(3))
    var, rstd, nmr = msq, std, mean
    R = ("lnsm" + t)
    P.act(lambda e: e.activation(out=mean, in_=kb.pb[6][:, :n], func=AF.Identity), reads=[("pb", 6)], writes=[(R, 0)])
    P.pool(lambda e: e.tensor_tensor(out=msq, in0=mean, in1=mean, op=ALU.mult), reads=[(R, 0)], writes=[(R, 1)])
    P.dve(lambda e: e.tensor_tensor(out=var, in0=kb.pb[7][:, :n], in1=msq, op=ALU.subtract),
          reads=[("pb", 7), (R, 1)], writes=[(R, 1)])
    P.act(lambda e: e.activation(out=std, in_=var, func=AF.Sqrt, bias=kb.epsc[:, 0:1]), reads=[(R, 1), "epsc"],
          writes=[(R, 2)])
    P.dve(lambda e: e.reciprocal(out=rstd, in_=std), reads=[(R, 2)], writes=[(R, 2)])
    P.dve(lambda e: e.scalar_tensor_tensor(out=nmr, in0=mean, scalar=-1.0, in1=rstd, op0=ALU.mult, op1=ALU.mult),
          reads=[(R, 0), (R, 2)], writes=[(R, 0)])
    for o in range(8):
        zo = lb.z[:, o, :n]
        P.dve(lambda e, zo=zo: e.tensor_tensor(out=zo, in0=zo, in1=rstd, op=ALU.mult),
              reads=[("lnz" + t, o), (R, 2)], writes=[("lnz" + t, o)])
        P.pool(lambda e, zo=zo: e.tensor_tensor(out=zo, in0=zo, in1=nmr, op=ALU.add),
               reads=[("lnz" + t, o), (R, 0)], writes=[("lnz" + t, o)])
        ho = lb.zq[:, o, :n]
        g_ap = kb.v("ln_g", (li * 3 + k) * 8 + o)
        b_ap = kb.v("ln_b", (li * 3 + k) * 8 + o)
        P.act(lambda e, zo=zo, ho=ho, g_ap=g_ap, b_ap=b_ap: e.activation(out=ho, in_=zo, func=AF.Identity,
                                                                         bias=b_ap, scale=g_ap),
              reads=[("lnz" + t, o), "vecs"], writes=[("lnzq" + t, o)])
    ops = []
    ops.append(P.dma(dst_ap_fn(), lb.zq[:, :, :n], reads=[("lnzq" + t, o) for o in range(8)], writes=list(dst_res),
                     q="act"))
    return ops


def hres(name, t0, n):
    return [(name, c) for c in range(t0 // 128, (t0 + n + 127) // 128)]


def hT_tile(h_ap, t0, n):
    return h_ap.rearrange("(c p) t -> p c t", p=128)[:, :, t0:t0 + n]


def ffn_sublayer(kb, li, which, h_in, h_in_res, h_out, h_out_res, w_in_d, w_out_d, tiles, out_col0=None):
    import contextlib
    nc, P = kb.nc, kb.P
    slot = 0 if which == 0 else 2
    k = slot
    w_in = w_in_d[li, which]
    w_out = w_out_d[li, which]
    w_in_v = w_in.rearrange("(kc p) n -> p kc n", p=128)
    w_out_v = w_out.rearrange("(j p) n -> p j n", p=128)
    tag = "f"
    final_ops = []
    with contextlib.ExitStack() as es:
        A = lambda name, shape, dt: es.enter_context(nc.sbuf_tensor(UN(name), shape, dt))
        stg = A("ff_stg", [128, 2, 8, 512], F32)
        wb = A("ff_wb", [128, 2, 8, 512], BF16)
        stgo = A("ff_stgo", [128, 22, 128], F32)
        wob = A("ff_wob", [128, 2, 22, 128], BF16)
        ht = A("ff_ht", [128, 2, 8, 512], F32)
        u = A("ff_u", [128, 8, 512], BF16)
        a = A("ff_a", [128, 22, 512], BF16)
        sg = A("ff_sg", [128, 2, 512], F32)
        lb = LNBufs(nc, es, 512, tag)
        wcnt = 0
        ocnt = 0
        gcnt = 0
        for ti, (t0, n, col) in enumerate(tiles):
            hs = ti % 2
            P.dma(ht[:, hs, :, :n], hT_tile(h_in, t0, n), reads=hres(h_in_res, t0, n), writes=[("ff_ht", hs)])
            for c in range(8):
                P.act(lambda e, c=c, hs=hs, n=n, col=col: e.activation(
                    out=u[:, c, :n], in_=ht[:, hs, c, :n], func=AF.Identity,
                    bias=kb.mv(li, 3 * slot, c, col), scale=kb.mv(li, 3 * slot + 1, c, col)),
                    reads=[("ff_ht", hs), ("mod", li)], writes=[("ff_u", c)])
            first = (ti == 0)
            for jp in range(NJ // 2):
                ws = wcnt % 2
                wcnt += 1
                if first:
                    P.dma(stg[:, ws, :, 0:256], w_in_v[:, :, jp * 256:(jp + 1) * 256], writes=[("ff_stg", ws, 0)])
                    P.dma(stg[:, ws, :, 256:512], w_in_v[:, :, DFF + jp * 256:DFF + (jp + 1) * 256],
                          writes=[("ff_stg", ws, 1)])
                    for hf in range(2):
                        P.dve(lambda e, ws=ws, hf=hf: e.tensor_copy(out=wb[:, ws, :, hf * 256:(hf + 1) * 256],
                                                                    in_=stg[:, ws, :, hf * 256:(hf + 1) * 256]),
                              reads=[("ff_stg", ws, hf)], writes=[("ff_wb", ws, hf)])
                    P.dma(kb.wbf_in[jp], wb[:, ws], reads=[("ff_wb", ws, 0), ("ff_wb", ws, 1)],
                          writes=[("wbf_in", jp)], q="pool")
                else:
                    P.dma(wb[:, ws], kb.wbf_in[jp], reads=[("wbf_in", jp)],
                          writes=[("ff_wb", ws, 0), ("ff_wb", ws, 1)])
                for jj in range(2):
                    j = jp * 2 + jj
                    gs_ = gcnt % 2
                    gcnt += 1
                    pg, pv = kb.pb[gs_], kb.pb[2 + gs_]
                    for kc in range(8):
                        P.pe(lambda e, kc=kc, ws=ws, jj=jj, pg=pg, n=n: e.matmul(
                            pg[:, :n], lhsT=wb[:, ws, kc, jj * 128:(jj + 1) * 128], rhs=u[:, kc, :n],
                            start=(kc == 0), stop=(kc == 7)),
                            reads=[("ff_wb", ws, 0), ("ff_u", kc)], writes=[("pb", gs_)])
                    for kc in range(8):
                        P.pe(lambda e, kc=kc, ws=ws, jj=jj, pv=pv, n=n: e.matmul(
                            pv[:, :n], lhsT=wb[:, ws, kc, 256 + jj * 128:256 + (jj + 1) * 128], rhs=u[:, kc, :n],
                            start=(kc == 0), stop=(kc == 7)),
                            reads=[("ff_wb", ws, 1), ("ff_u", kc)], writes=[("pb", 2 + gs_)])
                    P.act(lambda e, gs_=gs_, pg=pg, n=n: e.activation(out=sg[:, gs_, :n], in_=pg[:, :n], func=AF.Silu),
                          reads=[("pb", gs_)], writes=[("ff_sg", gs_)])
                    P.dve(lambda e, gs_=gs_, pv=pv, j=j, n=n: e.tensor_tensor(out=a[:, j, :n], in0=pv[:, :n],
                                                                              in1=sg[:, gs_, :n], op=ALU.mult),
                          reads=[("pb", 2 + gs_), ("ff_sg", gs_)], writes=[("ff_a", j)])
            for o in range(8):
                os_ = ocnt % 2
                ocnt += 1
                if first:
                    P.dma(stgo[:], w_out_v[:, :, o * 128:(o + 1) * 128], writes=["ff_stgo"])
                    P.act(lambda e, os_=os_: e.activation(out=wob[:, os_], in_=stgo[:], func=AF.Identity),
                          reads=["ff_stgo"], writes=[("ff_wob", os_)])
                    P.dma(kb.wbf_out[o], wob[:, os_], reads=[("ff_wob", os_)], writes=[("wbf_out", o)], q="pool")
                else:
                    P.dma(wob[:, os_], kb.wbf_out[o], reads=[("wbf_out", o)], writes=[("ff_wob", os_)])
                py = kb.pb[4 + os_]
                for j in range(NJ):
                    P.pe(lambda e, j=j, os_=os_, py=py, n=n: e.matmul(
                        py[:, :n], lhsT=wob[:, os_, j, :], rhs=a[:, j, :n], start=(j == 0), stop=(j == NJ - 1)),
                        reads=[("ff_wob", os_), ("ff_a", j)], writes=[("pb", 4 + os_)])
                ln_accum(kb, lb, o, py[:, :n], ("pb", 4 + os_), ht[:, hs, o, :n], ("ff_ht", hs),
                         kb.mv(li, 3 * slot + 2, o, col), n, extra_reads=[("mod", li)])
            oc = t0 - (out_col0 or 0)
            final_ops += ln_finish(kb, lb, n, li, k, lambda oc=oc, n=n: hT_tile(h_out, oc, n), hres(h_out_res, t0, n))
    kb.P.barrier()
    return final_ops

import contextlib

NEG = -30000.0


def load_weight_bf16(kb, stage, wb, res, w_ap, KC, N, col0=0, eng="dve"):
    nc, P = kb.nc, kb.P
    wv = w_ap.rearrange("(kc p) n -> p kc n", p=128)
    nb = min(N, stage.shape[-1])
    for kc in range(KC):
        for n0 in range(0, N, nb):
            n1 = min(N, n0 + nb)
            s = kb.uid % 2
            kb.uid += 1
            P.dma(stage[:, s, :n1 - n0], wv[:, kc, col0 + n0:col0 + n1], writes=[("wstage", s)])
            dst = wb[:, kc, n0:n1]
            src = stage[:, s, :n1 - n0]
            if eng == "act":
                P.act(lambda e, dst=dst, src=src: e.activation(out=dst, in_=src, func=AF.Identity),
                      reads=[("wstage", s)], writes=[res])
            else:
                P.add(eng, lambda e, dst=dst, src=src: e.tensor_copy(out=dst, in_=src),
                      reads=[("wstage", s)], writes=[res])


def load_mod_tile(kb, ht, u, rname, h_in, h_in_res, li, slot, t0, n, col, s0, s1, halo):
    P = kb.P
    a0 = max(t0 - halo, s0)
    a1 = min(t0 + n + halo, s1)
    off = a0 - (t0 - halo)
    w = a1 - a0
    P.dma(ht[:, :, off:off + w], hT_tile(h_in, a0, w), reads=hres(h_in_res, a0, w), writes=[rname + "_ht"])
    for c in range(8):
        P.act(lambda e, c=c: e.activation(out=u[:, c, off:off + w], in_=ht[:, c, off:off + w], func=AF.Identity,
                                          bias=kb.mv(li, 3 * slot, c, col), scale=kb.mv(li, 3 * slot + 1, c, col)),
              reads=[rname + "_ht", ("mod", li)], writes=[(rname + "_u", c)])
        tot = n + 2 * halo
        for (z0, z1) in ((0, off), (off + w, tot)):
            if z1 > z0:
                P.act(lambda e, c=c, z0=z0, z1=z1: e.activation(out=u[:, c, z0:z1], in_=ht[:, c, off:off + z1 - z0],
                                                                func=AF.Identity, scale=0.0),
                      reads=[rname + "_ht"], writes=[(rname + "_u", c)])
    return off, w


def proj_ln(kb, lb, xin, xin_res, KC, wob, wob_res, hsrc_fn, hsrc_res, li, n, col, dst_fn, dst_res):
    P = kb.P
    for o in range(8):
        py = kb.pb[4 + (o % 2)]
        for kc in range(KC):
            P.pe(lambda e, o=o, kc=kc, py=py: e.matmul(py[:, :n], lhsT=wob[:, kc, o * 128:(o + 1) * 128],
                                                        rhs=xin[:, kc, :n], start=(kc == 0), stop=(kc == KC - 1)),
                 reads=[wob_res, xin_res(kc)], writes=[("pb", 4 + (o % 2))])
        ln_accum(kb, lb, o, py[:, :n], ("pb", 4 + (o % 2)), hsrc_fn(o), hsrc_res, kb.mv(li, 5, o, col), n,
                 extra_reads=[("mod", li)])
    return ln_finish(kb, lb, n, li, 1, dst_fn, dst_res)


def split_cols(m):
    if m <= 512:
        return [(0, m)]
    h = m // 2
    return [(0, h), (h, m)]


def shortconv_mixer(kb, li, h_in, h_in_res, h_out, h_out_res, w_in_d, w_out_d, tiles):
    nc, P = kb.nc, kb.P
    finals = []
    with contextlib.ExitStack() as es:
        A = lambda name, shape, dt: es.enter_context(nc.sbuf_tensor(UN(name), shape, dt))
        win = A("sc_win", [128, 8, 3072], BF16)
        wout = A("sc_wout", [128, 8, 1024], BF16)
        wst = A("sc_wst", [128, 2, 2048], F32)
        load_weight_bf16(kb, wst, win, "sc_win", w_in_d[0], 8, 3072)
        load_weight_bf16(kb, wst, wout, "sc_wout", w_out_d[0], 8, 1024)
        ht = A("sc_ht", [128, 8, 514], F32)
        u = A("sc_u", [128, 8, 514], BF16)
        cgs = A("sc_cg", [128, 514], F32)
        T = A("sc_T", [128, 514], F32)
        cv = A("sc_cv", [128, 512], F32)
        pin = A("sc_pin", [128, 8, 512], BF16)
        lb = LNBufs(nc, es, 512, "s")
        for (t0, n, col) in tiles:
            s0, s1 = (0, NCTX) if col == 1 else (NCTX, LT)
            off, w = load_mod_tile(kb, ht, u, "sc", h_in, h_in_res, li, 1, t0, n, col, s0, s1, 1)
            m = n + 2
            for e_ in range(8):
                for (hi, (c0, c1)) in enumerate(split_cols(m)):
                    pc, px = kb.pb[hi], kb.pb[2 + hi]
                    for kc in range(8):
                        P.pe(lambda e, kc=kc, e_=e_, pc=pc, c0=c0, c1=c1: e.matmul(
                            pc[:, :c1 - c0], lhsT=win[:, kc, 1024 + e_ * 128:1024 + (e_ + 1) * 128],
                            rhs=u[:, kc, c0:c1], start=(kc == 0), stop=(kc == 7)),
                            reads=["sc_win", ("sc_u", kc)], writes=[("pb", hi)])
                    for kc in range(8):
                        P.pe(lambda e, kc=kc, e_=e_, px=px, c0=c0, c1=c1: e.matmul(
                            px[:, :c1 - c0], lhsT=win[:, kc, 2048 + e_ * 128:2048 + (e_ + 1) * 128],
                            rhs=u[:, kc, c0:c1], start=(kc == 0), stop=(kc == 7)),
                            reads=["sc_win", ("sc_u", kc)], writes=[("pb", 2 + hi)])
                    P.act(lambda e, pc=pc, c0=c0, c1=c1: e.activation(out=cgs[:, c0:c1], in_=pc[:, :c1 - c0],
                                                                      func=AF.Identity),
                          reads=[("pb", hi)], writes=["sc_cg"])
                    P.dve(lambda e, px=px, c0=c0, c1=c1: e.tensor_tensor(out=T[:, c0:c1], in0=px[:, :c1 - c0],
                                                                         in1=cgs[:, c0:c1], op=ALU.mult),
                          reads=[("pb", 2 + hi), "sc_cg"], writes=["sc_T"])
                w0, w1, w2 = (kb.v("sc_conv_w", k_ * 8 + e_) for k_ in range(3))
                P.dve(lambda e, n=n, w1=w1: e.tensor_scalar(out=cv[:, :n], in0=T[:, 1:n + 1], scalar1=w1, scalar2=None,
                                                            op0=ALU.mult),
                      reads=["sc_T", "vecs"], writes=["sc_cv"])
                P.dve(lambda e, n=n, w0=w0: e.scalar_tensor_tensor(out=cv[:, :n], in0=T[:, 0:n], scalar=w0,
                                                                   in1=cv[:, :n], op0=ALU.mult, op1=ALU.add),
                      reads=["sc_T", "sc_cv"], writes=["sc_cv"])
                P.dve(lambda e, n=n, w2=w2: e.scalar_tensor_tensor(out=cv[:, :n], in0=T[:, 2:n + 2], scalar=w2,
                                                                   in1=cv[:, :n], op0=ALU.mult, op1=ALU.add),
                      reads=["sc_T", "sc_cv"], writes=["sc_cv"])
                pbg = kb.pb[4 + (e_ % 2)]
                for kc in range(8):
                    P.pe(lambda e, kc=kc, e_=e_, pbg=pbg, n=n: e.matmul(
                        pbg[:, :n], lhsT=win[:, kc, e_ * 128:(e_ + 1) * 128], rhs=u[:, kc, 1:n + 1],
                        start=(kc == 0), stop=(kc == 7)),
                        reads=["sc_win", ("sc_u", kc)], writes=[("pb", 4 + (e_ % 2))])
                P.dve(lambda e, pbg=pbg, e_=e_, n=n: e.tensor_tensor(out=pin[:, e_, :n], in0=pbg[:, :n], in1=cv[:, :n],
                                                                     op=ALU.mult),
                      reads=[("pb", 4 + (e_ % 2)), "sc_cv"], writes=[("sc_pin", e_)])
            finals += proj_ln(kb, lb, pin, lambda kc: ("sc_pin", kc), 8, wout, "sc_wout",
                              lambda o, n=n: ht[:, o, 1:n + 1], "sc_ht", li, n, col,
                              lambda t0=t0, n=n: hT_tile(h_out, t0, n), hres(h_out_res, t0, n))
    kb.P.barrier()
    return finals


def attention_mixer(kb, li, h_in, h_in_res, h_out, h_out_res, wqkv_d, wsw_d, wo_d, rope_d, tiles):
    nc, P = kb.nc, kb.P
    finals = []
    with contextlib.ExitStack() as es0:
      A0 = lambda name, shape, dt: es0.enter_context(nc.sbuf_tensor(UN(name), shape, dt))
      KT = A0("at_KT", [128, 4, LT], BF16)
      Vt = A0("at_Vt", [128, LT // 128, 256], BF16)
      maskb = A0("at_mask", [128, 384], F32)
      P.pool(lambda e: e.memset(maskb[:], 0.0), writes=["at_mask"])
      P.pool(lambda e: e.affine_select(out=maskb[:], in_=maskb[:], pattern=[[1, 384]], compare_op=ALU.is_ge,
                                       fill=NEG, base=0, channel_multiplier=-1),
             reads=["at_mask"], writes=["at_mask"])
      P.pool(lambda e: e.affine_select(out=maskb[:], in_=maskb[:], pattern=[[-1, 384]], compare_op=ALU.is_ge,
                                       fill=NEG, base=256, channel_multiplier=1),
             reads=["at_mask"], writes=["at_mask"])
      def phase1():
          with contextlib.ExitStack() as es:
            A = lambda name, shape, dt: es.enter_context(nc.sbuf_tensor(UN(name), shape, dt))
            wkd = A("at_wkd", [128, 8, 4, 128], BF16)
            wksd = A("at_wksd", [128, 8, 4, 128], BF16)
            wv = A("at_wv", [128, 8, 256], BF16)
            wst = A("at_wst", [128, 2, 512], F32)
            load_weight_bf16(kb, wst, wv, "at_wv", wqkv_d[0], 8, 256, 1280)
            for (w_ap, dst, rd) in ((wqkv_d[0], wkd, "at_wkd"), (wsw_d, wksd, "at_wksd")):
                wvw = w_ap.rearrange("(kc p) n -> p kc n", p=128)
                for kc in range(8):
                    s = kb.uid % 2
                    kb.uid += 1
                    P.dma(wst[:, s, :256], wvw[:, kc, 1024:1280], writes=[("wstage", s)])
                    for half in range(2):
                        P.pool(lambda e, dst=dst, kc=kc, half=half, s=s: e.tensor_copy(
                            out=dst[:, kc, :, half * 64:(half + 1) * 64],
                            in_=wst[:, s, :256].rearrange("p (g d) -> p g d", g=4)),
                            reads=[("wstage", s)], writes=[rd])
            ht = A("at_ht", [128, 8, 512], F32)
            u = A("at_u", [128, 8, 512], BF16)
            rp = A("at_rp", [128, 4, 512], F32)
            t1 = A("at_t1", [128, 2, 512], F32)
            t2 = A("at_t2", [128, 2, 512], F32)
            cnt = 0
            for (t0, n, col) in tiles:
                s0, s1 = (0, NCTX) if col == 1 else (NCTX, LT)
                load_mod_tile(kb, ht, u, "at", h_in, h_in_res, li, 1, t0, n, col, s0, s1, 0)
                if col == 0:
                    P.dma(rp[:, 2:4, :n], rope_d[2:4, :, t0 - NCTX:t0 - NCTX + n].rearrange("a p t -> p a t"),
                          writes=["at_rp"])
                for g in range(4):
                    s = cnt % 2
                    cnt += 1
                    pk, pks = kb.pb[s], kb.pb[2 + s]
                    for kc in range(8):
                        P.pe(lambda e, kc=kc, g=g, pk=pk, n=n: e.matmul(pk[:, :n], lhsT=wkd[:, kc, g, :], rhs=u[:, kc, :n],
                                                                        start=(kc == 0), stop=(kc == 7)),
                             reads=["at_wkd", ("at_u", kc)], writes=[("pb", s)])
                    if col == 0:
                        for kc in range(8):
                            P.pe(lambda e, kc=kc, g=g, pks=pks, n=n: e.matmul(pks[:, :n], lhsT=wksd[:, kc, g, :],
                                                                              rhs=u[:, kc, :n], start=(kc == 0),
                                                                              stop=(kc == 7)),
                                 reads=["at_wksd", ("at_u", kc)], writes=[("pb", 2 + s)])
                        P.dve(lambda e, pk=pk, s=s, n=n: e.tensor_tensor(out=t1[:, s, :n], in0=pk[:, :n], in1=rp[:, 2, :n],
                                                                         op=ALU.mult),
                              reads=[("pb", s), "at_rp"], writes=[("at_t1", s)])
                        P.dve(lambda e, pks=pks, s=s, n=n: e.tensor_tensor(out=t2[:, s, :n], in0=pks[:, :n],
                                                                           in1=rp[:, 3, :n], op=ALU.mult),
                              reads=[("pb", 2 + s), "at_rp"], writes=[("at_t2", s)])
                        P.pool(lambda e, g=g, s=s, t0=t0, n=n: e.tensor_tensor(out=KT[:, g, t0:t0 + n], in0=t1[:, s, :n],
                                                                              in1=t2[:, s, :n], op=ALU.add),
                               reads=[("at_t1", s), ("at_t2", s)], writes=[("at_KT", t0 // 512 if col == 0 else "c")])
                    else:
                        P.act(lambda e, pk=pk, g=g, t0=t0, n=n: e.activation(out=KT[:, g, t0:t0 + n], in_=pk[:, :n],
                                                                             func=AF.Identity),
                              reads=[("pb", s)], writes=[("at_KT", "c")])
                for bi in range(n // 128):
                    pv = kb.pb[4 + (bi % 2)]
                    for kc in range(8):
                        P.pe(lambda e, kc=kc, bi=bi, pv=pv: e.matmul(pv[:, :256], lhsT=u[:, kc, bi * 128:(bi + 1) * 128],
                                                                     rhs=wv[:, kc, :], start=(kc == 0), stop=(kc == 7)),
                             reads=["at_wv", ("at_u", kc)], writes=[("pb", 4 + (bi % 2))])
                    blk = t0 // 128 + bi
                    P.act(lambda e, pv=pv, blk=blk: e.activation(out=Vt[:, blk, :], in_=pv[:, :256], func=AF.Identity),
                          reads=[("pb", 4 + (bi % 2))], writes=[("at_Vt", blk)])
      phase1()
      P.barrier()
      tiles = [(0, 256, 1)] + [(NCTX + 256 * i, 256, 0) for i in range(SEQ // 256)] if len(tiles) == 9 else tiles
      def phase2(tiles=tiles):
          with contextlib.ExitStack() as es:
            A = lambda name, shape, dt: es.enter_context(nc.sbuf_tensor(UN(name), shape, dt))
            NQ = 256
            wq = A("at_wq", [128, 8, 1024], BF16)
            wqs = A("at_wqs", [128, 8, 1024], BF16)
            wo = A("at_wo", [128, 8, 1024], BF16)
            wst = A("at_wst2", [128, 2, 1024], F32)
            load_weight_bf16(kb, wst, wq, "at_wq", wqkv_d[0], 8, 1024, 0)
            load_weight_bf16(kb, wst, wqs, "at_wqs", wsw_d, 8, 1024, 0)
            load_weight_bf16(kb, wst, wo, "at_wo", wo_d[0], 8, 1024, 0)
            ht = A("at_ht2", [128, 8, NQ], F32)
            u = A("at_u2", [128, 8, NQ], BF16)
            rp = A("at_rp2", [128, 4, NQ], F32)
            t1 = A("at_t12", [128, 2, NQ], F32)
            t2 = A("at_t22", [128, 2, NQ], F32)
            QT = A("at_QT", [128, 8, NQ], BF16)
            Sm = A("at_Sm", [128, 2, 384], F32)
            pbuf = A("at_p", [128, 2, 640], BF16)
            PT = A("at_PT", [128, 2, 5, 128], BF16)
            Osb = A("at_O", [128, 1024], BF16)
            OT = A("at_OT", [128, 8, NQ], BF16)
            st = A("at_st", [128, 2, 8], F32)
            lb = LNBufs(nc, es, NQ, "a")
            hcnt = 0
            for (t0, n, col) in tiles:
                s0, s1 = (0, NCTX) if col == 1 else (NCTX, LT)
                load_mod_tile(kb, ht, u, "at", h_in, h_in_res, li, 1, t0, n, col, s0, s1, 0)
                if col == 0:
                    P.dma(rp[:, 0:2, :n], rope_d[0:2, :, t0 - NCTX:t0 - NCTX + n].rearrange("a p t -> p a t"),
                          writes=["at_rp"])
                for c in range(8):
                    s = c % 2
                    pq, pqs = kb.pb[4 + s], kb.pb[6 + s]
                    for kc in range(8):
                        P.pe(lambda e, kc=kc, c=c, pq=pq, n=n: e.matmul(pq[:, :n], lhsT=wq[:, kc, c * 128:(c + 1) * 128],
                                                                        rhs=u[:, kc, :n], start=(kc == 0), stop=(kc == 7)),
                             reads=["at_wq", ("at_u", kc)], writes=[("pb", 4 + s)])
                    if col == 0:
                        for kc in range(8):
                            P.pe(lambda e, kc=kc, c=c, pqs=pqs, n=n: e.matmul(pqs[:, :n],
                                                                              lhsT=wqs[:, kc, c * 128:(c + 1) * 128],
                                                                              rhs=u[:, kc, :n], start=(kc == 0),
                                                                              stop=(kc == 7)),
                                 reads=["at_wqs", ("at_u", kc)], writes=[("pb", 6 + s)])
                        P.dve(lambda e, pq=pq, s=s, n=n: e.tensor_tensor(out=t1[:, s, :n], in0=pq[:, :n], in1=rp[:, 0, :n],
                                                                         op=ALU.mult),
                              reads=[("pb", 4 + s), "at_rp"], writes=[("at_t1", s)])
                        P.dve(lambda e, pqs=pqs, s=s, n=n: e.tensor_tensor(out=t2[:, s, :n], in0=pqs[:, :n],
                                                                           in1=rp[:, 1, :n], op=ALU.mult),
                              reads=[("pb", 6 + s), "at_rp"], writes=[("at_t2", s)])
                        P.pool(lambda e, c=c, s=s, n=n: e.tensor_tensor(out=QT[:, c, :n], in0=t1[:, s, :n],
                                                                       in1=t2[:, s, :n], op=ALU.add),
                               reads=[("at_t1", s), ("at_t2", s)], writes=[("at_QT", c)])
                    else:
                        P.act(lambda e, pq=pq, c=c, n=n: e.activation(out=QT[:, c, :n], in_=pq[:, :n], func=AF.Identity,
                                                                      scale=0.125),
                              reads=[("pb", 4 + s)], writes=[("at_QT", c)])
                for qb in range(n // 128):
                    if col == 0:
                        L = (t0 - NCTX) // 128 + qb
                        b0, b1 = max(L - 1, 0), min(L + 2, SEQ // 128)
                        kw0, kw1 = NCTX + 128 * b0, NCTX + 128 * b1
                        nw = kw1 - kw0
                        moff = 128 * (b0 - (L - 1))
                        kres = sorted(set(("at_KT", (kk - NCTX) // 512) for kk in (kw0, kw1 - 1))) + [("at_KT", "c")]
                        vblocks = list(range(kw0 // 128, kw1 // 128)) + [0, 1]
                    else:
                        nw = 0
                        kres = [("at_KT", "c")]
                        vblocks = [0, 1]
                    nk = nw + 256
                    for h in range(16):
                        c, base, g = h // 2, (h % 2) * 64, h // 4
                        s = hcnt % 2
                        hcnt += 1
                        pS, pS2, pTO = kb.pb[2 * s], kb.pb[2 * s + 1], kb.pb[4 + s]
                        q_ap = QT[base:base + 64, c, qb * 128:(qb + 1) * 128]
                        if nw:
                            P.pe(lambda e, pS=pS, q_ap=q_ap, base=base, g=g, kw0=kw0, kw1=kw1, nw=nw: e.matmul(
                                pS[:, :nw], lhsT=q_ap, rhs=KT[base:base + 64, g, kw0:kw1], start=True, stop=True),
                                reads=[("at_QT", c)] + kres, writes=[("pb", 2 * s)])
                        P.pe(lambda e, pS2=pS2, q_ap=q_ap, base=base, g=g: e.matmul(
                            pS2[:, :256], lhsT=q_ap, rhs=KT[base:base + 64, g, 0:256], start=True, stop=True),
                            reads=[("at_QT", c), ("at_KT", "c")], writes=[("pb", 2 * s + 1)])
                        m1, m2, mx, negm, rs1, rs2, esk, den = (st[:, s, i:i + 1] for i in range(8))
                        sres = ("at_st", s)
                        if nw:
                            P.dve(lambda e, pS=pS, s=s, nw=nw, moff=moff: e.tensor_tensor(
                                out=Sm[:, s, :nw], in0=pS[:, :nw], in1=maskb[:, moff:moff + nw], op=ALU.add),
                                reads=[("pb", 2 * s), "at_mask"], writes=[("at_Sm", s)])
                            P.dve(lambda e, s=s, nw=nw, m1=m1: e.reduce_max(out=m1, in_=Sm[:, s, :nw], axis=AX.X),
                                  reads=[("at_Sm", s)], writes=[sres])
                        P.dve(lambda e, pS2=pS2, m2=m2: e.reduce_max(out=m2, in_=pS2[:, :256], axis=AX.X),
                              reads=[("pb", 2 * s + 1)], writes=[sres])
                        sink_ap = kb.v("at_sink", h)
                        if nw:
                            P.dve(lambda e, m1=m1, m2=m2, mx=mx, sink_ap=sink_ap: e.tensor_scalar(
                                out=mx, in0=m1, scalar1=m2, scalar2=sink_ap, op0=ALU.max, op1=ALU.max),
                                reads=[sres, "vecs"], writes=[sres])
                        else:
                            P.dve(lambda e, m2=m2, mx=mx, sink_ap=sink_ap: e.tensor_scalar(
                                out=mx, in0=m2, scalar1=sink_ap, scalar2=None, op0=ALU.max),
                                reads=[sres, "vecs"], writes=[sres])
                        P.dve(lambda e, mx=mx, negm=negm: e.tensor_scalar(out=negm, in0=mx, scalar1=-1.0, scalar2=None,
                                                                          op0=ALU.mult),
                              reads=[sres], writes=[sres])
                        if nw:
                            P.act(lambda e, s=s, nw=nw, negm=negm, rs1=rs1: e.activation(
                                out=pbuf[:, s, :nw], in_=Sm[:, s, :nw], func=AF.Exp, bias=negm, accum_out=rs1),
                                reads=[("at_Sm", s), sres], writes=[("at_p", s), ("at_st2", s)])
                        P.act(lambda e, s=s, nw=nw, pS2=pS2, negm=negm, rs2=rs2: e.activation(
                            out=pbuf[:, s, nw:nw + 256], in_=pS2[:, :256], func=AF.Exp, bias=negm, accum_out=rs2),
                            reads=[("pb", 2 * s + 1), sres], writes=[("at_p", s), ("at_st2", s)])
                        P.act(lambda e, negm=negm, esk=esk, sink_ap=sink_ap: e.activation(
                            out=esk, in_=negm, func=AF.Exp, bias=sink_ap),
                            reads=[sres, "vecs"], writes=[("at_st2", s)])
                        if nw:
                            P.dve(lambda e, rs1=rs1, rs2=rs2, esk=esk, den=den: e.tensor_scalar(
                                out=den, in0=rs1, scalar1=rs2, scalar2=esk, op0=ALU.add, op1=ALU.add),
                                reads=[("at_st2", s)], writes=[("at_st3", s)])
                        else:
                            P.dve(lambda e, rs2=rs2, esk=esk, den=den: e.tensor_scalar(
                                out=den, in0=rs2, scalar1=esk, scalar2=None, op0=ALU.add),
                                reads=[("at_st2", s)], writes=[("at_st3", s)])
                        P.dve(lambda e, den=den: e.reciprocal(out=den, in_=den), reads=[("at_st3", s)],
                              writes=[("at_st3", s)])
                        nkb = nk // 128
                        ptp = pTO[:].bitcast(BF16)
                        for kk in range(nkb):
                            P.pe(lambda e, kk=kk, s=s, ptp=ptp: e.transpose(
                                out=ptp[:, kk * 128:(kk + 1) * 128], in_=pbuf[:, s, kk * 128:(kk + 1) * 128],
                                identity=kb.identb[:]),
                                reads=[("at_p", s), "identb"], writes=[("pb", 4 + s)])
                        P.dve(lambda e, s=s, nkb=nkb, ptp=ptp: e.tensor_copy(
                            out=PT[:, s, :nkb, :].rearrange("p a b -> p (a b)"), in_=ptp[:, :nkb * 128]),
                            reads=[("pb", 4 + s)], writes=[("at_PT", s)])
                        po = pTO[:, 384:448]
                        for kk in range(nkb):
                            vb = vblocks[kk]
                            P.pe(lambda e, kk=kk, s=s, po=po, vb=vb, g=g, nkb=nkb: e.matmul(
                                po, lhsT=PT[:, s, kk, :], rhs=Vt[:, vb, g * 64:(g + 1) * 64], start=(kk == 0),
                                stop=(kk == nkb - 1)),
                                reads=[("at_PT", s), ("at_Vt", vb)], writes=[("pb", 4 + s)])
                        P.act(lambda e, po=po, h=h, den=den: e.activation(out=Osb[:, h * 64:(h + 1) * 64], in_=po,
                                                                          func=AF.Identity, scale=den),
                              reads=[("pb", 4 + s), ("at_st3", s)], writes=["at_O"])
                    pot = kb.pb[6][:].bitcast(BF16)
                    for c in range(8):
                        P.pe(lambda e, c=c, pot=pot: e.transpose(out=pot[:, c * 128:(c + 1) * 128],
                                                                 in_=Osb[:, c * 128:(c + 1) * 128], identity=kb.identb[:]),
                             reads=["at_O", "identb"], writes=[("pb", 6)])
                    P.dve(lambda e, qb=qb, pot=pot: e.tensor_copy(
                        out=OT[:, :, qb * 128:(qb + 1) * 128], in_=pot[:].rearrange("p (c t) -> p c t", c=8)),
                        reads=[("pb", 6)], writes=["at_OT"])
                finals.extend(proj_ln(kb, lb, OT, lambda kc: "at_OT", 8, wo, "at_wo",
                                  lambda o, n=n: ht[:, o, :n], "at_ht", li, n, col,
                                  lambda t0=t0, n=n: hT_tile(h_out, t0, n), hres(h_out_res, t0, n)))
      phase2()
    kb.P.barrier()
    return finals

import math

NCH = LT // 128
E2 = 2048
DH = 512
ORDER = {0: list(range(NCH)), 1: [1, 0] + list(range(NCH - 1, 1, -1))}
PT256 = [(0, 256, 1)] + [(NCTX + 256 * i, 256, 0) for i in range(SEQ // 256)]


def host_bd(w_qkv_j):
    out = np.zeros((3, 128, 16, 128), np.float32)
    w = np.asarray(w_qkv_j, np.float32).reshape(3, 16, 32, 4, 4)
    for m in range(32):
        out[:, 4 * m:4 * m + 4, :, 4 * m:4 * m + 4] = np.transpose(w[:, :, m], (0, 2, 1, 3))
    return out


def host_wif(w_if_j):
    w = np.asarray(w_if_j, np.float32).reshape(2, 48, 128, 8)
    return np.ascontiguousarray(np.transpose(w, (2, 1, 0, 3)).reshape(128, 48, 16))


def mlstm_scratch(kb):
    if hasattr(kb, "ml_scr"):
        return kb.ml_scr
    s = {}
    s["qTb"] = kb.dscr("ml_qTb", [NCH, 128, 16, 128], BF16)
    s["kTb"] = kb.dscr("ml_kTb", [NCH, 128, 16, 128], BF16)
    s["xcTb"] = kb.dscr("ml_xcTb", [NCH, 128, 16, 128], BF16)
    s["ktok"] = kb.dscr("ml_ktok", [LT, E2], BF16)
    s["vtok"] = kb.dscr("ml_vtok", [LT, E2], BF16)
    s["sztok"] = kb.dscr("ml_sztok", [LT, E2], F32)
    s["gtok"] = kb.dscr("ml_gtok", [LT, 16], F32)
    s["hftok"] = kb.dscr("ml_hftok", [LT, E2], F32)
    kb.ml_scr = s
    return s


def mlstm_prep(kb, li, j, h_in, h_in_res, wup_d, bd_d, wif_d):
    nc, P = kb.nc, kb.P
    S = mlstm_scratch(kb)
    with contextlib.ExitStack() as es:
        A = lambda name, shape, dt: es.enter_context(nc.sbuf_tensor(UN(name), shape, dt))
        wst = A("mp_wst", [128, 2, 2048], F32)
        wup = A("mp_wup", [128, 8, 4096], BF16)
        load_weight_bf16(kb, wst, wup, "mp_wup", wup_d[j], 8, 4096)
        bd = A("mp_bd", [128, 3, 16, 128], BF16)
        for x in range(3):
            s = kb.uid % 2
            kb.uid += 1
            P.dma(wst[:, s, :], bd_d[j, x].rearrange("p e o -> p (e o)"), writes=[("wstage", s)])
            P.pool(lambda e, x=x, s=s: e.tensor_copy(out=bd[:, x].rearrange("p e o -> p (e o)"), in_=wst[:, s, :]),
                   reads=[("wstage", s)], writes=["mp_bd"])
        wif = A("mp_wif", [128, 48, 16], BF16)
        s = kb.uid % 2
        kb.uid += 1
        P.dma(wst[:, s, :768], wif_d[j].rearrange("p r g -> p (r g)"), writes=[("wstage", s)])
        P.pool(lambda e, s=s: e.tensor_copy(out=wif[:].rearrange("p r g -> p (r g)"), in_=wst[:, s, :768]),
               reads=[("wstage", s)], writes=["mp_wif"])
        n = 256
        ht = A("mp_ht", [128, 8, n + 2], F32)
        u = A("mp_u", [128, 8, n + 2], BF16)
        xms = A("mp_xm", [128, 2, n + 2], F32)
        cv = A("mp_cv", [128, 2, n], F32)
        xmb = A("mp_xmb", [128, 16, n], BF16)
        xc = A("mp_xc", [128, 2, 16, 128], BF16)
        qTs = A("mp_qT", [128, 2, 16, 128], BF16)
        kTs = A("mp_kT", [128, 2, 16, 128], BF16)
        vTs = A("mp_vT", [128, 2, n], BF16)
        kst = A("mp_kst", [128, 2, E2], BF16)
        szs = A("mp_szs", [128, 2, E2], F32)
        gst = A("mp_gst", [128, 2, 16], F32)
        cnt = 0
        for (t0, n_, col) in PT256:
            s0, s1 = (0, NCTX) if col == 1 else (NCTX, LT)
            load_mod_tile(kb, ht, u, "mp", h_in, h_in_res, li, 1, t0, n, col, s0, s1, 1)
            c0 = t0 // 128
            for e_ in range(16):
                s = e_ % 2
                px = kb.pb[s]
                for kc in range(8):
                    P.pe(lambda e, kc=kc, e_=e_, px=px: e.matmul(px[:, :n + 2], lhsT=wup[:, kc, e_ * 128:(e_ + 1) * 128],
                                                                  rhs=u[:, kc, :], start=(kc == 0), stop=(kc == 7)),
                         reads=["mp_wup", ("mp_u", kc)], writes=[("pb", s)])
                P.act(lambda e, px=px, s=s: e.activation(out=xms[:, s, :], in_=px[:, :n + 2], func=AF.Identity),
                      reads=[("pb", s)], writes=[("mp_xm", s)])
                P.pool(lambda e, s=s, e_=e_: e.tensor_copy(out=xmb[:, e_, :], in_=xms[:, s, 1:n + 1]),
                       reads=[("mp_xm", s)], writes=[("mp_xmb", e_)])
                w0, w1, w2 = (kb.v("ml_conv_w", (j * 3 + k_) * 16 + e_) for k_ in range(3))
                bcv = kb.v("ml_conv_b", j * 16 + e_)
                P.dve(lambda e, s=s, w1=w1, bcv=bcv: e.tensor_scalar(out=cv[:, s, :], in0=xms[:, s, 1:n + 1], scalar1=w1,
                                                                     scalar2=bcv, op0=ALU.mult, op1=ALU.add),
                      reads=[("mp_xm", s), "vecs"], writes=[("mp_cv", s)])
                P.dve(lambda e, s=s, w0=w0: e.scalar_tensor_tensor(out=cv[:, s, :], in0=xms[:, s, 0:n], scalar=w0,
                                                                   in1=cv[:, s, :], op0=ALU.mult, op1=ALU.add),
                      reads=[("mp_xm", s), ("mp_cv", s)], writes=[("mp_cv", s)])
                P.dve(lambda e, s=s, w2=w2: e.scalar_tensor_tensor(out=cv[:, s, :], in0=xms[:, s, 2:n + 2], scalar=w2,
                                                                   in1=cv[:, s, :], op0=ALU.mult, op1=ALU.add),
                      reads=[("mp_xm", s), ("mp_cv", s)], writes=[("mp_cv", s)])
                P.act(lambda e, s=s, e_=e_: e.activation(out=xc[:, :, e_, :],
                                                         in_=cv[:, s, :].rearrange("p (c t) -> p c t", c=2),
                                                         func=AF.Silu),
                      reads=[("mp_cv", s)], writes=[("mp_xc", e_)])
                for x, (dst, dres) in enumerate(((qTs, "mp_qT"), (kTs, "mp_kT"), (vTs, "mp_vT"))):
                    pq = kb.pb[2 + (cnt % 2)]
                    pqi = 2 + (cnt % 2)
                    cnt += 1
                    if x < 2:
                        P.pe(lambda e, x=x, e_=e_, pq=pq: e.matmul(pq[:, :n], lhsT=bd[:, x, e_, :], rhs=xc[:, :, e_, :],
                                                                   start=True, stop=True),
                             reads=["mp_bd", ("mp_xc", e_)], writes=[("pb", pqi)])
                        P.act(lambda e, dst=dst, e_=e_, pq=pq: e.activation(
                            out=dst[:, :, e_, :], in_=pq[:, :n].rearrange("p (c t) -> p c t", c=2), func=AF.Identity),
                            reads=[("pb", pqi)], writes=[(dres, e_)])
                        src_fn = lambda cb, dst=dst, e_=e_: dst[:, cb, e_, :]
                        src_res = (dres, e_)
                    else:
                        P.pe(lambda e, e_=e_, pq=pq: e.matmul(pq[:, :n], lhsT=bd[:, 2, e_, :], rhs=xmb[:, e_, :],
                                                              start=True, stop=True),
                             reads=["mp_bd", ("mp_xmb", e_)], writes=[("pb", pqi)])
                        vs = e_ % 2
                        P.act(lambda e, vs=vs, pq=pq: e.activation(out=vTs[:, vs, :], in_=pq[:, :n], func=AF.Identity),
                              reads=[("pb", pqi)], writes=[("mp_vT", vs)])
                        src_fn = lambda cb, vs=vs: vTs[:, vs, cb * 128:(cb + 1) * 128]
                        src_res = ("mp_vT", vs)
                    for cb in range(2):
                        P.pe(lambda e, cb=cb, x=x, e_=e_, src_fn=src_fn: e.matmul(
                            kb.pb[6 + cb][:, :16], lhsT=src_fn(cb), rhs=wif[:, x * 16 + e_, :],
                            start=(x == 0 and e_ == 0), stop=(x == 2 and e_ == 15)),
                            reads=[src_res, "mp_wif"], writes=[("pb", 6 + cb)])
            for cb in range(2):
                P.dve(lambda e, cb=cb: e.tensor_tensor(out=gst[:, cb, :], in0=kb.pb[6 + cb][:, :16],
                                                       in1=kb.v("ml_b_if", j * 16, 16), op=ALU.add),
                      reads=[("pb", 6 + cb), "vecs"], writes=[("mp_gst", cb)])
                P.dma(S["gtok"][t0 + cb * 128:t0 + (cb + 1) * 128, :], gst[:, cb, :], reads=[("mp_gst", cb)],
                      writes=[("gtok", c0 + cb)], q="act")
            for cb in range(2):
                for (x, dname) in ((1, "ktok"), (2, "vtok")):
                    ks = x - 1
                    for hh in range(4):
                        pk = kb.pb[4 + (hh % 2)]
                        for jj in range(4):
                            e_ = 4 * hh + jj
                            lhs = xc[:, cb, e_, :] if x == 1 else xmb[:, e_, cb * 128:(cb + 1) * 128]
                            lres = ("mp_xc", e_) if x == 1 else ("mp_xmb", e_)
                            P.pe(lambda e, pk=pk, jj=jj, lhs=lhs, x=x, e_=e_: e.matmul(
                                pk[:, jj * 128:(jj + 1) * 128], lhsT=lhs, rhs=bd[:, x, e_, :], start=True, stop=True),
                                reads=[lres, "mp_bd"], writes=[("pb", 4 + (hh % 2))])
                        P.act(lambda e, pk=pk, ks=ks, hh=hh: e.activation(out=kst[:, ks, hh * 512:(hh + 1) * 512],
                                                                          in_=pk[:, :512], func=AF.Identity),
                              reads=[("pb", 4 + (hh % 2))], writes=[("mp_kst", ks)])
                    P.dma(S[dname][t0 + cb * 128:t0 + (cb + 1) * 128, :], kst[:, ks, :], reads=[("mp_kst", ks)],
                          writes=[(dname, c0 + cb)], q="act")
                for hh in range(4):
                    pz = kb.pb[hh % 2]
                    for kc in range(8):
                        P.pe(lambda e, pz=pz, kc=kc, cb=cb, hh=hh: e.matmul(
                            pz[:, :512], lhsT=u[:, kc, 1 + cb * 128:1 + (cb + 1) * 128],
                            rhs=wup[:, kc, E2 + hh * 512:E2 + (hh + 1) * 512], start=(kc == 0), stop=(kc == 7)),
                            reads=[("mp_u", kc), "mp_wup"], writes=[("pb", hh % 2)])
                    P.act(lambda e, pz=pz, cb=cb, hh=hh: e.activation(out=szs[:, cb, hh * 512:(hh + 1) * 512],
                                                                      in_=pz[:, :512], func=AF.Sigmoid),
                          reads=[("pb", hh % 2)], writes=[("mp_szs", cb)])
                P.dma(S["sztok"][t0 + cb * 128:t0 + (cb + 1) * 128, :], szs[:, cb, :], reads=[("mp_szs", cb)],
                      writes=[("sztok", c0 + cb)], q="act")
            for (src, dname, rname) in ((qTs, "qTb", "mp_qT"), (kTs, "kTb", "mp_kT"), (xc, "xcTb", "mp_xc")):
                P.dma(S[dname][c0:c0 + 2].rearrange("c p e t -> p c e t"), src[:],
                      reads=[(rname, e_) for e_ in range(16)], writes=[(dname, c0), (dname, c0 + 1)], q="act")
    P.barrier()


def mlstm_gates(kb, es, tag):
    nc, P = kb.nc, kb.P
    S = mlstm_scratch(kb)
    A = lambda name, shape, dt: es.enter_context(nc.sbuf_tensor(UN(name), shape, dt))
    tri = A("mg_tri", [128, 2, 128], F32)
    pers = {}
    for d_ in range(2):
        pers[d_] = {nm: A("mg_%s%d" % (nm, d_), [128, NCH, 4], F32) for nm in ("w", "wa", "e", "am")}
    es_t = contextlib.ExitStack()
    At = lambda name, shape, dt: es_t.enter_context(nc.sbuf_tensor(UN(name), shape, dt))
    G = At("mg_G", [128, NCH, 16], F32)
    P.dma(G[:], S["gtok"].rearrange("(c p) g -> p c g", p=128), reads=[("gtok", c) for c in range(NCH)], writes=["mg_G"])
    ones = At("mg_ones", [128, 128], F32)
    P.pool(lambda e: e.memset(ones[:], 1.0), writes=["mg_ones"])
    P.pool(lambda e: e.memset(tri[:], 1.0), writes=["mg_tri"])
    P.pool(lambda e: e.affine_select(out=tri[:, 0, :], in_=tri[:, 0, :], pattern=[[1, 128]], compare_op=ALU.is_ge,
                                     fill=0.0, base=0, channel_multiplier=-1), reads=["mg_tri"], writes=["mg_tri"])
    P.pool(lambda e: e.affine_select(out=tri[:, 1, :], in_=tri[:, 1, :], pattern=[[-1, 128]], compare_op=ALU.is_ge,
                                     fill=0.0, base=0, channel_multiplier=1), reads=["mg_tri"], writes=["mg_tri"])
    NC4 = NCH * 4
    out = {"tri": tri}
    tmp = At("mg_tmp", [128, 6, NCH, 4], F32)
    X = At("mg_X", [128, NCH, 4], F32)
    dg = At("mg_dg", [68, 68], F32)
    xm = At("mg_xm", [68, 1], F32)
    for d in range(2):
        IG = G[:, :, d * 8:d * 8 + 4]
        FG = G[:, :, d * 8 + 4:d * 8 + 8]
        ab, ex, lf, bsb, gsb, x = (tmp[:, i] for i in range(6))
        R = lambda i: ("mg_tmp", i)
        P.act(lambda e, FG=FG, ab=ab: e.activation(out=ab, in_=FG, func=AF.Abs),
              reads=["mg_G"], writes=[R(0)])
        P.act(lambda e, ab=ab, ex=ex: e.activation(out=ex, in_=ab, func=AF.Exp, scale=-1.0), reads=[R(0)], writes=[R(1)])
        P.act(lambda e, ex=ex: e.activation(out=ex, in_=ex, func=AF.Ln, bias=kb.epsc[:, 3:4]), reads=[R(1), "epsc"],
              writes=[R(1)])
        P.dve(lambda e, FG=FG, ab=ab: e.tensor_single_scalar(out=ab, in_=FG, scalar=0.0, op=ALU.min),
              reads=["mg_G", R(0)], writes=[R(0)])
        P.dve(lambda e, ab=ab, ex=ex, lf=lf: e.tensor_tensor(out=lf, in0=ab, in1=ex, op=ALU.subtract),
              reads=[R(0), R(1)], writes=[R(2)])
        lf2 = lf.rearrange("p c h -> p (c h)")
        P.pe(lambda e, d=d, lf2=lf2: e.matmul(kb.pb[0][:, :NC4], lhsT=tri[:, d, :], rhs=lf2, start=True, stop=True),
             reads=["mg_tri", R(2)], writes=[("pb", 0)])
        P.pe(lambda e, lf2=lf2: e.matmul(kb.pb[1][:, :NC4], lhsT=ones[:], rhs=lf2, start=True, stop=True),
             reads=["mg_ones", R(2)], writes=[("pb", 1)])
        P.act(lambda e, bsb=bsb: e.activation(out=bsb.rearrange("p c h -> p (c h)"), in_=kb.pb[0][:, :NC4],
                                              func=AF.Identity), reads=[("pb", 0)], writes=[R(3)])
        P.act(lambda e, gsb=gsb: e.activation(out=gsb.rearrange("p c h -> p (c h)"), in_=kb.pb[1][:, :NC4],
                                              func=AF.Identity), reads=[("pb", 1)], writes=[R(4)])
        P.dve(lambda e, IG=IG, bsb=bsb, x=x: e.tensor_tensor(out=x, in0=IG, in1=bsb, op=ALU.subtract),
              reads=["mg_G", R(3)], writes=[R(5)])
        x2 = x.rearrange("p c h -> p (c h)")
        X2 = X[:].rearrange("p c h -> p (c h)")
        for half in range(2):
            pt = kb.pb[2 + half]
            P.pe(lambda e, half=half, pt=pt, x2=x2: e.transpose(out=pt[0:68, 0:128], in_=x2[:, half * 68:(half + 1) * 68],
                                                                identity=kb.ident[:]),
                 reads=[R(5), "ident"], writes=[("pb", 2 + half)])
            P.dve(lambda e, pt=pt: e.reduce_max(out=xm[:], in_=pt[0:68, 0:128], axis=AX.X), reads=[("pb", 2 + half)],
                  writes=["mg_xm"])
            P.dve(lambda e: e.tensor_scalar(out=dg[:], in0=kb.ident[0:68, 0:68], scalar1=xm[:, 0:1], scalar2=None,
                                            op0=ALU.mult), reads=["mg_xm", "ident"], writes=["mg_dg"])
            pr = kb.pb[4 + half]
            P.pe(lambda e, pr=pr: e.matmul(pr[:, 0:68], lhsT=ones[0:68, :], rhs=dg[:], start=True, stop=True),
                 reads=["mg_ones", "mg_dg"], writes=[("pb", 4 + half)])
            P.act(lambda e, half=half, pr=pr, X2=X2: e.activation(out=X2[:, half * 68:(half + 1) * 68], in_=pr[:, 0:68],
                                                                  func=AF.Identity),
                  reads=[("pb", 4 + half)], writes=["mg_X"])
        M = At("mg_M%d" % d, [128, NCH, 4], F32)
        mseq = At("mg_ms%d" % d, [128, NCH + 1, 4], F32)
        am = pers[d]["am"]
        P.dve(lambda e, mseq=mseq: e.memset(mseq[:, 0, :], 0.0), writes=["mg_mseq"])
        for k, c in enumerate(ORDER[d]):
            P.dve(lambda e, k=k, c=c, M=M, mseq=mseq: e.tensor_tensor(out=M[:, c, :], in0=mseq[:, k, :], in1=X[:, c, :],
                                                                      op=ALU.max),
                  reads=["mg_mseq", "mg_X"], writes=["mg_M"])
            P.dve(lambda e, k=k, c=c, M=M, mseq=mseq, gsb=gsb: e.tensor_tensor(out=mseq[:, k + 1, :], in0=M[:, c, :],
                                                                               in1=gsb[:, c, :], op=ALU.add),
                  reads=["mg_M", R(4)], writes=["mg_mseq"])
            P.dve(lambda e, k=k, c=c, M=M, mseq=mseq, am=am: e.tensor_tensor(out=am[:, c, :], in0=mseq[:, k, :],
                                                                             in1=M[:, c, :], op=ALU.subtract),
                  reads=["mg_M", "mg_mseq"], writes=["mg_am"])
        w, wa, ee = pers[d]["w"], pers[d]["wa"], pers[d]["e"]
        P.dve(lambda e, x=x, M=M: e.tensor_tensor(out=x, in0=x, in1=M[:], op=ALU.subtract), reads=[R(5), "mg_M"],
              writes=[R(5)])
        P.act(lambda e, x=x, w=w: e.activation(out=w[:], in_=x, func=AF.Exp, bias=kb.epsc[:, 2:3]),
              reads=[R(5), "epsc"], writes=["mg_w"])
        P.dve(lambda e, bsb=bsb, M=M: e.tensor_tensor(out=bsb, in0=bsb, in1=M[:], op=ALU.add), reads=[R(3), "mg_M"],
              writes=[R(3)])
        P.act(lambda e, bsb=bsb, ee=ee: e.activation(out=ee[:], in_=bsb, func=AF.Exp, scale=-1.0), reads=[R(3)],
              writes=["mg_e"])
        P.act(lambda e, am=am: e.activation(out=am[:], in_=am[:], func=AF.Exp), reads=["mg_am"], writes=["mg_am"])
        P.dve(lambda e, w=w, wa=wa: e.tensor_copy(out=wa[:], in_=w[:]), reads=["mg_w"], writes=["mg_wa"])
        if d == 0:
            P.dve(lambda e, w=w, wa=wa, am=am: e.tensor_tensor(out=wa[:, 0:NCH - 1, :], in0=w[:, 0:NCH - 1, :],
                                                               in1=am[:, 1:NCH, :], op=ALU.mult),
                  reads=["mg_w", "mg_am", "mg_wa"], writes=["mg_wa"])
        else:
            P.dve(lambda e, w=w, wa=wa, am=am: e.tensor_tensor(out=wa[:, 3:NCH, :], in0=w[:, 3:NCH, :],
                                                               in1=am[:, 2:NCH - 1, :], op=ALU.mult),
                  reads=["mg_w", "mg_am", "mg_wa"], writes=["mg_wa"])
            P.dve(lambda e, w=w, wa=wa, am=am: e.tensor_tensor(out=wa[:, 1, :], in0=w[:, 1, :], in1=am[:, 0, :],
                                                               op=ALU.mult),
                  reads=["mg_w", "mg_am", "mg_wa"], writes=["mg_wa"])
            P.dve(lambda e, w=w, wa=wa, am=am: e.tensor_tensor(out=wa[:, 0, :], in0=w[:, 0, :], in1=am[:, NCH - 1, :],
                                                               op=ALU.mult),
                  reads=["mg_w", "mg_am", "mg_wa"], writes=["mg_wa"])
        out[d] = {"w": w, "wa": wa, "e": ee, "a": am}
    P.barrier()
    es_t.close()
    return out


def mlstm_scan(kb, li, j, last, gates, h_in, h_in_res, h_out, h_out_res, wdown_d, ngbc_d):
    nc, P = kb.nc, kb.P
    S = mlstm_scratch(kb)
    finals = []
    tri = gates["tri"]
    for d in range(2):
        gd = gates[d]
        order = ORDER[d]

        def run_dir(d=d, gd=gd, order=order):
            with contextlib.ExitStack() as es:
                A = lambda name, shape, dt: es.enter_context(nc.sbuf_tensor(UN(name), shape, dt))
                T = "ms%d_" % d
                Cs = A(T + "Cs", [128, 4, 4, 512], F32)
                Cb = A(T + "Cb", [128, 4, 4, 512], BF16)
                ns = A(T + "ns", [128, 4, 4], F32)
                nb = A(T + "nb", [128, 4, 4], BF16)
                onesb = A(T + "onesb", [128, 2], BF16)
                qT = A(T + "qT", [128, 2, 16, 128], BF16)
                kT = A(T + "kT", [128, 2, 16, 128], BF16)
                ktk = A(T + "ktk", [128, E2], BF16)
                vtk = A(T + "vtk", [128, E2], BF16)
                AT = A(T + "AT", [128, 4, 128], BF16)
                KW = A(T + "KW", [128, 2, 512], BF16)
                hst = A(T + "hst", [128, E2], F32)
                rr = A(T + "rr", [128, 4], F32)
                P.pool(lambda e: e.memset(Cs[:], 0.0), writes=[("Cs", h) for h in range(4)])
                P.pool(lambda e: e.memset(Cb[:], 0.0), writes=[("Cb", h) for h in range(4)])
                P.pool(lambda e: e.memset(ns[:], 0.0), writes=["ns"])
                P.pool(lambda e: e.memset(nb[:], 0.0), writes=["nb"])
                P.pool(lambda e: e.memset(onesb[:], 1.0), writes=["onesb"])
                if d == 1:
                    wst = A(T + "wst", [128, 2, 512], F32)
                    wdn = A(T + "wdn", [128, 16, 1024], BF16)
                    load_weight_bf16(kb, wst, wdn, "wdn", wdown_d[j], 16, 1024)
                    ngbc = A(T + "ngbc", [128, E2], F32)
                    P.dma(ngbc[:], ngbc_d[j], writes=["ngbc"])
                    hf = A(T + "hf", [128, E2], F32)
                    sz = A(T + "sz", [128, E2], F32)
                    xcT = A(T + "xcT", [128, 16, 128], BF16)
                    hnb = A(T + "hnb", [128, E2], BF16)
                    xin = A(T + "xin", [128, 16, 128], BF16)
                    hrs = A(T + "hrs", [128, 8, 128], F32)
                    bst = A(T + "bst", [128, 4, 8], F32)
                    lb = LNBufs(nc, es, 128, "m")
                for k, c in enumerate(order):
                    is_last = (k == len(order) - 1)
                    cn = order[k + 1] if not is_last else None
                    col = 1 if c < 2 else 0
                    need_h = not (last and col == 1)
                    qs = k % 2
                    P.dma(qT[:, qs], S["qTb"][c], reads=[("qTb", c)], writes=[("qT", qs)])
                    P.dma(kT[:, qs], S["kTb"][c], reads=[("kTb", c)], writes=[("kT", qs)])
                    P.dma(ktk[:], S["ktok"][c * 128:(c + 1) * 128, :], reads=[("ktok", c)], writes=["ktk"])
                    P.dma(vtk[:], S["vtok"][c * 128:(c + 1) * 128, :], reads=[("vtok", c)], writes=["vtk"])
                    if need_h:
                        for hh in range(4):
                            for jj in range(4):
                                P.pe(lambda e, hh=hh, jj=jj, qs=qs: e.matmul(
                                    kb.pb[0][:, hh * 128:(hh + 1) * 128], lhsT=kT[:, qs, 4 * hh + jj, :],
                                    rhs=qT[:, qs, 4 * hh + jj, :], start=(jj == 0), stop=(jj == 3)),
                                    reads=[("kT", qs), ("qT", qs)], writes=[("pb", 0)])
                        for hh in range(4):
                            P.dve(lambda e, hh=hh, c=c: e.scalar_tensor_tensor(
                                out=AT[:, hh, :], in0=kb.pb[0][:, hh * 128:(hh + 1) * 128], scalar=gd["w"][:, c, hh:hh + 1],
                                in1=tri[:, d, :], op0=ALU.mult, op1=ALU.mult),
                                reads=[("pb", 0), "mg_w", "mg_tri"], writes=[("AT", hh)])
                    for hh in range(4):
                        if need_h:
                            pn = kb.pb[2 + (hh % 2)]
                            P.pe(lambda e, hh=hh, pn=pn: e.matmul(pn[:, :512], lhsT=AT[:, hh, :],
                                                                  rhs=vtk[:, hh * 512:(hh + 1) * 512], start=True,
                                                                  stop=False),
                                 reads=[("AT", hh), "vtk"], writes=[("pb", 2 + (hh % 2))])
                            for jj in range(4):
                                P.pe(lambda e, hh=hh, jj=jj, pn=pn, qs=qs: e.matmul(
                                    pn[:, :512], lhsT=qT[:, qs, 4 * hh + jj, :], rhs=Cb[:, hh, jj, :], start=False,
                                    stop=(jj == 3)),
                                    reads=[("qT", qs), ("Cb", hh)], writes=[("pb", 2 + (hh % 2))])
                            pd = kb.pb[1][:, hh:hh + 1]
                            P.pe(lambda e, hh=hh, pd=pd: e.matmul(pd, lhsT=AT[:, hh, :], rhs=onesb[:, 0:1], start=True,
                                                                  stop=False),
                                 reads=[("AT", hh), "onesb"], writes=[("pb", 1)])
                            for jj in range(4):
                                P.pe(lambda e, hh=hh, jj=jj, pd=pd, qs=qs: e.matmul(
                                    pd, lhsT=qT[:, qs, 4 * hh + jj, :], rhs=nb[:, hh, jj:jj + 1], start=False,
                                    stop=(jj == 3)),
                                    reads=[("qT", qs), "nb"], writes=[("pb", 1)])
                            P.act(lambda e, hh=hh, pd=pd: e.activation(out=rr[:, hh:hh + 1], in_=pd, func=AF.Abs),
                                  reads=[("pb", 1)], writes=[("rr", hh)])
                            P.dve(lambda e, hh=hh, c=c: e.tensor_tensor(out=rr[:, hh:hh + 1], in0=rr[:, hh:hh + 1],
                                                                        in1=gd["e"][:, c, hh:hh + 1], op=ALU.max),
                                  reads=[("rr", hh), "mg_e"], writes=[("rr", hh)])
                            P.dve(lambda e, hh=hh: e.reciprocal(out=rr[:, hh:hh + 1], in_=rr[:, hh:hh + 1]),
                                  reads=[("rr", hh)], writes=[("rr", hh)])
                            P.act(lambda e, hh=hh, pn=pn: e.activation(out=hst[:, hh * 512:(hh + 1) * 512], in_=pn[:, :512],
                                                                       func=AF.Identity, scale=rr[:, hh:hh + 1]),
                                  reads=[("pb", 2 + (hh % 2)), ("rr", hh)], writes=[("hst", hh)])
                        if not is_last:
                            ks = hh % 2
                            P.act(lambda e, hh=hh, ks=ks, c=c: e.activation(out=KW[:, ks, :],
                                                                            in_=ktk[:, hh * 512:(hh + 1) * 512],
                                                                            func=AF.Identity,
                                                                            scale=gd["wa"][:, c, hh:hh + 1]),
                                  reads=["ktk", "mg_wa"], writes=[("KW", ks)])
                            a_ap = gd["a"][:, cn, hh:hh + 1]
                            for jj in range(4):
                                P.pe(lambda e, hh=hh, jj=jj, ks=ks: e.matmul(
                                    kb.pb[4 + jj][:, :512], lhsT=KW[:, ks, jj * 128:(jj + 1) * 128],
                                    rhs=vtk[:, hh * 512:(hh + 1) * 512], start=True, stop=True),
                                    reads=[("KW", ks), "vtk"], writes=[("pb", 4 + jj)])
                                P.dve(lambda e, hh=hh, jj=jj, a_ap=a_ap: e.scalar_tensor_tensor(
                                    out=Cs[:, hh, jj, :], in0=Cs[:, hh, jj, :], scalar=a_ap, in1=kb.pb[4 + jj][:, :512],
                                    op0=ALU.mult, op1=ALU.add),
                                    reads=[("Cs", hh), ("pb", 4 + jj), "mg_am"], writes=[("Cs", hh)])
                            P.act(lambda e, hh=hh: e.activation(out=Cb[:, hh], in_=Cs[:, hh], func=AF.Identity),
                                  reads=[("Cs", hh)], writes=[("Cb", hh)])
                            pnu = kb.pb[1][:, 16 + hh * 4:16 + hh * 4 + 4]
                            for jj in range(4):
                                P.pe(lambda e, hh=hh, jj=jj, ks=ks: e.matmul(
                                    kb.pb[1][:, 16 + hh * 4 + jj:16 + hh * 4 + jj + 1],
                                    lhsT=KW[:, ks, jj * 128:(jj + 1) * 128], rhs=onesb[:, 0:1], start=True, stop=True),
                                    reads=[("KW", ks), "onesb"], writes=[("pb", 1)])
                            P.dve(lambda e, hh=hh, pnu=pnu, a_ap=a_ap: e.scalar_tensor_tensor(
                                out=ns[:, hh, :], in0=ns[:, hh, :], scalar=a_ap, in1=pnu, op0=ALU.mult, op1=ALU.add),
                                reads=["ns", ("pb", 1), "mg_am"], writes=["ns"])
                            P.dve(lambda e, hh=hh: e.tensor_copy(out=nb[:, hh, :], in_=ns[:, hh, :]), reads=["ns"],
                                  writes=["nb"])
                    if not need_h:
                        continue
                    hst_res = [("hst", hh) for hh in range(4)]
                    if d == 0:
                        P.dma(S["hftok"][c * 128:(c + 1) * 128, :], hst[:], reads=hst_res, writes=[("hftok", c)],
                              q="act")
                        continue
                    P.dma(hf[:], S["hftok"][c * 128:(c + 1) * 128, :], reads=[("hftok", c)], writes=["hf"])
                    P.dma(sz[:], S["sztok"][c * 128:(c + 1) * 128, :], reads=[("sztok", c)], writes=["sz"])
                    P.dma(xcT[:], S["xcTb"][c], reads=[("xcTb", c)], writes=["xcT"])
                    t0 = c * 128
                    P.dma(hrs[:], hT_tile(h_in, t0, 128), reads=hres(h_in_res, t0, 128), writes=["hrs"])
                    P.pool(lambda e: e.tensor_tensor(out=hf[:], in0=hf[:], in1=hst[:], op=ALU.add),
                           reads=["hf"] + hst_res, writes=["hf"])
                    for hh in range(4):
                        hsl = slice(hh * 512, (hh + 1) * 512)
                        P.dve(lambda e, hsl=hsl: e.tensor_tensor(out=hf[:, hsl], in0=hf[:, hsl], in1=sz[:, hsl],
                                                                 op=ALU.mult),
                              reads=["hf", "sz"], writes=["hf"])
                        P.dve(lambda e, hh=hh, hsl=hsl: e.bn_stats(out=bst[:, hh, 0:6], in_=hf[:, hsl]), reads=["hf"],
                              writes=[("bst", hh)])
                        P.dve(lambda e, hh=hh: e.bn_aggr(out=bst[:, hh, 6:8], in_=bst[:, hh, 0:6]), reads=[("bst", hh)],
                              writes=[("bst", hh)])
                        P.act(lambda e, hh=hh: e.activation(out=bst[:, hh, 7:8], in_=bst[:, hh, 7:8], func=AF.Sqrt,
                                                            bias=kb.epsc[:, 1:2]),
                              reads=[("bst", hh), "epsc"], writes=[("bst", hh)])
                        P.dve(lambda e, hh=hh: e.reciprocal(out=bst[:, hh, 7:8], in_=bst[:, hh, 7:8]),
                              reads=[("bst", hh)], writes=[("bst", hh)])
                        P.dve(lambda e, hh=hh, hsl=hsl: e.tensor_scalar(out=hf[:, hsl], in0=hf[:, hsl],
                                                                        scalar1=bst[:, hh, 6:7], scalar2=bst[:, hh, 7:8],
                                                                        op0=ALU.subtract, op1=ALU.mult),
                              reads=["hf", ("bst", hh)], writes=["hf"])
                        P.pool(lambda e, hsl=hsl: e.tensor_tensor(out=hnb[:, hsl], in0=hf[:, hsl], in1=ngbc[:, hsl],
                                                                  op=ALU.mult),
                               reads=["hf", "ngbc"], writes=["hnb"])
                    for half in range(2):
                        ptv = kb.pb[2 + half][:].bitcast(BF16)
                        for e8 in range(8):
                            e_ = half * 8 + e8
                            P.pe(lambda e, e_=e_, e8=e8, ptv=ptv: e.transpose(out=ptv[:, e8 * 128:(e8 + 1) * 128],
                                                                              in_=hnb[:, e_ * 128:(e_ + 1) * 128],
                                                                              identity=kb.identb[:]),
                                 reads=["hnb", "identb"], writes=[("pb", 2 + half)])
                        for e8 in range(8):
                            e_ = half * 8 + e8
                            P.dve(lambda e, e_=e_, e8=e8, ptv=ptv: e.scalar_tensor_tensor(
                                out=xin[:, e_, :], in0=xcT[:, e_, :], scalar=kb.v("ml_skip", j * 16 + e_),
                                in1=ptv[:, e8 * 128:(e8 + 1) * 128], op0=ALU.mult, op1=ALU.add),
                                reads=["xcT", ("pb", 2 + half), "vecs"], writes=[("xin", e_)])
                    finals.extend(proj_ln(kb, lb, xin, lambda kc: ("xin", kc), 16, wdn, "wdn",
                                          lambda o: hrs[:, o, :], "hrs", li, 128, col,
                                          lambda t0=t0: hT_tile(h_out, t0, 128), hres(h_out_res, t0, 128)))
            P.barrier()

        run_dir()
    return finals


def mlstm_mixer(kb, li, j, last, h_in, h_in_res, h_out, h_out_res, wup_d, bd_d, wif_d, wdown_d, ngbc_d):
    mlstm_prep(kb, li, j, h_in, h_in_res, wup_d, bd_d, wif_d)
    with contextlib.ExitStack() as es:
        gates = mlstm_gates(kb, es, "g%d" % li)
        finals = mlstm_scan(kb, li, j, last, gates, h_in, h_in_res, h_out, h_out_res, wdown_d, ngbc_d)
    kb.P.barrier()
    return finals


def rope_tables_host():
    axis_dim = 32
    freqs = (10000.0 ** (-np.arange(0, axis_dim, 2, dtype=np.float32) / axis_dim)).astype(np.float32)
    t = np.arange(SEQ)
    rows = (t // 64).astype(np.float32)
    cols = (t % 64).astype(np.float32)
    ang = np.concatenate([rows[:, None] * freqs, cols[:, None] * freqs], axis=-1).astype(np.float32)
    cos = np.cos(ang).astype(np.float32)
    sin = np.sin(ang).astype(np.float32)
    p = np.arange(128)
    i = (p % 64) // 2
    par = p % 2
    c = cos[:, i].T
    s = sin[:, i].T * np.where(par == 0, -1.0, 1.0)[:, None]
    return np.ascontiguousarray(np.stack([c * 0.125, s * 0.125, c, s]).astype(np.float32))


def swap_pairs(w):
    w = w.reshape(w.shape[0], -1, 2)[:, :, ::-1]
    return np.ascontiguousarray(w.reshape(w.shape[0], -1))


def build_program(n_layers=DEPTH):
    nc = bass.Bass("TRN2", target_bir_lowering=False)
    kb = KB(nc)
    h0 = kb.din("h0", [D, LT])
    vecs = kb.din("vecs", [128, NV])
    ident = kb.din("ident", [128, 128])
    modw = kb.din("mod_w", [DEPTH, D, 9216])
    w_in = kb.din("ffn_w_in", [DEPTH, 2, D, 2 * DFF])
    w_out = kb.din("ffn_w_out", [DEPTH, 2, DFF, D])
    wup = kb.din("ml_w_up", [2, D, 4096])
    bd = kb.din("ml_bd", [2, 3, 128, 16, 128])
    wif = kb.din("ml_wif", [2, 128, 48, 16])
    wdn = kb.din("ml_w_down", [2, E2, D])
    ngbc = kb.din("ml_ngbc", [2, 128, E2])
    wqkv = kb.din("at_w_qkv", [1, D, 1536])
    wsw = kb.din("at_wsw", [D, 1280])
    wo = kb.din("at_w_o", [1, D, D])
    rope = kb.din("rope", [4, 128, SEQ])
    scw_in = kb.din("sc_w_in", [1, D, 3072])
    scw_out = kb.din("sc_w_out", [1, D, D])
    out = kb.dout("out", [D, SEQ])
    hA = kb.dscr("hA", [D, LT])
    hB = kb.dscr("hB", [D, LT])
    setup_consts(kb, vecs, ident)
    compute_mod(kb, modw, list(range(n_layers)))
    bufs = [(hA, "hA"), (hB, "hB")]
    cur = (h0, "h0")
    nxt_i = 0
    finals = []
    lat_tiles = [t for t in TILES if t[2] == 0]
    for i in range(n_layers):
        last = (i == DEPTH - 1)
        kind, j = i % 3, i // 3
        nxt = bufs[nxt_i]
        ffn_sublayer(kb, i, 0, cur[0], cur[1], nxt[0], nxt[1], w_in, w_out, TILES)
        cur, nxt_i = nxt, 1 - nxt_i
        nxt = bufs[nxt_i]
        if kind == 0:
            mlstm_mixer(kb, i, j, last, cur[0], cur[1], nxt[0], nxt[1], wup, bd, wif, wdn, ngbc)
        elif kind == 1:
            attention_mixer(kb, i, cur[0], cur[1], nxt[0], nxt[1], wqkv, wsw, wo, rope, TILES)
        else:
            shortconv_mixer(kb, i, cur[0], cur[1], nxt[0], nxt[1], scw_in, scw_out, TILES)
        cur, nxt_i = nxt, 1 - nxt_i
        nxt = bufs[nxt_i]
        if last:
            finals = ffn_sublayer(kb, i, 1, cur[0], cur[1], out, "out", w_in, w_out, lat_tiles, out_col0=NCTX)
        else:
            finals = ffn_sublayer(kb, i, 1, cur[0], cur[1], nxt[0], nxt[1], w_in, w_out, TILES)
            cur, nxt_i = nxt, 1 - nxt_i
    if n_layers < DEPTH:
        finals = [kb.P.dma(out, cur[0][:, NCTX:LT], reads=hres(cur[1], NCTX, SEQ), writes=["out"])]
    stats = kb.P.emit(final_waits=[o.idx for o in finals])
    return nc, stats


def host_inputs(inputs, n_cores=8):
    I = {k: np.asarray(v) for k, v in inputs.items()}
    wqkv = np.ascontiguousarray(I["at_w_qkv"], np.float32)
    shared = {
        "ident": np.eye(128, dtype=np.float32),
        "mod_w": np.ascontiguousarray(I["mod_w"], np.float32),
        "ffn_w_in": np.ascontiguousarray(I["ffn_w_in"], np.float32),
        "ffn_w_out": np.ascontiguousarray(I["ffn_w_out"], np.float32),
        "ml_w_up": np.ascontiguousarray(I["ml_w_up"], np.float32),
        "ml_bd": np.stack([host_bd(I["ml_w_qkv"][j]) for j in range(2)]),
        "ml_wif": np.stack([host_wif(I["ml_w_if"][j]) for j in range(2)]),
        "ml_w_down": np.ascontiguousarray(I["ml_w_down"], np.float32),
        "ml_ngbc": np.ascontiguousarray(np.broadcast_to(I["ml_norm_g"].astype(np.float32)[:, None, :], (2, 128, E2))),
        "at_w_qkv": wqkv,
        "at_wsw": swap_pairs(wqkv[0][:, :1280]),
        "at_w_o": np.ascontiguousarray(I["at_w_o"], np.float32),
        "rope": rope_tables_host(),
        "sc_w_in": np.ascontiguousarray(I["sc_w_in"], np.float32),
        "sc_w_out": np.ascontiguousarray(I["sc_w_out"], np.float32),
    }
    maps = []
    for b in range(n_cores):
        m = dict(shared)
        m["h0"] = np.ascontiguousarray(np.concatenate([I["ctx"][b].T, I["x"][b].T], axis=1), np.float32)
        m["vecs"] = pack_vecs(I, b)
        maps.append(m)
    return maps


def kernel(**inputs):
    nc, _ = build_program()
    maps = host_inputs(inputs, 8)
    res = run_bass_kernel_spmd(nc, maps, core_ids=list(range(8)))
    outs = [np.asarray(r["out"], np.float32).T for r in res.results]
    return np.ascontiguousarray(np.stack(outs, axis=0))
```

```python
import contextlib
import numpy as np
import concourse.bass as bass
import concourse.mybir as mybir
from concourse.bass_utils import run_bass_kernel_spmd

F32 = mybir.dt.float32
BF16 = mybir.dt.bfloat16
I32 = mybir.dt.int32
AF = mybir.ActivationFunctionType
ALU = mybir.AluOpType
AX = mybir.AxisListType


ATTACH_WAITS = True


class Op:
    __slots__ = ("idx", "eng", "fn", "dma", "deps", "signal", "sem", "val", "prewait")

    def __init__(self, idx, eng, fn, dma):
        self.idx = idx
        self.eng = eng
        self.fn = fn
        self.dma = dma
        self.deps = set()
        self.signal = False
        self.sem = None
        self.val = 0
        self.prewait = None


class Prog:
    ENGS = ("pe", "act", "dve", "pool", "sp")

    def __init__(self, nc, n_dma_sems=6):
        self.nc = nc
        self.ops = []
        self.lastw = {}
        self.readers = {}
        self.n_dma_sems = n_dma_sems

    def add(self, eng, fn, reads=(), writes=(), dma=False):
        op = Op(len(self.ops), eng, fn, dma)
        raw = set()
        for r in reads:
            w = self.lastw.get(r)
            if w is not None:
                raw.add(w)
        other = set()
        for r in writes:
            w = self.lastw.get(r)
            if w is not None:
                other.add(w)
            for q in self.readers.get(r, ()):
                other.add(q)
        op.deps = set(raw)
        for d in other:
            dop = self.ops[d]
            if (not dma) and (not dop.dma) and dop.eng == eng and eng != "pool":
                continue
            op.deps.add(d)
        for r in reads:
            self.readers.setdefault(r, []).append(op.idx)
        for r in writes:
            self.lastw[r] = op.idx
            self.readers[r] = []
        op.deps.discard(op.idx)
        self.ops.append(op)
        return op

    def pe(self, fn, reads=(), writes=()):
        return self.add("pe", fn, reads, writes)

    def act(self, fn, reads=(), writes=()):
        return self.add("act", fn, reads, writes)

    def dve(self, fn, reads=(), writes=()):
        return self.add("dve", fn, reads, writes)

    def pool(self, fn, reads=(), writes=()):
        return self.add("pool", fn, reads, writes)

    def dma(self, out, in_, reads=(), writes=(), q="sp"):
        return self.add(q, lambda e: e.dma_start(out=out, in_=in_), reads, writes, dma=True)

    def barrier(self):
        last = {}
        dmas = {e: [] for e in self.ENGS}
        for op in self.ops:
            if op.fn is None:
                continue
            if op.dma:
                dmas[op.eng].append(op.idx)
            else:
                last[op.eng] = op.idx
        deps = set(last.values())
        for e in self.ENGS:
            deps.update(dmas[e][-self.n_dma_sems:])
        for e in self.ENGS:
            op = Op(len(self.ops), e, None, False)
            op.deps = set(deps)
            self.ops.append(op)
        self.lastw = {}
        self.readers = {}

    def emit(self, final_waits=()):
        nc = self.nc
        ops = self.ops
        for op in ops:
            if op.eng == "pe" and not op.dma:
                op.deps = {d for d in op.deps if not (ops[d].eng == "pe" and not ops[d].dma)}
            op.deps = {d for d in op.deps if ops[d].fn is not None}
            for d in op.deps:
                ops[d].signal = True
        for idx in final_waits:
            ops[idx].signal = True
        esem = {e: nc.alloc_semaphore("s_" + e) for e in self.ENGS}
        dsem = {e: [nc.alloc_semaphore("d_%s%d" % (e, i)) for i in range(self.n_dma_sems)]
                for e in ("sp", "act", "pool")}
        ecount = {e: 0 for e in self.ENGS}
        dcount = {e: 0 for e in self.ENGS}
        P = self.n_dma_sems
        for op in ops:
            if op.dma:
                i = dcount[op.eng]
                dcount[op.eng] += 1
                op.sem = dsem[op.eng][i % P]
                op.val = 16 * (i // P + 1)
                op.signal = True
                if i >= P:
                    op.prewait = (op.sem, 16 * (i // P))
            elif op.signal:
                ecount[op.eng] += 1
                op.sem = esem[op.eng]
                op.val = ecount[op.eng]
        per_eng = {e: [] for e in self.ENGS}
        for op in ops:
            per_eng[op.eng].append(op)
        self.stats = {e: len(v) for e, v in per_eng.items()}
        nwaits = [0]

        def run(e, eng_handle, extra_final):
            waited = {}

            def wait(sem, val):
                k = id(sem)
                if waited.get(k, 0) >= val:
                    return
                waited[k] = val
                eng_handle.wait_ge(sem, val)
                nwaits[0] += 1

            for op in per_eng[e]:
                if op.prewait is not None:
                    wait(*op.prewait)
                need = {}
                for d in op.deps:
                    dop = ops[d]
                    k = id(dop.sem)
                    if k not in need or need[k][1] < dop.val:
                        need[k] = (dop.sem, dop.val)
                pend = [(sem, val) for sem, val in need.values() if waited.get(id(sem), 0) < val]
                if op.fn is None:
                    for sem, val in pend:
                        wait(sem, val)
                    continue
                attach = None
                if pend and ATTACH_WAITS:
                    attach = pend.pop()
                for sem, val in pend:
                    wait(sem, val)
                ins = op.fn(eng_handle)
                if attach is not None:
                    ins._wait_ge(attach[0], attach[1])
                    waited[id(attach[0])] = attach[1]
                if op.signal:
                    ins.then_inc(op.sem, 16 if op.dma else 1)
            if extra_final:
                for idx in final_waits:
                    wait(ops[idx].sem, ops[idx].val)
                for q in ("sp", "act", "pool"):
                    n = dcount[q]
                    for s in range(min(P, n)):
                        last_i = ((n - 1 - s) // P) * P + s
                        wait(dsem[q][s], 16 * (last_i // P + 1))

        with nc.Block() as block:
            @block.tensor
            def _(eng):
                run("pe", eng, False)

            @block.scalar
            def _(eng):
                run("act", eng, False)

            @block.vector
            def _(eng):
                run("dve", eng, False)

            @block.gpsimd
            def _(eng):
                run("pool", eng, False)

            @block.sync
            def _(eng):
                run("sp", eng, True)
        self.stats["waits"] = nwaits[0]
        return self.stats

import numpy as np

D = 1024
NCTX = 256
SEQ = 4096
LT = NCTX + SEQ
DEPTH = 4
DFF = 2816
NJ = DFF // 128
ALPHA = (2 * DEPTH) ** 0.25
LN_EPS = 1e-5
TILES = [(0, 256, 1)] + [(256 + 512 * i, 512, 0) for i in range(8)]

VEC_SPEC = [("cond", 16), ("mod_b", 4 * 72), ("ln_g", 4 * 3 * 8), ("ln_b", 4 * 3 * 8),
            ("ml_conv_w", 2 * 3 * 16), ("ml_conv_b", 2 * 16), ("ml_skip", 2 * 16),
            ("sc_conv_w", 3 * 8), ("at_sink", 16), ("ml_b_if", 2 * 16)]
VOFF = {}
_o = 0
for _n, _c in VEC_SPEC:
    VOFF[_n] = _o
    _o += _c
NV = _o


_UNC = [0]


def UN(name):
    _UNC[0] += 1
    return "%s_%d" % (name, _UNC[0])


def fm(v):
    v = np.asarray(v, np.float32)
    F = v.shape[-1]
    lead = v.shape[:-1]
    a = v.reshape(lead + (F // 128, 128))
    a = np.moveaxis(a, -1, 0)
    return a.reshape(128, -1)


def pack_vecs(inp, b):
    out = np.zeros((128, NV), np.float32)

    def put(name, arr):
        arr = np.asarray(arr, np.float32)
        out[:, VOFF[name]:VOFF[name] + arr.shape[1]] = arr

    cond = np.stack([inp["c"][b], inp["c_ctx"]], axis=-1)
    put("cond", cond.reshape(8, 128, 2).transpose(1, 0, 2).reshape(128, 16))
    put("mod_b", fm(inp["mod_b"]))
    put("ln_g", fm(inp["ln_g"]))
    put("ln_b", fm(inp["ln_b"]))
    put("ml_conv_w", fm(inp["ml_conv_w"]))
    put("ml_conv_b", fm(inp["ml_conv_b"]))
    put("ml_skip", fm(inp["ml_skip"]))
    put("sc_conv_w", fm(inp["sc_conv_w"]))
    put("at_sink", np.broadcast_to(inp["at_sink"].reshape(1, 16), (128, 16)))
    put("ml_b_if", np.broadcast_to(inp["ml_b_if"].reshape(1, 32), (128, 32)))
    return out


class KB:
    def __init__(self, nc):
        self.nc = nc
        self.P = Prog(nc)
        self.uid = 0
        nc_ = nc
        self.pb = [nc_.alloc_psum_tensor("pb%d" % i, [128, 512], F32) for i in range(8)]
        self.vecs = nc_.alloc_sbuf_tensor("sb_vecs", [128, NV], F32)
        self.mod = nc_.alloc_sbuf_tensor("sb_mod", [128, DEPTH, 72, 2], F32)
        self.ident = nc_.alloc_sbuf_tensor("sb_ident", [128, 128], F32)
        self.identb = nc_.alloc_sbuf_tensor("identb", [128, 128], BF16)
        self.onesm = nc_.alloc_sbuf_tensor("onesm", [128, 128], F32)
        self.epsc = nc_.alloc_sbuf_tensor("epsc", [128, 4], F32)
        self.dram = {}
        self.wbf_in = nc_.dram_tensor("wbf_in", [NJ // 2, 128, 8, 512], BF16).ap()
        self.wbf_out = nc_.dram_tensor("wbf_out", [8, 128, NJ, 128], BF16).ap()

    def din(self, name, shape, dt=F32):
        t = self.nc.dram_tensor(name, list(shape), dt, kind="ExternalInput")
        self.dram[name] = t
        return t.ap()

    def dout(self, name, shape, dt=F32):
        t = self.nc.dram_tensor(name, list(shape), dt, kind="ExternalOutput")
        self.dram[name] = t
        return t.ap()

    def dscr(self, name, shape, dt=F32):
        t = self.nc.dram_tensor(name, list(shape), dt)
        self.dram[name] = t
        return t.ap()

    def v(self, name, idx=0, n=1):
        o = VOFF[name] + idx
        return self.vecs[:, o:o + n]

    def mv(self, i, slot, c, col):
        return self.mod[:, i, slot * 8 + c, col:col + 1]


def setup_consts(kb, vecs_d, ident_d):
    P = kb.P
    P.dma(kb.vecs[:], vecs_d, writes=["vecs"])
    P.dma(kb.ident[:], ident_d, writes=["ident"])
    P.dve(lambda e: e.tensor_copy(out=kb.identb[:], in_=kb.ident[:]), reads=["ident"], writes=["identb"])
    P.pool(lambda e: e.memset(kb.onesm[:], 1.0 / 1024.0), writes=["onesm"])
    P.pool(lambda e: e.memset(kb.epsc[:, 0:1], LN_EPS / (ALPHA * ALPHA)), writes=["epsc"])
    P.pool(lambda e: e.memset(kb.epsc[:, 1:2], LN_EPS), writes=["epsc"])
    P.pool(lambda e: e.memset(kb.epsc[:, 2:3], -0.5 * float(np.log(512.0))), writes=["epsc"])
    P.pool(lambda e: e.memset(kb.epsc[:, 3:4], 1.0), writes=["epsc"])


def compute_mod(kb, modw_d, layers, dmap=None):
    nc, P = kb.nc, kb.P
    condT = kb.v("cond", 0, 16)
    P.act(lambda e: e.activation(out=condT, in_=condT, func=AF.Silu), reads=["vecs"], writes=["vecs"])
    with nc.sbuf_tensor(UN("mw_stage"), [128, 2, 8, 512], F32) as stage, \
            nc.sbuf_tensor(UN("mrow"), [2, 9216], F32) as mrow:
        cnt = 0
        for i in layers:
            for nb in range(18):
                s = cnt % 2
                cnt += 1
                src = modw_d[i if dmap is None else dmap[i]].rearrange("(kc p) n -> p kc n", p=128)[:, :, nb * 512:(nb + 1) * 512]
                P.dma(stage[:, s], src, writes=[("mws", s)])
                ps = kb.pb[s]
                for kc in range(8):
                    P.pe(lambda e, kc=kc, s=s, ps=ps: e.matmul(
                        ps[0:2, :], lhsT=kb.vecs[:, VOFF["cond"] + 2 * kc:VOFF["cond"] + 2 * kc + 2],
                        rhs=stage[:, s, kc, :], start=(kc == 0), stop=(kc == 7)),
                        reads=["vecs", ("mws", s)], writes=[("pb", s)])
                P.act(lambda e, nb=nb, ps=ps: e.activation(out=mrow[0:2, nb * 512:(nb + 1) * 512], in_=ps[0:2, :],
                                                            func=AF.Identity),
                      reads=[("pb", s)], writes=["mrow"])
            pst = kb.pb[2]
            for k in range(72):
                P.pe(lambda e, k=k: e.transpose(out=pst[:, 2 * k:2 * k + 2], in_=mrow[0:2, k * 128:(k + 1) * 128],
                                                identity=kb.ident[0:2, 0:2]),
                     reads=["mrow", "ident"], writes=[("pb", 2)])
            for j in range(2):
                P.dve(lambda e, i=i, j=j: e.tensor_tensor(
                    out=kb.mod[:, i, :, j], in0=pst[:, j:144:2], in1=kb.v("mod_b", i * 72, 72), op=ALU.add),
                    reads=[("pb", 2), "vecs"], writes=[("mod", i)])
            for s3 in range(3):
                w = 1.0 if s3 == 1 else 0.5
                sl = kb.mod[:, i, (3 * s3 + 1) * 8:(3 * s3 + 2) * 8, :]
                P.dve(lambda e, sl=sl: e.tensor_scalar(out=sl, in0=sl, scalar1=1.0, scalar2=None, op0=ALU.add),
                      reads=[("mod", i)], writes=[("mod", i)])
                gl = kb.mod[:, i, (3 * s3 + 2) * 8:(3 * s3 + 3) * 8, :]
                P.dve(lambda e, gl=gl, w=w: e.tensor_scalar(out=gl, in0=gl, scalar1=w / ALPHA, scalar2=None,
                                                            op0=ALU.mult),
                      reads=[("mod", i)], writes=[("mod", i)])
    P.barrier()


class LNBufs:
    def __init__(self, nc, es, n, tag):
        self.n = n
        self.z = es.enter_context(nc.sbuf_tensor(UN("lnz" + tag), [128, 8, n], F32))
        self.zq = es.enter_context(nc.sbuf_tensor(UN("lnzq" + tag), [128, 8, n], F32))
        self.sm = es.enter_context(nc.sbuf_tensor(UN("lnsm" + tag), [128, 3, n], F32))
        self.tag = tag


def ln_accum_elem(kb, lb, o, y_ps, y_res, hsrc, hres, gs_ap, n, extra_reads=()):
    P = kb.P
    t = lb.tag
    P.dve(lambda e: e.scalar_tensor_tensor(out=lb.z[:, o, :n], in0=y_ps, scalar=gs_ap, in1=hsrc,
                                           op0=ALU.mult, op1=ALU.add),
          reads=[y_res, hres] + list(extra_reads), writes=[("lnz" + t, o)])
    P.act(lambda e: e.activation(out=lb.zq[:, o, :n], in_=lb.z[:, o, :n], func=AF.Square),
          reads=[("lnz" + t, o)], writes=[("lnzq" + t, o)])


def ln_accum_stats(kb, lb, o, n):
    P = kb.P
    t = lb.tag
    P.pe(lambda e: e.matmul(kb.pb[6][:, :n], lhsT=kb.onesm[:], rhs=lb.z[:, o, :n], start=(o == 0), stop=(o == 7)),
         reads=[("lnz" + t, o), "onesm"], writes=[("pb", 6)])
    P.pe(lambda e: e.matmul(kb.pb[7][:, :n], lhsT=kb.onesm[:], rhs=lb.zq[:, o, :n], start=(o == 0), stop=(o == 7)),
         reads=[("lnzq" + t, o), "onesm"], writes=[("pb", 7)])


def ln_accum(kb, lb, o, y_ps, y_res, hsrc, hres, gs_ap, n, extra_reads=()):
    ln_accum_elem(kb, lb, o, y_ps, y_res, hsrc, hres, gs_ap, n, extra_reads)
    ln_accum_stats(kb, lb, o, n)


def ln_finish(kb, lb, n, li, k, dst_ap_fn, dst_res):
    P = kb.P
    t = lb.tag
    sm = lb.sm
    mean, msq, std = (sm[:, i, :n] for i in range(3))
    var, rstd, nmr = msq, std, mean
    R = ("lnsm" + t)
    P.act(lambda e: e.activation(out=mean, in_=kb.pb[6][:, :n], func=AF.Identity), reads=[("pb", 6)], writes=[(R, 0)])
    P.pool(lambda e: e.tensor_tensor(out=msq, in0=mean, in1=mean, op=ALU.mult), reads=[(R, 0)], writes=[(R, 1)])
    P.dve(lambda e: e.tensor_tensor(out=var, in0=kb.pb[7][:, :n], in1=msq, op=ALU.subtract),
          reads=[("pb", 7), (R, 1)], writes=[(R, 1)])
    P.act(lambda e: e.activation(out=std, in_=var, func=AF.Sqrt, bias=kb.epsc[:, 0:1]), reads=[(R, 1), "epsc"],
          writes=[(R, 2)])
    P.dve(lambda e: e.reciprocal(out=rstd, in_=std), reads=[(R, 2)], writes=[(R, 2)])
    P.dve(lambda e: e.scalar_tensor_tensor(out=nmr, in0=mean, scalar=-1.0, in1=rstd, op0=ALU.mult, op1=ALU.mult),
          reads=[(R, 0), (R, 2)], writes=[(R, 0)])
    for o in range(8):
        zo = lb.z[:, o, :n]
        P.dve(lambda e, zo=zo: e.tensor_tensor(out=zo, in0=zo, in1=rstd, op=ALU.mult),
              reads=[("lnz" + t, o), (R, 2)], writes=[("lnz" + t, o)])
        P.pool(lambda e, zo=zo: e.tensor_tensor(out=zo, in0=zo, in1=nmr, op=ALU.add),
               reads=[("lnz" + t, o), (R, 0)], writes=[("lnz" + t, o)])
        ho = lb.zq[:, o, :n]
        g_ap = kb.v("ln_g", (li * 3 + k) * 8 + o)
        b_ap = kb.v("ln_b", (li * 3 + k) * 8 + o)
        P.act(lambda e, zo=zo, ho=ho, g_ap=g_ap, b_ap=b_ap: e.activation(out=ho, in_=zo, func=AF.Identity,
                                                                         bias=b_ap, scale=g_ap),
              reads=[("lnz" + t, o), "vecs"], writes=[("lnzq" + t, o)])
    ops = []
    ops.append(P.dma(dst_ap_fn(), lb.zq[:, :, :n], reads=[("lnzq" + t, o) for o in range(8)], writes=list(dst_res),
                     q="act"))
    return ops


def hres(name, t0, n):
    return [(name, c) for c in range(t0 // 128, (t0 + n + 127) // 128)]


def hT_tile(h_ap, t0, n):
    return h_ap.rearrange("(c p) t -> p c t", p=128)[:, :, t0:t0 + n]


def ffn_sublayer(kb, li, which, h_in, h_in_res, h_out, h_out_res, w_in_d, w_out_d, tiles, out_col0=None):
    import contextlib
    nc, P = kb.nc, kb.P
    slot = 0 if which == 0 else 2
    k = slot
    w_in = w_in_d[li, which]
    w_out = w_out_d[li, which]
    w_in_v = w_in.rearrange("(kc p) n -> p kc n", p=128)
    w_out_v = w_out.rearrange("(j p) n -> p j n", p=128)
    tag = "f"
    final_ops = []
    with contextlib.ExitStack() as es:
        A = lambda name, shape, dt: es.enter_context(nc.sbuf_tensor(UN(name), shape, dt))
        stg = A("ff_stg", [128, 2, 8, 512], F32)
        wb = A("ff_wb", [128, 2, 8, 512], BF16)
        stgo = A("ff_stgo", [128, 22, 128], F32)
        wob = A("ff_wob", [128, 2, 22, 128], BF16)
        ht = A("ff_ht", [128, 2, 8, 512], F32)
        u = A("ff_u", [128, 8, 512], BF16)
        a = A("ff_a", [128, 22, 512], BF16)
        sg = A("ff_sg", [128, 2, 512], F32)
        lb = LNBufs(nc, es, 512, tag)
        wcnt = 0
        ocnt = 0
        gcnt = 0
        def prep_tile(ti_):
            t0_, n_, col_ = tiles[ti_]
            hs_ = ti_ % 2
            P.dma(ht[:, hs_, :, :n_], hT_tile(h_in, t0_, n_), reads=hres(h_in_res, t0_, n_), writes=[("ff_ht", hs_)])
            for c in range(8):
                P.act(lambda e, c=c: e.activation(
                    out=u[:, c, :n_], in_=ht[:, hs_, c, :n_], func=AF.Identity,
                    bias=kb.mv(li, 3 * slot, c, col_), scale=kb.mv(li, 3 * slot + 1, c, col_)),
                    reads=[("ff_ht", hs_), ("mod", li)], writes=[("ff_u", c)])

        for ti, (t0, n, col) in enumerate(tiles):
            hs = ti % 2
            if ti == 0:
                prep_tile(0)
            first = (ti == 0)
            for jp in range(NJ // 2):
                ws = wcnt % 2
                wcnt += 1
                if first:
                    P.dma(stg[:, ws, :, 0:256], w_in_v[:, :, jp * 256:(jp + 1) * 256], writes=[("ff_stg", ws, 0)])
                    P.dma(stg[:, ws, :, 256:512], w_in_v[:, :, DFF + jp * 256:DFF + (jp + 1) * 256],
                          writes=[("ff_stg", ws, 1)])
                    for hf in range(2):
                        P.dve(lambda e, ws=ws, hf=hf: e.tensor_copy(out=wb[:, ws, :, hf * 256:(hf + 1) * 256],
                                                                    in_=stg[:, ws, :, hf * 256:(hf + 1) * 256]),
                              reads=[("ff_stg", ws, hf)], writes=[("ff_wb", ws, hf)])
                    P.dma(kb.wbf_in[jp], wb[:, ws], reads=[("ff_wb", ws, 0), ("ff_wb", ws, 1)],
                          writes=[("wbf_in", jp)], q="pool")
                else:
                    P.dma(wb[:, ws], kb.wbf_in[jp], reads=[("wbf_in", jp)],
                          writes=[("ff_wb", ws, 0), ("ff_wb", ws, 1)])
                for jj in range(2):
                    j = jp * 2 + jj
                    gs_ = gcnt % 2
                    gcnt += 1
                    pg, pv = kb.pb[gs_], kb.pb[2 + gs_]
                    for kc in range(8):
                        P.pe(lambda e, kc=kc, ws=ws, jj=jj, pg=pg, n=n: e.matmul(
                            pg[:, :n], lhsT=wb[:, ws, kc, jj * 128:(jj + 1) * 128], rhs=u[:, kc, :n],
                            start=(kc == 0), stop=(kc == 7)),
                            reads=[("ff_wb", ws, 0), ("ff_u", kc)], writes=[("pb", gs_)])
                    for kc in range(8):
                        P.pe(lambda e, kc=kc, ws=ws, jj=jj, pv=pv, n=n: e.matmul(
                            pv[:, :n], lhsT=wb[:, ws, kc, 256 + jj * 128:256 + (jj + 1) * 128], rhs=u[:, kc, :n],
                            start=(kc == 0), stop=(kc == 7)),
                            reads=[("ff_wb", ws, 1), ("ff_u", kc)], writes=[("pb", 2 + gs_)])
                    P.act(lambda e, gs_=gs_, pg=pg, n=n: e.activation(out=sg[:, gs_, :n], in_=pg[:, :n], func=AF.Silu),
                          reads=[("pb", gs_)], writes=[("ff_sg", gs_)])
                    P.dve(lambda e, gs_=gs_, pv=pv, j=j, n=n: e.tensor_tensor(out=a[:, j, :n], in0=pv[:, :n],
                                                                              in1=sg[:, gs_, :n], op=ALU.mult),
                          reads=[("pb", 2 + gs_), ("ff_sg", gs_)], writes=[("ff_a", j)])
            if ti + 1 < len(tiles):
                prep_tile(ti + 1)
            for o in range(8):
                os_ = ocnt % 2
                ocnt += 1
                if first:
                    P.dma(stgo[:], w_out_v[:, :, o * 128:(o + 1) * 128], writes=["ff_stgo"])
                    P.act(lambda e, os_=os_: e.activation(out=wob[:, os_], in_=stgo[:], func=AF.Identity),
                          reads=["ff_stgo"], writes=[("ff_wob", os_)])
                    P.dma(kb.wbf_out[o], wob[:, os_], reads=[("ff_wob", os_)], writes=[("wbf_out", o)], q="pool")
                else:
                    P.dma(wob[:, os_], kb.wbf_out[o], reads=[("wbf_out", o)], writes=[("ff_wob", os_)])
                py = kb.pb[4 + os_]
                for j in range(NJ):
                    P.pe(lambda e, j=j, os_=os_, py=py, n=n: e.matmul(
                        py[:, :n], lhsT=wob[:, os_, j, :], rhs=a[:, j, :n], start=(j == 0), stop=(j == NJ - 1)),
                        reads=[("ff_wob", os_), ("ff_a", j)], writes=[("pb", 4 + os_)])
                ln_accum_elem(kb, lb, o, py[:, :n], ("pb", 4 + os_), ht[:, hs, o, :n], ("ff_ht", hs),
                              kb.mv(li, 3 * slot + 2, o, col), n, extra_reads=[("mod", li)])
                if o > 0:
                    ln_accum_stats(kb, lb, o - 1, n)
            ln_accum_stats(kb, lb, 7, n)
            oc = t0 - (out_col0 or 0)
            final_ops += ln_finish(kb, lb, n, li, k, lambda oc=oc, n=n: hT_tile(h_out, oc, n), hres(h_out_res, t0, n))
    kb.P.barrier()
    return final_ops

import contextlib

NEG = -30000.0


def load_weight_bf16(kb, stage, wb, res, w_ap, KC, N, col0=0, eng="dve"):
    nc, P = kb.nc, kb.P
    wv = w_ap.rearrange("(kc p) n -> p kc n", p=128)
    nb = min(N, stage.shape[-1])
    for kc in range(KC):
        for n0 in range(0, N, nb):
            n1 = min(N, n0 + nb)
            s = kb.uid % 2
            kb.uid += 1
            P.dma(stage[:, s, :n1 - n0], wv[:, kc, col0 + n0:col0 + n1], writes=[("wstage", s)])
            dst = wb[:, kc, n0:n1]
            src = stage[:, s, :n1 - n0]
            if eng == "act":
                P.act(lambda e, dst=dst, src=src: e.activation(out=dst, in_=src, func=AF.Identity),
                      reads=[("wstage", s)], writes=[res])
            else:
                P.add(eng, lambda e, dst=dst, src=src: e.tensor_copy(out=dst, in_=src),
                      reads=[("wstage", s)], writes=[res])


def load_mod_tile(kb, ht, u, rname, h_in, h_in_res, li, slot, t0, n, col, s0, s1, halo):
    P = kb.P
    a0 = max(t0 - halo, s0)
    a1 = min(t0 + n + halo, s1)
    off = a0 - (t0 - halo)
    w = a1 - a0
    P.dma(ht[:, :, off:off + w], hT_tile(h_in, a0, w), reads=hres(h_in_res, a0, w), writes=[rname + "_ht"])
    for c in range(8):
        P.act(lambda e, c=c: e.activation(out=u[:, c, off:off + w], in_=ht[:, c, off:off + w], func=AF.Identity,
                                          bias=kb.mv(li, 3 * slot, c, col), scale=kb.mv(li, 3 * slot + 1, c, col)),
              reads=[rname + "_ht", ("mod", li)], writes=[(rname + "_u", c)])
        tot = n + 2 * halo
        for (z0, z1) in ((0, off), (off + w, tot)):
            if z1 > z0:
                P.act(lambda e, c=c, z0=z0, z1=z1: e.activation(out=u[:, c, z0:z1], in_=ht[:, c, off:off + z1 - z0],
                                                                func=AF.Identity, scale=0.0),
                      reads=[rname + "_ht"], writes=[(rname + "_u", c)])
    return off, w


def proj_ln(kb, lb, xin, xin_res, KC, wob, wob_res, hsrc_fn, hsrc_res, li, n, col, dst_fn, dst_res):
    P = kb.P
    for o in range(8):
        py = kb.pb[4 + (o % 2)]
        for kc in range(KC):
            P.pe(lambda e, o=o, kc=kc, py=py: e.matmul(py[:, :n], lhsT=wob[:, kc, o * 128:(o + 1) * 128],
                                                        rhs=xin[:, kc, :n], start=(kc == 0), stop=(kc == KC - 1)),
                 reads=[wob_res, xin_res(kc)], writes=[("pb", 4 + (o % 2))])
        ln_accum_elem(kb, lb, o, py[:, :n], ("pb", 4 + (o % 2)), hsrc_fn(o), hsrc_res, kb.mv(li, 5, o, col), n,
                      extra_reads=[("mod", li)])
        if o > 0:
            ln_accum_stats(kb, lb, o - 1, n)
    ln_accum_stats(kb, lb, 7, n)
    return ln_finish(kb, lb, n, li, 1, dst_fn, dst_res)


def split_cols(m):
    if m <= 512:
        return [(0, m)]
    h = m // 2
    return [(0, h), (h, m)]


def shortconv_mixer(kb, li, h_in, h_in_res, h_out, h_out_res, w_in_d, w_out_d, tiles):
    nc, P = kb.nc, kb.P
    finals = []
    with contextlib.ExitStack() as es:
        A = lambda name, shape, dt: es.enter_context(nc.sbuf_tensor(UN(name), shape, dt))
        win = A("sc_win", [128, 8, 3072], BF16)
        wout = A("sc_wout", [128, 8, 1024], BF16)
        wst = A("sc_wst", [128, 2, 2048], F32)
        load_weight_bf16(kb, wst, win, "sc_win", w_in_d[0], 8, 3072)
        load_weight_bf16(kb, wst, wout, "sc_wout", w_out_d[0], 8, 1024)
        ht = A("sc_ht", [128, 8, 514], F32)
        u = A("sc_u", [128, 8, 514], BF16)
        cgs = A("sc_cg", [128, 514], F32)
        T = A("sc_T", [128, 514], F32)
        cv = A("sc_cv", [128, 512], F32)
        pin = A("sc_pin", [128, 8, 512], BF16)
        lb = LNBufs(nc, es, 512, "s")
        for (t0, n, col) in tiles:
            s0, s1 = (0, NCTX) if col == 1 else (NCTX, LT)
            off, w = load_mod_tile(kb, ht, u, "sc", h_in, h_in_res, li, 1, t0, n, col, s0, s1, 1)
            m = n + 2
            for e_ in range(8):
                for (hi, (c0, c1)) in enumerate(split_cols(m)):
                    pc, px = kb.pb[hi], kb.pb[2 + hi]
                    for kc in range(8):
                        P.pe(lambda e, kc=kc, e_=e_, pc=pc, c0=c0, c1=c1: e.matmul(
                            pc[:, :c1 - c0], lhsT=win[:, kc, 1024 + e_ * 128:1024 + (e_ + 1) * 128],
                            rhs=u[:, kc, c0:c1], start=(kc == 0), stop=(kc == 7)),
                            reads=["sc_win", ("sc_u", kc)], writes=[("pb", hi)])
                    for kc in range(8):
                        P.pe(lambda e, kc=kc, e_=e_, px=px, c0=c0, c1=c1: e.matmul(
                            px[:, :c1 - c0], lhsT=win[:, kc, 2048 + e_ * 128:2048 + (e_ + 1) * 128],
                            rhs=u[:, kc, c0:c1], start=(kc == 0), stop=(kc == 7)),
                            reads=["sc_win", ("sc_u", kc)], writes=[("pb", 2 + hi)])
                    P.act(lambda e, pc=pc, c0=c0, c1=c1: e.activation(out=cgs[:, c0:c1], in_=pc[:, :c1 - c0],
                                                                      func=AF.Identity),
                          reads=[("pb", hi)], writes=["sc_cg"])
                    P.dve(lambda e, px=px, c0=c0, c1=c1: e.tensor_tensor(out=T[:, c0:c1], in0=px[:, :c1 - c0],
                                                                         in1=cgs[:, c0:c1], op=ALU.mult),
                          reads=[("pb", 2 + hi), "sc_cg"], writes=["sc_T"])
                w0, w1, w2 = (kb.v("sc_conv_w", k_ * 8 + e_) for k_ in range(3))
                P.dve(lambda e, n=n, w1=w1: e.tensor_scalar(out=cv[:, :n], in0=T[:, 1:n + 1], scalar1=w1, scalar2=None,
                                                            op0=ALU.mult),
                      reads=["sc_T", "vecs"], writes=["sc_cv"])
                P.dve(lambda e, n=n, w0=w0: e.scalar_tensor_tensor(out=cv[:, :n], in0=T[:, 0:n], scalar=w0,
                                                                   in1=cv[:, :n], op0=ALU.mult, op1=ALU.add),
                      reads=["sc_T", "sc_cv"], writes=["sc_cv"])
                P.dve(lambda e, n=n, w2=w2: e.scalar_tensor_tensor(out=cv[:, :n], in0=T[:, 2:n + 2], scalar=w2,
                                                                   in1=cv[:, :n], op0=ALU.mult, op1=ALU.add),
                      reads=["sc_T", "sc_cv"], writes=["sc_cv"])
                pbg = kb.pb[4 + (e_ % 2)]
                for kc in range(8):
                    P.pe(lambda e, kc=kc, e_=e_, pbg=pbg, n=n: e.matmul(
                        pbg[:, :n], lhsT=win[:, kc, e_ * 128:(e_ + 1) * 128], rhs=u[:, kc, 1:n + 1],
                        start=(kc == 0), stop=(kc == 7)),
                        reads=["sc_win", ("sc_u", kc)], writes=[("pb", 4 + (e_ % 2))])
                P.dve(lambda e, pbg=pbg, e_=e_, n=n: e.tensor_tensor(out=pin[:, e_, :n], in0=pbg[:, :n], in1=cv[:, :n],
                                                                     op=ALU.mult),
                      reads=[("pb", 4 + (e_ % 2)), "sc_cv"], writes=[("sc_pin", e_)])
            finals += proj_ln(kb, lb, pin, lambda kc: ("sc_pin", kc), 8, wout, "sc_wout",
                              lambda o, n=n: ht[:, o, 1:n + 1], "sc_ht", li, n, col,
                              lambda t0=t0, n=n: hT_tile(h_out, t0, n), hres(h_out_res, t0, n))
    kb.P.barrier()
    return finals


def attention_mixer(kb, li, h_in, h_in_res, h_out, h_out_res, wqkv_d, wsw_d, wo_d, rope_d, tiles):
    nc, P = kb.nc, kb.P
    finals = []
    with contextlib.ExitStack() as es0:
      A0 = lambda name, shape, dt: es0.enter_context(nc.sbuf_tensor(UN(name), shape, dt))
      KT = A0("at_KT", [128, 4, LT], BF16)
      Vt = A0("at_Vt", [128, LT // 128, 256], BF16)
      maskb = A0("at_mask", [128, 384], F32)
      P.pool(lambda e: e.memset(maskb[:], 0.0), writes=["at_mask"])
      P.pool(lambda e: e.affine_select(out=maskb[:], in_=maskb[:], pattern=[[1, 384]], compare_op=ALU.is_ge,
                                       fill=NEG, base=0, channel_multiplier=-1),
             reads=["at_mask"], writes=["at_mask"])
      P.pool(lambda e: e.affine_select(out=maskb[:], in_=maskb[:], pattern=[[-1, 384]], compare_op=ALU.is_ge,
                                       fill=NEG, base=256, channel_multiplier=1),
             reads=["at_mask"], writes=["at_mask"])
      def phase1():
          with contextlib.ExitStack() as es:
            A = lambda name, shape, dt: es.enter_context(nc.sbuf_tensor(UN(name), shape, dt))
            wkd = A("at_wkd", [128, 8, 4, 128], BF16)
            wksd = A("at_wksd", [128, 8, 4, 128], BF16)
            wv = A("at_wv", [128, 8, 256], BF16)
            wst = A("at_wst", [128, 2, 512], F32)
            load_weight_bf16(kb, wst, wv, "at_wv", wqkv_d[0], 8, 256, 1280)
            for (w_ap, dst, rd) in ((wqkv_d[0], wkd, "at_wkd"), (wsw_d, wksd, "at_wksd")):
                wvw = w_ap.rearrange("(kc p) n -> p kc n", p=128)
                for kc in range(8):
                    s = kb.uid % 2
                    kb.uid += 1
                    P.dma(wst[:, s, :256], wvw[:, kc, 1024:1280], writes=[("wstage", s)])
                    for half in range(2):
                        P.pool(lambda e, dst=dst, kc=kc, half=half, s=s: e.tensor_copy(
                            out=dst[:, kc, :, half * 64:(half + 1) * 64],
                            in_=wst[:, s, :256].rearrange("p (g d) -> p g d", g=4)),
                            reads=[("wstage", s)], writes=[rd])
            ht = A("at_ht", [128, 8, 512], F32)
            u = A("at_u", [128, 8, 512], BF16)
            rp = A("at_rp", [128, 4, 512], F32)
            t1 = A("at_t1", [128, 2, 512], F32)
            t2 = A("at_t2", [128, 2, 512], F32)
            cnt = 0
            for (t0, n, col) in tiles:
                s0, s1 = (0, NCTX) if col == 1 else (NCTX, LT)
                load_mod_tile(kb, ht, u, "at", h_in, h_in_res, li, 1, t0, n, col, s0, s1, 0)
                if col == 0:
                    P.dma(rp[:, 2:4, :n], rope_d[2:4, :, t0 - NCTX:t0 - NCTX + n].rearrange("a p t -> p a t"),
                          writes=["at_rp"])
                for g in range(4):
                    s = cnt % 2
                    cnt += 1
                    pk, pks = kb.pb[s], kb.pb[2 + s]
                    for kc in range(8):
                        P.pe(lambda e, kc=kc, g=g, pk=pk, n=n: e.matmul(pk[:, :n], lhsT=wkd[:, kc, g, :], rhs=u[:, kc, :n],
                                                                        start=(kc == 0), stop=(kc == 7)),
                             reads=["at_wkd", ("at_u", kc)], writes=[("pb", s)])
                    if col == 0:
                        for kc in range(8):
                            P.pe(lambda e, kc=kc, g=g, pks=pks, n=n: e.matmul(pks[:, :n], lhsT=wksd[:, kc, g, :],
                                                                              rhs=u[:, kc, :n], start=(kc == 0),
                                                                              stop=(kc == 7)),
                                 reads=["at_wksd", ("at_u", kc)], writes=[("pb", 2 + s)])
                        P.dve(lambda e, pk=pk, s=s, n=n: e.tensor_tensor(out=t1[:, s, :n], in0=pk[:, :n], in1=rp[:, 2, :n],
                                                                         op=ALU.mult),
                              reads=[("pb", s), "at_rp"], writes=[("at_t1", s)])
                        P.dve(lambda e, pks=pks, s=s, n=n: e.tensor_tensor(out=t2[:, s, :n], in0=pks[:, :n],
                                                                           in1=rp[:, 3, :n], op=ALU.mult),
                              reads=[("pb", 2 + s), "at_rp"], writes=[("at_t2", s)])
                        P.pool(lambda e, g=g, s=s, t0=t0, n=n: e.tensor_tensor(out=KT[:, g, t0:t0 + n], in0=t1[:, s, :n],
                                                                              in1=t2[:, s, :n], op=ALU.add),
                               reads=[("at_t1", s), ("at_t2", s)], writes=[("at_KT", t0 // 512 if col == 0 else "c")])
                    else:
                        P.act(lambda e, pk=pk, g=g, t0=t0, n=n: e.activation(out=KT[:, g, t0:t0 + n], in_=pk[:, :n],
                                                                             func=AF.Identity),
                              reads=[("pb", s)], writes=[("at_KT", "c")])
                for bi in range(n // 128):
                    pv = kb.pb[4 + (bi % 2)]
                    for kc in range(8):
                        P.pe(lambda e, kc=kc, bi=bi, pv=pv: e.matmul(pv[:, :256], lhsT=u[:, kc, bi * 128:(bi + 1) * 128],
                                                                     rhs=wv[:, kc, :], start=(kc == 0), stop=(kc == 7)),
                             reads=["at_wv", ("at_u", kc)], writes=[("pb", 4 + (bi % 2))])
                    blk = t0 // 128 + bi
                    P.act(lambda e, pv=pv, blk=blk: e.activation(out=Vt[:, blk, :], in_=pv[:, :256], func=AF.Identity),
                          reads=[("pb", 4 + (bi % 2))], writes=[("at_Vt", blk)])
      phase1()
      P.barrier()
      tiles = [(0, 256, 1)] + [(NCTX + 256 * i, 256, 0) for i in range(SEQ // 256)] if len(tiles) == 9 else tiles
      def phase2(tiles=tiles):
          with contextlib.ExitStack() as es:
            A = lambda name, shape, dt: es.enter_context(nc.sbuf_tensor(UN(name), shape, dt))
            NQ = 256
            wq = A("at_wq", [128, 8, 1024], BF16)
            wqs = A("at_wqs", [128, 8, 1024], BF16)
            wo = A("at_wo", [128, 8, 1024], BF16)
            wst = A("at_wst2", [128, 2, 1024], F32)
            load_weight_bf16(kb, wst, wq, "at_wq", wqkv_d[0], 8, 1024, 0)
            load_weight_bf16(kb, wst, wqs, "at_wqs", wsw_d, 8, 1024, 0)
            load_weight_bf16(kb, wst, wo, "at_wo", wo_d[0], 8, 1024, 0)
            ht = A("at_ht2", [128, 8, NQ], F32)
            u = A("at_u2", [128, 8, NQ], BF16)
            rp = A("at_rp2", [128, 4, NQ], F32)
            t1 = A("at_t12", [128, 2, NQ], F32)
            t2 = A("at_t22", [128, 2, NQ], F32)
            QT = A("at_QT", [128, 8, NQ], BF16)
            Sm = A("at_Sm", [128, 2, 384], F32)
            pbuf = A("at_p", [128, 2, 640], BF16)
            PT = A("at_PT", [128, 2, 5, 128], BF16)
            Osb = A("at_O", [128, 1024], BF16)
            OT = A("at_OT", [128, 8, NQ], BF16)
            st = A("at_st", [128, 2, 8], F32)
            lb = LNBufs(nc, es, NQ, "a")
            hcnt = 0
            for (t0, n, col) in tiles:
                s0, s1 = (0, NCTX) if col == 1 else (NCTX, LT)
                load_mod_tile(kb, ht, u, "at", h_in, h_in_res, li, 1, t0, n, col, s0, s1, 0)
                if col == 0:
                    P.dma(rp[:, 0:2, :n], rope_d[0:2, :, t0 - NCTX:t0 - NCTX + n].rearrange("a p t -> p a t"),
                          writes=["at_rp"])
                for c in range(8):
                    s = c % 2
                    pq, pqs = kb.pb[4 + s], kb.pb[6 + s]
                    for kc in range(8):
                        P.pe(lambda e, kc=kc, c=c, pq=pq, n=n: e.matmul(pq[:, :n], lhsT=wq[:, kc, c * 128:(c + 1) * 128],
                                                                        rhs=u[:, kc, :n], start=(kc == 0), stop=(kc == 7)),
                             reads=["at_wq", ("at_u", kc)], writes=[("pb", 4 + s)])
                    if col == 0:
                        for kc in range(8):
                            P.pe(lambda e, kc=kc, c=c, pqs=pqs, n=n: e.matmul(pqs[:, :n],
                                                                              lhsT=wqs[:, kc, c * 128:(c + 1) * 128],
                                                                              rhs=u[:, kc, :n], start=(kc == 0),
                                                                              stop=(kc == 7)),
                                 reads=["at_wqs", ("at_u", kc)], writes=[("pb", 6 + s)])
                        P.dve(lambda e, pq=pq, s=s, n=n: e.tensor_tensor(out=t1[:, s, :n], in0=pq[:, :n], in1=rp[:, 0, :n],
                                                                         op=ALU.mult),
                              reads=[("pb", 4 + s), "at_rp"], writes=[("at_t1", s)])
                        P.dve(lambda e, pqs=pqs, s=s, n=n: e.tensor_tensor(out=t2[:, s, :n], in0=pqs[:, :n],
                                                                           in1=rp[:, 1, :n], op=ALU.mult),
                              reads=[("pb", 6 + s), "at_rp"], writes=[("at_t2", s)])
                        P.pool(lambda e, c=c, s=s, n=n: e.tensor_tensor(out=QT[:, c, :n], in0=t1[:, s, :n],
                                                                       in1=t2[:, s, :n], op=ALU.add),
                               reads=[("at_t1", s), ("at_t2", s)], writes=[("at_QT", c)])
                    else:
                        P.act(lambda e, pq=pq, c=c, n=n: e.activation(out=QT[:, c, :n], in_=pq[:, :n], func=AF.Identity,
                                                                      scale=0.125),
                              reads=[("pb", 4 + s)], writes=[("at_QT", c)])
                for qb in range(n // 128):
                    if col == 0:
                        L = (t0 - NCTX) // 128 + qb
                        b0, b1 = max(L - 1, 0), min(L + 2, SEQ // 128)
                        kw0, kw1 = NCTX + 128 * b0, NCTX + 128 * b1
                        nw = kw1 - kw0
                        moff = 128 * (b0 - (L - 1))
                        kres = sorted(set(("at_KT", (kk - NCTX) // 512) for kk in (kw0, kw1 - 1))) + [("at_KT", "c")]
                        vblocks = list(range(kw0 // 128, kw1 // 128)) + [0, 1]
                    else:
                        nw = 0
                        kres = [("at_KT", "c")]
                        vblocks = [0, 1]
                    nk = nw + 256
                    for h in range(16):
                        c, base, g = h // 2, (h % 2) * 64, h // 4
                        s = hcnt % 2
                        hcnt += 1
                        pS, pS2, pTO = kb.pb[2 * s], kb.pb[2 * s + 1], kb.pb[4 + s]
                        q_ap = QT[base:base + 64, c, qb * 128:(qb + 1) * 128]
                        if nw:
                            P.pe(lambda e, pS=pS, q_ap=q_ap, base=base, g=g, kw0=kw0, kw1=kw1, nw=nw: e.matmul(
                                pS[:, :nw], lhsT=q_ap, rhs=KT[base:base + 64, g, kw0:kw1], start=True, stop=True),
                                reads=[("at_QT", c)] + kres, writes=[("pb", 2 * s)])
                        P.pe(lambda e, pS2=pS2, q_ap=q_ap, base=base, g=g: e.matmul(
                            pS2[:, :256], lhsT=q_ap, rhs=KT[base:base + 64, g, 0:256], start=True, stop=True),
                            reads=[("at_QT", c), ("at_KT", "c")], writes=[("pb", 2 * s + 1)])
                        m1, m2, mx, negm, rs1, rs2, esk, den = (st[:, s, i:i + 1] for i in range(8))
                        sres = ("at_st", s)
                        if nw:
                            P.dve(lambda e, pS=pS, s=s, nw=nw, moff=moff: e.tensor_tensor(
                                out=Sm[:, s, :nw], in0=pS[:, :nw], in1=maskb[:, moff:moff + nw], op=ALU.add),
                                reads=[("pb", 2 * s), "at_mask"], writes=[("at_Sm", s)])
                            P.dve(lambda e, s=s, nw=nw, m1=m1: e.reduce_max(out=m1, in_=Sm[:, s, :nw], axis=AX.X),
                                  reads=[("at_Sm", s)], writes=[sres])
                        P.dve(lambda e, pS2=pS2, m2=m2: e.reduce_max(out=m2, in_=pS2[:, :256], axis=AX.X),
                              reads=[("pb", 2 * s + 1)], writes=[sres])
                        sink_ap = kb.v("at_sink", h)
                        if nw:
                            P.dve(lambda e, m1=m1, m2=m2, mx=mx, sink_ap=sink_ap: e.tensor_scalar(
                                out=mx, in0=m1, scalar1=m2, scalar2=sink_ap, op0=ALU.max, op1=ALU.max),
                                reads=[sres, "vecs"], writes=[sres])
                        else:
                            P.dve(lambda e, m2=m2, mx=mx, sink_ap=sink_ap: e.tensor_scalar(
                                out=mx, in0=m2, scalar1=sink_ap, scalar2=None, op0=ALU.max),
                                reads=[sres, "vecs"], writes=[sres])
                        P.dve(lambda e, mx=mx, negm=negm: e.tensor_scalar(out=negm, in0=mx, scalar1=-1.0, scalar2=None,
                                                                          op0=ALU.mult),
                              reads=[sres], writes=[sres])
                        if nw:
                            P.act(lambda e, s=s, nw=nw, negm=negm, rs1=rs1: e.activation(
                                out=pbuf[:, s, :nw], in_=Sm[:, s, :nw], func=AF.Exp, bias=negm, accum_out=rs1),
                                reads=[("at_Sm", s), sres], writes=[("at_p", s), ("at_st2", s)])
                        P.act(lambda e, s=s, nw=nw, pS2=pS2, negm=negm, rs2=rs2: e.activation(
                            out=pbuf[:, s, nw:nw + 256], in_=pS2[:, :256], func=AF.Exp, bias=negm, accum_out=rs2),
                            reads=[("pb", 2 * s + 1), sres], writes=[("at_p", s), ("at_st2", s)])
                        P.act(lambda e, negm=negm, esk=esk, sink_ap=sink_ap: e.activation(
                            out=esk, in_=negm, func=AF.Exp, bias=sink_ap),
                            reads=[sres, "vecs"], writes=[("at_st2", s)])
                        if nw:
                            P.dve(lambda e, rs1=rs1, rs2=rs2, esk=esk, den=den: e.tensor_scalar(
                                out=den, in0=rs1, scalar1=rs2, scalar2=esk, op0=ALU.add, op1=ALU.add),
                                reads=[("at_st2", s)], writes=[("at_st3", s)])
                        else:
                            P.dve(lambda e, rs2=rs2, esk=esk, den=den: e.tensor_scalar(
                                out=den, in0=rs2, scalar1=esk, scalar2=None, op0=ALU.add),
                                reads=[("at_st2", s)], writes=[("at_st3", s)])
                        P.dve(lambda e, den=den: e.reciprocal(out=den, in_=den), reads=[("at_st3", s)],
                              writes=[("at_st3", s)])
                        nkb = nk // 128
                        ptp = pTO[:].bitcast(BF16)
                        for kk in range(nkb):
                            P.pe(lambda e, kk=kk, s=s, ptp=ptp: e.transpose(
                                out=ptp[:, kk * 128:(kk + 1) * 128], in_=pbuf[:, s, kk * 128:(kk + 1) * 128],
                                identity=kb.identb[:]),
                                reads=[("at_p", s), "identb"], writes=[("pb", 4 + s)])
                        P.dve(lambda e, s=s, nkb=nkb, ptp=ptp: e.tensor_copy(
                            out=PT[:, s, :nkb, :].rearrange("p a b -> p (a b)"), in_=ptp[:, :nkb * 128]),
                            reads=[("pb", 4 + s)], writes=[("at_PT", s)])
                        po = pTO[:, 384:448]
                        for kk in range(nkb):
                            vb = vblocks[kk]
                            P.pe(lambda e, kk=kk, s=s, po=po, vb=vb, g=g, nkb=nkb: e.matmul(
                                po, lhsT=PT[:, s, kk, :], rhs=Vt[:, vb, g * 64:(g + 1) * 64], start=(kk == 0),
                                stop=(kk == nkb - 1)),
                                reads=[("at_PT", s), ("at_Vt", vb)], writes=[("pb", 4 + s)])
                        P.act(lambda e, po=po, h=h, den=den: e.activation(out=Osb[:, h * 64:(h + 1) * 64], in_=po,
                                                                          func=AF.Identity, scale=den),
                              reads=[("pb", 4 + s), ("at_st3", s)], writes=["at_O"])
                    pot = kb.pb[6][:].bitcast(BF16)
                    for c in range(8):
                        P.pe(lambda e, c=c, pot=pot: e.transpose(out=pot[:, c * 128:(c + 1) * 128],
                                                                 in_=Osb[:, c * 128:(c + 1) * 128], identity=kb.identb[:]),
                             reads=["at_O", "identb"], writes=[("pb", 6)])
                    P.dve(lambda e, qb=qb, pot=pot: e.tensor_copy(
                        out=OT[:, :, qb * 128:(qb + 1) * 128], in_=pot[:].rearrange("p (c t) -> p c t", c=8)),
                        reads=[("pb", 6)], writes=["at_OT"])
                finals.extend(proj_ln(kb, lb, OT, lambda kc: "at_OT", 8, wo, "at_wo",
                                  lambda o, n=n: ht[:, o, :n], "at_ht", li, n, col,
                                  lambda t0=t0, n=n: hT_tile(h_out, t0, n), hres(h_out_res, t0, n)))
      phase2()
    kb.P.barrier()
    return finals

import math

NCH = LT // 128
E2 = 2048
DH = 512
ORDER = {0: list(range(NCH)), 1: [1, 0] + list(range(NCH - 1, 1, -1))}
PT256 = [(0, 256, 1)] + [(NCTX + 256 * i, 256, 0) for i in range(SEQ // 256)]


def host_bd(w_qkv_j):
    out = np.zeros((3, 128, 16, 128), np.float32)
    w = np.asarray(w_qkv_j, np.float32).reshape(3, 16, 32, 4, 4)
    for m in range(32):
        out[:, 4 * m:4 * m + 4, :, 4 * m:4 * m + 4] = np.transpose(w[:, :, m], (0, 2, 1, 3))
    return out


def host_wif(w_if_j):
    w = np.asarray(w_if_j, np.float32).reshape(2, 48, 128, 8)
    return np.ascontiguousarray(np.transpose(w, (2, 1, 0, 3)).reshape(128, 48, 16))


def mlstm_scratch(kb):
    if hasattr(kb, "ml_scr"):
        return kb.ml_scr
    s = {}
    s["qTb"] = kb.dscr("ml_qTb", [NCH, 128, 16, 128], BF16)
    s["kTb"] = kb.dscr("ml_kTb", [NCH, 128, 16, 128], BF16)
    s["xcTb"] = kb.dscr("ml_xcTb", [NCH, 128, 16, 128], BF16)
    s["ktok"] = kb.dscr("ml_ktok", [LT, E2], BF16)
    s["vtok"] = kb.dscr("ml_vtok", [LT, E2], BF16)
    s["sztok"] = kb.dscr("ml_sztok", [LT, E2], F32)
    s["gtok"] = kb.dscr("ml_gtok", [LT, 16], F32)
    s["hftok"] = kb.dscr("ml_hftok", [LT, E2], F32)
    kb.ml_scr = s
    return s


def mlstm_prep(kb, li, j, h_in, h_in_res, wup_d, bd_d, wif_d):
    nc, P = kb.nc, kb.P
    S = mlstm_scratch(kb)
    with contextlib.ExitStack() as es:
        A = lambda name, shape, dt: es.enter_context(nc.sbuf_tensor(UN(name), shape, dt))
        wst = A("mp_wst", [128, 2, 2048], F32)
        wup = A("mp_wup", [128, 8, 4096], BF16)
        load_weight_bf16(kb, wst, wup, "mp_wup", wup_d[j], 8, 4096)
        bd = A("mp_bd", [128, 3, 16, 128], BF16)
        for x in range(3):
            s = kb.uid % 2
            kb.uid += 1
            P.dma(wst[:, s, :], bd_d[j, x].rearrange("p e o -> p (e o)"), writes=[("wstage", s)])
            P.pool(lambda e, x=x, s=s: e.tensor_copy(out=bd[:, x].rearrange("p e o -> p (e o)"), in_=wst[:, s, :]),
                   reads=[("wstage", s)], writes=["mp_bd"])
        wif = A("mp_wif", [128, 48, 16], BF16)
        s = kb.uid % 2
        kb.uid += 1
        P.dma(wst[:, s, :768], wif_d[j].rearrange("p r g -> p (r g)"), writes=[("wstage", s)])
        P.pool(lambda e, s=s: e.tensor_copy(out=wif[:].rearrange("p r g -> p (r g)"), in_=wst[:, s, :768]),
               reads=[("wstage", s)], writes=["mp_wif"])
        n = 256
        ht = A("mp_ht", [128, 8, n + 2], F32)
        u = A("mp_u", [128, 8, n + 2], BF16)
        xms = A("mp_xm", [128, 2, n + 2], F32)
        cv = A("mp_cv", [128, 2, n], F32)
        xmb = A("mp_xmb", [128, 16, n], BF16)
        xc = A("mp_xc", [128, 2, 16, 128], BF16)
        qTs = A("mp_qT", [128, 2, 16, 128], BF16)
        kTs = A("mp_kT", [128, 2, 16, 128], BF16)
        vTs = A("mp_vT", [128, 2, n], BF16)
        kst = A("mp_kst", [128, 2, E2], BF16)
        szs = A("mp_szs", [128, 2, E2], F32)
        gst = A("mp_gst", [128, 2, 16], F32)
        cnt = 0
        for (t0, n_, col) in PT256:
            s0, s1 = (0, NCTX) if col == 1 else (NCTX, LT)
            load_mod_tile(kb, ht, u, "mp", h_in, h_in_res, li, 1, t0, n, col, s0, s1, 1)
            c0 = t0 // 128
            for e_ in range(16):
                s = e_ % 2
                px = kb.pb[s]
                for kc in range(8):
                    P.pe(lambda e, kc=kc, e_=e_, px=px: e.matmul(px[:, :n + 2], lhsT=wup[:, kc, e_ * 128:(e_ + 1) * 128],
                                                                  rhs=u[:, kc, :], start=(kc == 0), stop=(kc == 7)),
                         reads=["mp_wup", ("mp_u", kc)], writes=[("pb", s)])
                P.act(lambda e, px=px, s=s: e.activation(out=xms[:, s, :], in_=px[:, :n + 2], func=AF.Identity),
                      reads=[("pb", s)], writes=[("mp_xm", s)])
                P.pool(lambda e, s=s, e_=e_: e.tensor_copy(out=xmb[:, e_, :], in_=xms[:, s, 1:n + 1]),
                       reads=[("mp_xm", s)], writes=[("mp_xmb", e_)])
                w0, w1, w2 = (kb.v("ml_conv_w", (j * 3 + k_) * 16 + e_) for k_ in range(3))
                bcv = kb.v("ml_conv_b", j * 16 + e_)
                P.dve(lambda e, s=s, w1=w1, bcv=bcv: e.tensor_scalar(out=cv[:, s, :], in0=xms[:, s, 1:n + 1], scalar1=w1,
                                                                     scalar2=bcv, op0=ALU.mult, op1=ALU.add),
                      reads=[("mp_xm", s), "vecs"], writes=[("mp_cv", s)])
                P.dve(lambda e, s=s, w0=w0: e.scalar_tensor_tensor(out=cv[:, s, :], in0=xms[:, s, 0:n], scalar=w0,
                                                                   in1=cv[:, s, :], op0=ALU.mult, op1=ALU.add),
                      reads=[("mp_xm", s), ("mp_cv", s)], writes=[("mp_cv", s)])
                P.dve(lambda e, s=s, w2=w2: e.scalar_tensor_tensor(out=cv[:, s, :], in0=xms[:, s, 2:n + 2], scalar=w2,
                                                                   in1=cv[:, s, :], op0=ALU.mult, op1=ALU.add),
                      reads=[("mp_xm", s), ("mp_cv", s)], writes=[("mp_cv", s)])
                P.act(lambda e, s=s, e_=e_: e.activation(out=xc[:, :, e_, :],
                                                         in_=cv[:, s, :].rearrange("p (c t) -> p c t", c=2),
                                                         func=AF.Silu),
                      reads=[("mp_cv", s)], writes=[("mp_xc", e_)])
                for x, (dst, dres) in enumerate(((qTs, "mp_qT"), (kTs, "mp_kT"), (vTs, "mp_vT"))):
                    pq = kb.pb[2 + (cnt % 2)]
                    pqi = 2 + (cnt % 2)
                    cnt += 1
                    if x < 2:
                        P.pe(lambda e, x=x, e_=e_, pq=pq: e.matmul(pq[:, :n], lhsT=bd[:, x, e_, :], rhs=xc[:, :, e_, :],
                                                                   start=True, stop=True),
                             reads=["mp_bd", ("mp_xc", e_)], writes=[("pb", pqi)])
                        P.act(lambda e, dst=dst, e_=e_, pq=pq: e.activation(
                            out=dst[:, :, e_, :], in_=pq[:, :n].rearrange("p (c t) -> p c t", c=2), func=AF.Identity),
                            reads=[("pb", pqi)], writes=[(dres, e_)])
                        src_fn = lambda cb, dst=dst, e_=e_: dst[:, cb, e_, :]
                        src_res = (dres, e_)
                    else:
                        P.pe(lambda e, e_=e_, pq=pq: e.matmul(pq[:, :n], lhsT=bd[:, 2, e_, :], rhs=xmb[:, e_, :],
                                                              start=True, stop=True),
                             reads=["mp_bd", ("mp_xmb", e_)], writes=[("pb", pqi)])
                        vs = e_ % 2
                        P.act(lambda e, vs=vs, pq=pq: e.activation(out=vTs[:, vs, :], in_=pq[:, :n], func=AF.Identity),
                              reads=[("pb", pqi)], writes=[("mp_vT", vs)])
                        src_fn = lambda cb, vs=vs: vTs[:, vs, cb * 128:(cb + 1) * 128]
                        src_res = ("mp_vT", vs)
                    for cb in range(2):
                        P.pe(lambda e, cb=cb, x=x, e_=e_, src_fn=src_fn: e.matmul(
                            kb.pb[6 + cb][:, :16], lhsT=src_fn(cb), rhs=wif[:, x * 16 + e_, :],
                            start=(x == 0 and e_ == 0), stop=(x == 2 and e_ == 15)),
                            reads=[src_res, "mp_wif"], writes=[("pb", 6 + cb)])
            for cb in range(2):
                P.dve(lambda e, cb=cb: e.tensor_tensor(out=gst[:, cb, :], in0=kb.pb[6 + cb][:, :16],
                                                       in1=kb.v("ml_b_if", j * 16, 16), op=ALU.add),
                      reads=[("pb", 6 + cb), "vecs"], writes=[("mp_gst", cb)])
                P.dma(S["gtok"][t0 + cb * 128:t0 + (cb + 1) * 128, :], gst[:, cb, :], reads=[("mp_gst", cb)],
                      writes=[("gtok", c0 + cb)], q="act")
            for cb in range(2):
                for (x, dname) in ((1, "ktok"), (2, "vtok")):
                    ks = x - 1
                    for hh in range(4):
                        pk = kb.pb[4 + (hh % 2)]
                        for jj in range(4):
                            e_ = 4 * hh + jj
                            lhs = xc[:, cb, e_, :] if x == 1 else xmb[:, e_, cb * 128:(cb + 1) * 128]
                            lres = ("mp_xc", e_) if x == 1 else ("mp_xmb", e_)
                            P.pe(lambda e, pk=pk, jj=jj, lhs=lhs, x=x, e_=e_: e.matmul(
                                pk[:, jj * 128:(jj + 1) * 128], lhsT=lhs, rhs=bd[:, x, e_, :], start=True, stop=True),
                                reads=[lres, "mp_bd"], writes=[("pb", 4 + (hh % 2))])
                        P.act(lambda e, pk=pk, ks=ks, hh=hh: e.activation(out=kst[:, ks, hh * 512:(hh + 1) * 512],
                                                                          in_=pk[:, :512], func=AF.Identity),
                              reads=[("pb", 4 + (hh % 2))], writes=[("mp_kst", ks)])
                    P.dma(S[dname][t0 + cb * 128:t0 + (cb + 1) * 128, :], kst[:, ks, :], reads=[("mp_kst", ks)],
                          writes=[(dname, c0 + cb)], q="act")
                for hh in range(4):
                    pz = kb.pb[hh % 2]
                    for kc in range(8):
                        P.pe(lambda e, pz=pz, kc=kc, cb=cb, hh=hh: e.matmul(
                            pz[:, :512], lhsT=u[:, kc, 1 + cb * 128:1 + (cb + 1) * 128],
                            rhs=wup[:, kc, E2 + hh * 512:E2 + (hh + 1) * 512], start=(kc == 0), stop=(kc == 7)),
                            reads=[("mp_u", kc), "mp_wup"], writes=[("pb", hh % 2)])
                    P.act(lambda e, pz=pz, cb=cb, hh=hh: e.activation(out=szs[:, cb, hh * 512:(hh + 1) * 512],
                                                                      in_=pz[:, :512], func=AF.Sigmoid),
                          reads=[("pb", hh % 2)], writes=[("mp_szs", cb)])
                P.dma(S["sztok"][t0 + cb * 128:t0 + (cb + 1) * 128, :], szs[:, cb, :], reads=[("mp_szs", cb)],
                      writes=[("sztok", c0 + cb)], q="act")
            for (src, dname, rname) in ((qTs, "qTb", "mp_qT"), (kTs, "kTb", "mp_kT"), (xc, "xcTb", "mp_xc")):
                P.dma(S[dname][c0:c0 + 2].rearrange("c p e t -> p c e t"), src[:],
                      reads=[(rname, e_) for e_ in range(16)], writes=[(dname, c0), (dname, c0 + 1)], q="act")
    P.barrier()


def mlstm_gates(kb, es, tag):
    nc, P = kb.nc, kb.P
    S = mlstm_scratch(kb)
    A = lambda name, shape, dt: es.enter_context(nc.sbuf_tensor(UN(name), shape, dt))
    tri = A("mg_tri", [128, 2, 128], F32)
    pers = {}
    for d_ in range(2):
        pers[d_] = {nm: A("mg_%s%d" % (nm, d_), [128, NCH, 4], F32) for nm in ("w", "wa", "e", "am")}
    es_t = contextlib.ExitStack()
    At = lambda name, shape, dt: es_t.enter_context(nc.sbuf_tensor(UN(name), shape, dt))
    G = At("mg_G", [128, NCH, 16], F32)
    P.dma(G[:], S["gtok"].rearrange("(c p) g -> p c g", p=128), reads=[("gtok", c) for c in range(NCH)], writes=["mg_G"])
    ones = At("mg_ones", [128, 128], F32)
    P.pool(lambda e: e.memset(ones[:], 1.0), writes=["mg_ones"])
    P.pool(lambda e: e.memset(tri[:], 1.0), writes=["mg_tri"])
    P.pool(lambda e: e.affine_select(out=tri[:, 0, :], in_=tri[:, 0, :], pattern=[[1, 128]], compare_op=ALU.is_ge,
                                     fill=0.0, base=0, channel_multiplier=-1), reads=["mg_tri"], writes=["mg_tri"])
    P.pool(lambda e: e.affine_select(out=tri[:, 1, :], in_=tri[:, 1, :], pattern=[[-1, 128]], compare_op=ALU.is_ge,
                                     fill=0.0, base=0, channel_multiplier=1), reads=["mg_tri"], writes=["mg_tri"])
    NC4 = NCH * 4
    out = {"tri": tri}
    tmp = At("mg_tmp", [128, 6, NCH, 4], F32)
    X = At("mg_X", [128, NCH, 4], F32)
    dg = At("mg_dg", [68, 68], F32)
    xm = At("mg_xm", [68, 1], F32)
    for d in range(2):
        IG = G[:, :, d * 8:d * 8 + 4]
        FG = G[:, :, d * 8 + 4:d * 8 + 8]
        ab, ex, lf, bsb, gsb, x = (tmp[:, i] for i in range(6))
        R = lambda i: ("mg_tmp", i)
        P.act(lambda e, FG=FG, ab=ab: e.activation(out=ab, in_=FG, func=AF.Abs),
              reads=["mg_G"], writes=[R(0)])
        P.act(lambda e, ab=ab, ex=ex: e.activation(out=ex, in_=ab, func=AF.Exp, scale=-1.0), reads=[R(0)], writes=[R(1)])
        P.act(lambda e, ex=ex: e.activation(out=ex, in_=ex, func=AF.Ln, bias=kb.epsc[:, 3:4]), reads=[R(1), "epsc"],
              writes=[R(1)])
        P.dve(lambda e, FG=FG, ab=ab: e.tensor_single_scalar(out=ab, in_=FG, scalar=0.0, op=ALU.min),
              reads=["mg_G", R(0)], writes=[R(0)])
        P.dve(lambda e, ab=ab, ex=ex, lf=lf: e.tensor_tensor(out=lf, in0=ab, in1=ex, op=ALU.subtract),
              reads=[R(0), R(1)], writes=[R(2)])
        lf2 = lf.rearrange("p c h -> p (c h)")
        P.pe(lambda e, d=d, lf2=lf2: e.matmul(kb.pb[0][:, :NC4], lhsT=tri[:, d, :], rhs=lf2, start=True, stop=True),
             reads=["mg_tri", R(2)], writes=[("pb", 0)])
        P.pe(lambda e, lf2=lf2: e.matmul(kb.pb[1][:, :NC4], lhsT=ones[:], rhs=lf2, start=True, stop=True),
             reads=["mg_ones", R(2)], writes=[("pb", 1)])
        P.act(lambda e, bsb=bsb: e.activation(out=bsb.rearrange("p c h -> p (c h)"), in_=kb.pb[0][:, :NC4],
                                              func=AF.Identity), reads=[("pb", 0)], writes=[R(3)])
        P.act(lambda e, gsb=gsb: e.activation(out=gsb.rearrange("p c h -> p (c h)"), in_=kb.pb[1][:, :NC4],
                                              func=AF.Identity), reads=[("pb", 1)], writes=[R(4)])
        P.dve(lambda e, IG=IG, bsb=bsb, x=x: e.tensor_tensor(out=x, in0=IG, in1=bsb, op=ALU.subtract),
              reads=["mg_G", R(3)], writes=[R(5)])
        x2 = x.rearrange("p c h -> p (c h)")
        X2 = X[:].rearrange("p c h -> p (c h)")
        for half in range(2):
            pt = kb.pb[2 + half]
            P.pe(lambda e, half=half, pt=pt, x2=x2: e.transpose(out=pt[0:68, 0:128], in_=x2[:, half * 68:(half + 1) * 68],
                                                                identity=kb.ident[:]),
                 reads=[R(5), "ident"], writes=[("pb", 2 + half)])
            P.dve(lambda e, pt=pt: e.reduce_max(out=xm[:], in_=pt[0:68, 0:128], axis=AX.X), reads=[("pb", 2 + half)],
                  writes=["mg_xm"])
            P.dve(lambda e: e.tensor_scalar(out=dg[:], in0=kb.ident[0:68, 0:68], scalar1=xm[:, 0:1], scalar2=None,
                                            op0=ALU.mult), reads=["mg_xm", "ident"], writes=["mg_dg"])
            pr = kb.pb[4 + half]
            P.pe(lambda e, pr=pr: e.matmul(pr[:, 0:68], lhsT=ones[0:68, :], rhs=dg[:], start=True, stop=True),
                 reads=["mg_ones", "mg_dg"], writes=[("pb", 4 + half)])
            P.act(lambda e, half=half, pr=pr, X2=X2: e.activation(out=X2[:, half * 68:(half + 1) * 68], in_=pr[:, 0:68],
                                                                  func=AF.Identity),
                  reads=[("pb", 4 + half)], writes=["mg_X"])
        M = At("mg_M%d" % d, [128, NCH, 4], F32)
        mseq = At("mg_ms%d" % d, [128, NCH + 1, 4], F32)
        am = pers[d]["am"]
        P.dve(lambda e, mseq=mseq: e.memset(mseq[:, 0, :], 0.0), writes=["mg_mseq"])
        for k, c in enumerate(ORDER[d]):
            P.dve(lambda e, k=k, c=c, M=M, mseq=mseq: e.tensor_tensor(out=M[:, c, :], in0=mseq[:, k, :], in1=X[:, c, :],
                                                                      op=ALU.max),
                  reads=["mg_mseq", "mg_X"], writes=["mg_M"])
            P.dve(lambda e, k=k, c=c, M=M, mseq=mseq, gsb=gsb: e.tensor_tensor(out=mseq[:, k + 1, :], in0=M[:, c, :],
                                                                               in1=gsb[:, c, :], op=ALU.add),
                  reads=["mg_M", R(4)], writes=["mg_mseq"])
            P.dve(lambda e, k=k, c=c, M=M, mseq=mseq, am=am: e.tensor_tensor(out=am[:, c, :], in0=mseq[:, k, :],
                                                                             in1=M[:, c, :], op=ALU.subtract),
                  reads=["mg_M", "mg_mseq"], writes=["mg_am"])
        w, wa, ee = pers[d]["w"], pers[d]["wa"], pers[d]["e"]
        P.dve(lambda e, x=x, M=M: e.tensor_tensor(out=x, in0=x, in1=M[:], op=ALU.subtract), reads=[R(5), "mg_M"],
              writes=[R(5)])
        P.act(lambda e, x=x, w=w: e.activation(out=w[:], in_=x, func=AF.Exp, bias=kb.epsc[:, 2:3]),
              reads=[R(5), "epsc"], writes=["mg_w"])
        P.dve(lambda e, bsb=bsb, M=M: e.tensor_tensor(out=bsb, in0=bsb, in1=M[:], op=ALU.add), reads=[R(3), "mg_M"],
              writes=[R(3)])
        P.act(lambda e, bsb=bsb, ee=ee: e.activation(out=ee[:], in_=bsb, func=AF.Exp, scale=-1.0), reads=[R(3)],
              writes=["mg_e"])
        P.act(lambda e, am=am: e.activation(out=am[:], in_=am[:], func=AF.Exp), reads=["mg_am"], writes=["mg_am"])
        P.dve(lambda e, w=w, wa=wa: e.tensor_copy(out=wa[:], in_=w[:]), reads=["mg_w"], writes=["mg_wa"])
        if d == 0:
            P.dve(lambda e, w=w, wa=wa, am=am: e.tensor_tensor(out=wa[:, 0:NCH - 1, :], in0=w[:, 0:NCH - 1, :],
                                                               in1=am[:, 1:NCH, :], op=ALU.mult),
                  reads=["mg_w", "mg_am", "mg_wa"], writes=["mg_wa"])
        else:
            P.dve(lambda e, w=w, wa=wa, am=am: e.tensor_tensor(out=wa[:, 3:NCH, :], in0=w[:, 3:NCH, :],
                                                               in1=am[:, 2:NCH - 1, :], op=ALU.mult),
                  reads=["mg_w", "mg_am", "mg_wa"], writes=["mg_wa"])
            P.dve(lambda e, w=w, wa=wa, am=am: e.tensor_tensor(out=wa[:, 1, :], in0=w[:, 1, :], in1=am[:, 0, :],
                                                               op=ALU.mult),
                  reads=["mg_w", "mg_am", "mg_wa"], writes=["mg_wa"])
            P.dve(lambda e, w=w, wa=wa, am=am: e.tensor_tensor(out=wa[:, 0, :], in0=w[:, 0, :], in1=am[:, NCH - 1, :],
                                                               op=ALU.mult),
                  reads=["mg_w", "mg_am", "mg_wa"], writes=["mg_wa"])
        out[d] = {"w": w, "wa": wa, "e": ee, "a": am}
    P.barrier()
    es_t.close()
    return out


def mlstm_scan(kb, li, j, last, gates, h_in, h_in_res, h_out, h_out_res, wdown_d, ngbc_d):
    nc, P = kb.nc, kb.P
    S = mlstm_scratch(kb)
    finals = []
    tri = gates["tri"]
    for d in range(2):
        gd = gates[d]
        order = ORDER[d]

        def run_dir(d=d, gd=gd, order=order):
            with contextlib.ExitStack() as es:
                A = lambda name, shape, dt: es.enter_context(nc.sbuf_tensor(UN(name), shape, dt))
                T = "ms%d_" % d
                Cs = A(T + "Cs", [128, 4, 4, 512], F32)
                Cb = A(T + "Cb", [128, 4, 4, 512], BF16)
                ns = A(T + "ns", [128, 4, 4], F32)
                nb = A(T + "nb", [128, 4, 4], BF16)
                onesb = A(T + "onesb", [128, 2], BF16)
                qT = A(T + "qT", [128, 2, 16, 128], BF16)
                kT = A(T + "kT", [128, 2, 16, 128], BF16)
                ktk = A(T + "ktk", [128, E2], BF16)
                vtk = A(T + "vtk", [128, E2], BF16)
                AT = A(T + "AT", [128, 4, 128], BF16)
                KW = A(T + "KW", [128, 2, 512], BF16)
                hst = A(T + "hst", [128, E2], F32)
                rr = A(T + "rr", [128, 4], F32)
                P.pool(lambda e: e.memset(Cs[:], 0.0), writes=[("Cs", h) for h in range(4)])
                P.pool(lambda e: e.memset(Cb[:], 0.0), writes=[("Cb", h) for h in range(4)])
                P.pool(lambda e: e.memset(ns[:], 0.0), writes=["ns"])
                P.pool(lambda e: e.memset(nb[:], 0.0), writes=["nb"])
                P.pool(lambda e: e.memset(onesb[:], 1.0), writes=["onesb"])
                if d == 1:
                    wst = A(T + "wst", [128, 2, 512], F32)
                    wdn = A(T + "wdn", [128, 16, 1024], BF16)
                    load_weight_bf16(kb, wst, wdn, "wdn", wdown_d[j], 16, 1024)
                    ngbc = A(T + "ngbc", [128, E2], F32)
                    P.dma(ngbc[:], ngbc_d[j], writes=["ngbc"])
                    hf = A(T + "hf", [128, E2], F32)
                    sz = A(T + "sz", [128, E2], F32)
                    xcT = A(T + "xcT", [128, 16, 128], BF16)
                    hnb = A(T + "hnb", [128, E2], BF16)
                    xin = A(T + "xin", [128, 16, 128], BF16)
                    hrs = A(T + "hrs", [128, 8, 128], F32)
                    bst = A(T + "bst", [128, 4, 8], F32)
                    lb = LNBufs(nc, es, 128, "m")
                for k, c in enumerate(order):
                    is_last = (k == len(order) - 1)
                    cn = order[k + 1] if not is_last else None
                    col = 1 if c < 2 else 0
                    need_h = not (last and col == 1)
                    qs = k % 2
                    P.dma(qT[:, qs], S["qTb"][c], reads=[("qTb", c)], writes=[("qT", qs)])
                    P.dma(kT[:, qs], S["kTb"][c], reads=[("kTb", c)], writes=[("kT", qs)])
                    P.dma(ktk[:], S["ktok"][c * 128:(c + 1) * 128, :], reads=[("ktok", c)], writes=["ktk"])
                    P.dma(vtk[:], S["vtok"][c * 128:(c + 1) * 128, :], reads=[("vtok", c)], writes=["vtk"])
                    if need_h:
                        for hh in range(4):
                            for jj in range(4):
                                P.pe(lambda e, hh=hh, jj=jj, qs=qs: e.matmul(
                                    kb.pb[0][:, hh * 128:(hh + 1) * 128], lhsT=kT[:, qs, 4 * hh + jj, :],
                                    rhs=qT[:, qs, 4 * hh + jj, :], start=(jj == 0), stop=(jj == 3)),
                                    reads=[("kT", qs), ("qT", qs)], writes=[("pb", 0)])
                        for hh in range(4):
                            P.dve(lambda e, hh=hh, c=c: e.scalar_tensor_tensor(
                                out=AT[:, hh, :], in0=kb.pb[0][:, hh * 128:(hh + 1) * 128], scalar=gd["w"][:, c, hh:hh + 1],
                                in1=tri[:, d, :], op0=ALU.mult, op1=ALU.mult),
                                reads=[("pb", 0), "mg_w", "mg_tri"], writes=[("AT", hh)])
                    for hh in range(4):
                        if need_h:
                            pn = kb.pb[2 + (hh % 2)]
                            P.pe(lambda e, hh=hh, pn=pn: e.matmul(pn[:, :512], lhsT=AT[:, hh, :],
                                                                  rhs=vtk[:, hh * 512:(hh + 1) * 512], start=True,
                                                                  stop=False),
                                 reads=[("AT", hh), "vtk"], writes=[("pb", 2 + (hh % 2))])
                            for jj in range(4):
                                P.pe(lambda e, hh=hh, jj=jj, pn=pn, qs=qs: e.matmul(
                                    pn[:, :512], lhsT=qT[:, qs, 4 * hh + jj, :], rhs=Cb[:, hh, jj, :], start=False,
                                    stop=(jj == 3)),
                                    reads=[("qT", qs), ("Cb", hh)], writes=[("pb", 2 + (hh % 2))])
                            pd = kb.pb[1][:, hh:hh + 1]
                            P.pe(lambda e, hh=hh, pd=pd: e.matmul(pd, lhsT=AT[:, hh, :], rhs=onesb[:, 0:1], start=True,
                                                                  stop=False),
                                 reads=[("AT", hh), "onesb"], writes=[("pb", 1)])
                            for jj in range(4):
                                P.pe(lambda e, hh=hh, jj=jj, pd=pd, qs=qs: e.matmul(
                                    pd, lhsT=qT[:, qs, 4 * hh + jj, :], rhs=nb[:, hh, jj:jj + 1], start=False,
                                    stop=(jj == 3)),
                                    reads=[("qT", qs), "nb"], writes=[("pb", 1)])
                            P.act(lambda e, hh=hh, pd=pd: e.activation(out=rr[:, hh:hh + 1], in_=pd, func=AF.Abs),
                                  reads=[("pb", 1)], writes=[("rr", hh)])
                            P.dve(lambda e, hh=hh, c=c: e.tensor_tensor(out=rr[:, hh:hh + 1], in0=rr[:, hh:hh + 1],
                                                                        in1=gd["e"][:, c, hh:hh + 1], op=ALU.max),
                                  reads=[("rr", hh), "mg_e"], writes=[("rr", hh)])
                            P.dve(lambda e, hh=hh: e.reciprocal(out=rr[:, hh:hh + 1], in_=rr[:, hh:hh + 1]),
                                  reads=[("rr", hh)], writes=[("rr", hh)])
                            P.act(lambda e, hh=hh, pn=pn: e.activation(out=hst[:, hh * 512:(hh + 1) * 512], in_=pn[:, :512],
                                                                       func=AF.Identity, scale=rr[:, hh:hh + 1]),
                                  reads=[("pb", 2 + (hh % 2)), ("rr", hh)], writes=[("hst", hh)])
                        if not is_last:
                            ks = hh % 2
                            P.act(lambda e, hh=hh, ks=ks, c=c: e.activation(out=KW[:, ks, :],
                                                                            in_=ktk[:, hh * 512:(hh + 1) * 512],
                                                                            func=AF.Identity,
                                                                            scale=gd["wa"][:, c, hh:hh + 1]),
                                  reads=["ktk", "mg_wa"], writes=[("KW", ks)])
                            a_ap = gd["a"][:, cn, hh:hh + 1]
                            for jj in range(4):
                                P.pe(lambda e, hh=hh, jj=jj, ks=ks: e.matmul(
                                    kb.pb[4 + jj][:, :512], lhsT=KW[:, ks, jj * 128:(jj + 1) * 128],
                                    rhs=vtk[:, hh * 512:(hh + 1) * 512], start=True, stop=True),
                                    reads=[("KW", ks), "vtk"], writes=[("pb", 4 + jj)])
                                P.dve(lambda e, hh=hh, jj=jj, a_ap=a_ap: e.scalar_tensor_tensor(
                                    out=Cs[:, hh, jj, :], in0=Cs[:, hh, jj, :], scalar=a_ap, in1=kb.pb[4 + jj][:, :512],
                                    op0=ALU.mult, op1=ALU.add),
                                    reads=[("Cs", hh), ("pb", 4 + jj), "mg_am"], writes=[("Cs", hh)])
                            P.act(lambda e, hh=hh: e.activation(out=Cb[:, hh], in_=Cs[:, hh], func=AF.Identity),
                                  reads=[("Cs", hh)], writes=[("Cb", hh)])
                            pnu = kb.pb[1][:, 16 + hh * 4:16 + hh * 4 + 4]
                            for jj in range(4):
                                P.pe(lambda e, hh=hh, jj=jj, ks=ks: e.matmul(
                                    kb.pb[1][:, 16 + hh * 4 + jj:16 + hh * 4 + jj + 1],
                                    lhsT=KW[:, ks, jj * 128:(jj + 1) * 128], rhs=onesb[:, 0:1], start=True, stop=True),
                                    reads=[("KW", ks), "onesb"], writes=[("pb", 1)])
                            P.dve(lambda e, hh=hh, pnu=pnu, a_ap=a_ap: e.scalar_tensor_tensor(
                                out=ns[:, hh, :], in0=ns[:, hh, :], scalar=a_ap, in1=pnu, op0=ALU.mult, op1=ALU.add),
                                reads=["ns", ("pb", 1), "mg_am"], writes=["ns"])
                            P.dve(lambda e, hh=hh: e.tensor_copy(out=nb[:, hh, :], in_=ns[:, hh, :]), reads=["ns"],
                                  writes=["nb"])
                    if not need_h:
                        continue
                    hst_res = [("hst", hh) for hh in range(4)]
                    if d == 0:
                        P.dma(S["hftok"][c * 128:(c + 1) * 128, :], hst[:], reads=hst_res, writes=[("hftok", c)],
                              q="act")
                        continue
                    P.dma(hf[:], S["hftok"][c * 128:(c + 1) * 128, :], reads=[("hftok", c)], writes=["hf"])
                    P.dma(sz[:], S["sztok"][c * 128:(c + 1) * 128, :], reads=[("sztok", c)], writes=["sz"])
                    P.dma(xcT[:], S["xcTb"][c], reads=[("xcTb", c)], writes=["xcT"])
                    t0 = c * 128
                    P.dma(hrs[:], hT_tile(h_in, t0, 128), reads=hres(h_in_res, t0, 128), writes=["hrs"])
                    P.pool(lambda e: e.tensor_tensor(out=hf[:], in0=hf[:], in1=hst[:], op=ALU.add),
                           reads=["hf"] + hst_res, writes=["hf"])
                    for hh in range(4):
                        hsl = slice(hh * 512, (hh + 1) * 512)
                        P.dve(lambda e, hsl=hsl: e.tensor_tensor(out=hf[:, hsl], in0=hf[:, hsl], in1=sz[:, hsl],
                                                                 op=ALU.mult),
                              reads=["hf", "sz"], writes=["hf"])
                        P.dve(lambda e, hh=hh, hsl=hsl: e.bn_stats(out=bst[:, hh, 0:6], in_=hf[:, hsl]), reads=["hf"],
                              writes=[("bst", hh)])
                        P.dve(lambda e, hh=hh: e.bn_aggr(out=bst[:, hh, 6:8], in_=bst[:, hh, 0:6]), reads=[("bst", hh)],
                              writes=[("bst", hh)])
                        P.act(lambda e, hh=hh: e.activation(out=bst[:, hh, 7:8], in_=bst[:, hh, 7:8], func=AF.Sqrt,
                                                            bias=kb.epsc[:, 1:2]),
                              reads=[("bst", hh), "epsc"], writes=[("bst", hh)])
                        P.dve(lambda e, hh=hh: e.reciprocal(out=bst[:, hh, 7:8], in_=bst[:, hh, 7:8]),
                              reads=[("bst", hh)], writes=[("bst", hh)])
                        P.dve(lambda e, hh=hh, hsl=hsl: e.tensor_scalar(out=hf[:, hsl], in0=hf[:, hsl],
                                                                        scalar1=bst[:, hh, 6:7], scalar2=bst[:, hh, 7:8],
                                                                        op0=ALU.subtract, op1=ALU.mult),
                              reads=["hf", ("bst", hh)], writes=["hf"])
                        P.pool(lambda e, hsl=hsl: e.tensor_tensor(out=hnb[:, hsl], in0=hf[:, hsl], in1=ngbc[:, hsl],
                                                                  op=ALU.mult),
                               reads=["hf", "ngbc"], writes=["hnb"])
                    for half in range(2):
                        ptv = kb.pb[2 + half][:].bitcast(BF16)
                        for e8 in range(8):
                            e_ = half * 8 + e8
                            P.pe(lambda e, e_=e_, e8=e8, ptv=ptv: e.transpose(out=ptv[:, e8 * 128:(e8 + 1) * 128],
                                                                              in_=hnb[:, e_ * 128:(e_ + 1) * 128],
                                                                              identity=kb.identb[:]),
                                 reads=["hnb", "identb"], writes=[("pb", 2 + half)])
                        for e8 in range(8):
                            e_ = half * 8 + e8
                            P.dve(lambda e, e_=e_, e8=e8, ptv=ptv: e.scalar_tensor_tensor(
                                out=xin[:, e_, :], in0=xcT[:, e_, :], scalar=kb.v("ml_skip", j * 16 + e_),
                                in1=ptv[:, e8 * 128:(e8 + 1) * 128], op0=ALU.mult, op1=ALU.add),
                                reads=["xcT", ("pb", 2 + half), "vecs"], writes=[("xin", e_)])
                    finals.extend(proj_ln(kb, lb, xin, lambda kc: ("xin", kc), 16, wdn, "wdn",
                                          lambda o: hrs[:, o, :], "hrs", li, 128, col,
                                          lambda t0=t0: hT_tile(h_out, t0, 128), hres(h_out_res, t0, 128)))
            P.barrier()

        run_dir()
    return finals


def mlstm_mixer(kb, li, j, last, h_in, h_in_res, h_out, h_out_res, wup_d, bd_d, wif_d, wdown_d, ngbc_d):
    mlstm_prep(kb, li, j, h_in, h_in_res, wup_d, bd_d, wif_d)
    with contextlib.ExitStack() as es:
        gates = mlstm_gates(kb, es, "g%d" % li)
        finals = mlstm_scan(kb, li, j, last, gates, h_in, h_in_res, h_out, h_out_res, wdown_d, ngbc_d)
    kb.P.barrier()
    return finals


def rope_tables_host():
    axis_dim = 32
    freqs = (10000.0 ** (-np.arange(0, axis_dim, 2, dtype=np.float32) / axis_dim)).astype(np.float32)
    t = np.arange(SEQ)
    rows = (t // 64).astype(np.float32)
    cols = (t % 64).astype(np.float32)
    ang = np.concatenate([rows[:, None] * freqs, cols[:, None] * freqs], axis=-1).astype(np.float32)
    cos = np.cos(ang).astype(np.float32)
    sin = np.sin(ang).astype(np.float32)
    p = np.arange(128)
    i = (p % 64) // 2
    par = p % 2
    c = cos[:, i].T
    s = sin[:, i].T * np.where(par == 0, -1.0, 1.0)[:, None]
    return np.ascontiguousarray(np.stack([c * 0.125, s * 0.125, c, s]).astype(np.float32))


def swap_pairs(w):
    w = w.reshape(w.shape[0], -1, 2)[:, :, ::-1]
    return np.ascontiguousarray(w.reshape(w.shape[0], -1))


def build_program(n_layers=DEPTH):
    nc = bass.Bass("TRN2", target_bir_lowering=False)
    kb = KB(nc)
    h0 = kb.din("h0", [D, LT])
    vecs = kb.din("vecs", [128, NV])
    ident = kb.din("ident", [128, 128])
    modw = kb.din("mod_w", [DEPTH, D, 9216])
    w_in = kb.din("ffn_w_in", [DEPTH, 2, D, 2 * DFF])
    w_out = kb.din("ffn_w_out", [DEPTH, 2, DFF, D])
    wup = kb.din("ml_w_up", [2, D, 4096])
    bd = kb.din("ml_bd", [2, 3, 128, 16, 128])
    wif = kb.din("ml_wif", [2, 128, 48, 16])
    wdn = kb.din("ml_w_down", [2, E2, D])
    ngbc = kb.din("ml_ngbc", [2, 128, E2])
    wqkv = kb.din("at_w_qkv", [1, D, 1536])
    wsw = kb.din("at_wsw", [D, 1280])
    wo = kb.din("at_w_o", [1, D, D])
    rope = kb.din("rope", [4, 128, SEQ])
    scw_in = kb.din("sc_w_in", [1, D, 3072])
    scw_out = kb.din("sc_w_out", [1, D, D])
    out = kb.dout("out", [D, SEQ])
    hA = kb.dscr("hA", [D, LT])
    hB = kb.dscr("hB", [D, LT])
    setup_consts(kb, vecs, ident)
    compute_mod(kb, modw, list(range(n_layers)))
    bufs = [(hA, "hA"), (hB, "hB")]
    cur = (h0, "h0")
    nxt_i = 0
    finals = []
    lat_tiles = [t for t in TILES if t[2] == 0]
    for i in range(n_layers):
        last = (i == DEPTH - 1)
        kind, j = i % 3, i // 3
        nxt = bufs[nxt_i]
        ffn_sublayer(kb, i, 0, cur[0], cur[1], nxt[0], nxt[1], w_in, w_out, TILES)
        cur, nxt_i = nxt, 1 - nxt_i
        nxt = bufs[nxt_i]
        if kind == 0:
            mlstm_mixer(kb, i, j, last, cur[0], cur[1], nxt[0], nxt[1], wup, bd, wif, wdn, ngbc)
        elif kind == 1:
            attention_mixer(kb, i, cur[0], cur[1], nxt[0], nxt[1], wqkv, wsw, wo, rope, TILES)
        else:
            shortconv_mixer(kb, i, cur[0], cur[1], nxt[0], nxt[1], scw_in, scw_out, TILES)
        cur, nxt_i = nxt, 1 - nxt_i
        nxt = bufs[nxt_i]
        if last:
            finals = ffn_sublayer(kb, i, 1, cur[0], cur[1], out, "out", w_in, w_out, lat_tiles, out_col0=NCTX)
        else:
            finals = ffn_sublayer(kb, i, 1, cur[0], cur[1], nxt[0], nxt[1], w_in, w_out, TILES)
            cur, nxt_i = nxt, 1 - nxt_i
    if n_layers < DEPTH:
        finals = [kb.P.dma(out, cur[0][:, NCTX:LT], reads=hres(cur[1], NCTX, SEQ), writes=["out"])]
    stats = kb.P.emit(final_waits=[o.idx for o in finals])
    return nc, stats


def host_inputs(inputs, n_cores=8):
    I = {k: np.asarray(v) for k, v in inputs.items()}
    wqkv = np.ascontiguousarray(I["at_w_qkv"], np.float32)
    shared = {
        "ident": np.eye(128, dtype=np.float32),
        "mod_w": np.ascontiguousarray(I["mod_w"], np.float32),
        "ffn_w_in": np.ascontiguousarray(I["ffn_w_in"], np.float32),
        "ffn_w_out": np.ascontiguousarray(I["ffn_w_out"], np.float32),
        "ml_w_up": np.ascontiguousarray(I["ml_w_up"], np.float32),
        "ml_bd": np.stack([host_bd(I["ml_w_qkv"][j]) for j in range(2)]),
        "ml_wif": np.stack([host_wif(I["ml_w_if"][j]) for j in range(2)]),
        "ml_w_down": np.ascontiguousarray(I["ml_w_down"], np.float32),
        "ml_ngbc": np.ascontiguousarray(np.broadcast_to(I["ml_norm_g"].astype(np.float32)[:, None, :], (2, 128, E2))),
        "at_w_qkv": wqkv,
        "at_wsw": swap_pairs(wqkv[0][:, :1280]),
        "at_w_o": np.ascontiguousarray(I["at_w_o"], np.float32),
        "rope": rope_tables_host(),
        "sc_w_in": np.ascontiguousarray(I["sc_w_in"], np.float32),
        "sc_w_out": np.ascontiguousarray(I["sc_w_out"], np.float32),
    }
    maps = []
    for b in range(n_cores):
        m = dict(shared)
        m["h0"] = np.ascontiguousarray(np.concatenate([I["ctx"][b].T, I["x"][b].T], axis=1), np.float32)
        m["vecs"] = pack_vecs(I, b)
        maps.append(m)
    return maps


def kernel(**inputs):
    nc, _ = build_program()
    maps = host_inputs(inputs, 8)
    res = run_bass_kernel_spmd(nc, maps, core_ids=list(range(8)))
    outs = [np.asarray(r["out"], np.float32).T for r in res.results]
    return np.ascontiguousarray(np.stack(outs, axis=0))
```
